# Optimizing a Trainium2 kernel written in Bass

```python
import math
import jax, jax.numpy as jnp
from jax import lax
import numpy as np

D_MODEL = 1024
BATCH = 32
SEQ = 2048
DEPTH = 1
DEC_BATCH = 16
DEC_SEQ = 16
PAST_LEN = 1024

CHUNK = 64
N_META = 16
Q_BLOCK = 128
EPS = 1e-6
NEG = -1e30
FOX_HEADS = 8
FOX_HEAD_DIM = 64
FOX_W = FOX_HEADS * FOX_HEAD_DIM
FOX_SCALE = 1.0 / math.sqrt(FOX_HEAD_DIM)
MLA_HEADS = 8
MLA_NOPE = 64
MLA_ROPE = 32
MLA_V = 64
MLA_QK = MLA_NOPE + MLA_ROPE
MLA_Q_LORA = 384
MLA_KV_LORA = 256
MLA_SCALE = 1.0 / math.sqrt(MLA_QK)
ROPE_BASE = 10000.0
D_FF = 2816
CONV_W = 3
OFF_FK = FOX_W
OFF_FV = 2 * FOX_W
OFF_FF = 3 * FOX_W
OFF_CQ = OFF_FF + FOX_HEADS
OFF_CKV = OFF_CQ + MLA_Q_LORA
OFF_KR = OFF_CKV + MLA_KV_LORA
OFF_GATE = OFF_KR + MLA_ROPE
IN_WIDTH = OFF_GATE + 2 * D_MODEL

kernel_name = 'fox_mla_gated_hybrid_stream_step'


def rmsnorm(x, g):
    xf = x.astype(jnp.float32)
    y = xf * lax.rsqrt(jnp.mean(xf * xf, axis=-1, keepdims=True) + EPS)
    return (y * g.astype(jnp.float32)).astype(x.dtype)


def rope_tables(pos, dtype):
    inv = ROPE_BASE ** (-jnp.arange(0, MLA_ROPE, 2, dtype=jnp.float32) / MLA_ROPE)
    ang = pos.astype(jnp.float32)[:, None] * inv[None, :]
    return jnp.cos(ang).astype(dtype), jnp.sin(ang).astype(dtype)


def apply_rope(x, cos, sin):
    x1, x2 = jnp.split(x, 2, axis=-1)
    return jnp.concatenate([x1 * cos - x2 * sin, x1 * sin + x2 * cos], axis=-1)


def mixer_inputs(xn, pos, lw):
    B, S, _ = xn.shape
    proj = xn @ lw['w_in']
    fq, fk, fv, ff, cq, ckv, kr, gl = jnp.split(
        proj, [OFF_FK, OFF_FV, OFF_FF, OFF_CQ, OFF_CKV, OFF_KR, OFF_GATE], axis=-1)
    fox_q = fq.reshape(B, S, FOX_HEADS, FOX_HEAD_DIM)
    fox_k = fk.reshape(B, S, FOX_HEADS, FOX_HEAD_DIM)
    fox_v = fv.reshape(B, S, FOX_HEADS, FOX_HEAD_DIM)
    fox_logf = jax.nn.log_sigmoid(ff.astype(jnp.float32) + lw['b_forget'].astype(jnp.float32))
    cos, sin = rope_tables(pos, xn.dtype)
    q = (rmsnorm(cq, lw['mla_q_norm_g']) @ lw['w_uq']).reshape(B, S, MLA_HEADS, MLA_QK)
    q_nope, q_rope = jnp.split(q, [MLA_NOPE], axis=-1)
    mla_q = jnp.concatenate([q_nope, apply_rope(q_rope, cos[:, None, :], sin[:, None, :])], axis=-1)
    ckv_n = rmsnorm(ckv, lw['mla_kv_norm_g'])
    k_rope = apply_rope(kr, cos, sin)
    gate_a, gate_b = jnp.split(jax.nn.sigmoid(gl), 2, axis=-1)
    return fox_q, fox_k, fox_v, fox_logf, mla_q, ckv_n, k_rope, gate_a, gate_b


def mla_keys_values(ckv_n, k_rope, w_ukv):
    B, S, _ = ckv_n.shape
    kv = (ckv_n @ w_ukv).reshape(B, S, MLA_HEADS, MLA_NOPE + MLA_V)
    k_nope, v = jnp.split(kv, [MLA_NOPE], axis=-1)
    k_pe = jnp.broadcast_to(k_rope[:, :, None, :], (B, S, MLA_HEADS, MLA_ROPE))
    return jnp.concatenate([k_nope, k_pe], axis=-1), v


def attend(q, k, v, bias, scale):
    s = jnp.einsum('bqhd,bkhd->bhqk', q, k).astype(jnp.float32) * scale + bias
    p = jax.nn.softmax(s, axis=-1).astype(v.dtype)
    return jnp.einsum('bhqk,bkhd->bqhd', p, v)


def sweep_query_blocks(q, k, v, bias_for_block, scale):
    B, Lp, H, _ = q.shape
    starts = jnp.arange(Lp // Q_BLOCK) * Q_BLOCK

    def one(start):
        qb = lax.dynamic_slice_in_dim(q, start, Q_BLOCK, axis=1)
        return attend(qb, k, v, bias_for_block(start), scale)

    out = lax.map(one, starts)
    return jnp.moveaxis(out, 0, 1).reshape(B, Lp, H, v.shape[-1])


def merge_branches(o_fox, o_mla, gate_a, gate_b, lw):
    B, S = o_fox.shape[:2]
    ya = o_fox.reshape(B, S, FOX_W) @ lw['w_o_fox']
    yb = o_mla.reshape(B, S, MLA_HEADS * MLA_V) @ lw['w_o_mla']
    return (gate_a * ya + gate_b * yb) @ lw['w_out']


def conv_ffn(xn, left, lw):
    u = xn @ lw['w_up']
    S = u.shape[1]
    up = jnp.concatenate([left.astype(u.dtype), u], axis=1)
    c = lw['conv_b']
    for j in range(CONV_W):
        c = c + up[:, j:j + S] * lw['conv_w'][j]
    gate, val = jnp.split(c, 2, axis=-1)
    return (jax.nn.silu(gate) * val) @ lw['w_down'], up[:, up.shape[1] - (CONV_W - 1):]


def prompt_layer(h, lw):
    B, L, _ = h.shape
    Lp = -(-L // Q_BLOCK) * Q_BLOCK
    pad = Lp - L
    xn = rmsnorm(h, lw['norm_mix_g'])
    fq, fk, fv, flogf, mq, ckv_n, k_rope, ga, gb = mixer_inputs(xn, jnp.arange(L), lw)
    padseq = lambda a: jnp.pad(a, [(0, 0), (0, pad)] + [(0, 0)] * (a.ndim - 2))
    idx = jnp.arange(Lp)
    F = jnp.swapaxes(jnp.cumsum(padseq(flogf), axis=1), 1, 2)

    def fox_bias(start):
        tq = start + jnp.arange(Q_BLOCK)
        Fq = lax.dynamic_slice_in_dim(F, start, Q_BLOCK, axis=2)
        causal = (idx[None, :] <= tq[:, None])[None, None]
        return jnp.where(causal, Fq[..., None] - F[:, :, None, :], NEG)

    o_fox = sweep_query_blocks(padseq(fq), padseq(fk), padseq(fv), fox_bias, FOX_SCALE)[:, :L]
    k_mla, v_mla = mla_keys_values(ckv_n, k_rope, lw['w_ukv'])
    cid = jnp.where(idx < N_META, 0, 1 + (idx - N_META) // CHUNK)
    key_ok = idx < L

    def mla_bias(start):
        cq = lax.dynamic_slice_in_dim(cid, start, Q_BLOCK)
        ok = (cid[None, :] <= cq[:, None]) & key_ok[None, :]
        return jnp.where(ok, 0.0, NEG)[None, None]

    o_mla = sweep_query_blocks(padseq(mq), padseq(k_mla), padseq(v_mla), mla_bias, MLA_SCALE)[:, :L]
    h = h + merge_branches(o_fox, o_mla, ga, gb, lw)
    left = jnp.zeros((B, CONV_W - 1, 2 * D_FF), h.dtype)
    f, conv_state = conv_ffn(rmsnorm(h, lw['norm_ffn_g']), left, lw)
    h = h + f
    return h, (fk, fv, flogf, ckv_n, k_rope, conv_state)


def sample_layer(h, ck, cv, clogf, cckv, ckr, cconv, lw):
    B, n, _ = h.shape
    P = ck.shape[1]
    xn = rmsnorm(h, lw['norm_mix_g'])
    fq, fk, fv, flogf, mq, ckv_n, k_rope, ga, gb = mixer_inputs(xn, P + jnp.arange(n), lw)
    k_all = jnp.concatenate([ck.astype(fk.dtype), fk], axis=1)
    v_all = jnp.concatenate([cv.astype(fv.dtype), fv], axis=1)
    F = jnp.swapaxes(jnp.cumsum(jnp.concatenate([clogf.astype(jnp.float32), flogf], axis=1), axis=1), 1, 2)
    j = jnp.arange(P + n)
    t = P + jnp.arange(n)
    causal = (j[None, :] <= t[:, None])[None, None]
    fox_bias = jnp.where(causal, F[:, :, P:, None] - F[:, :, None, :], NEG)
    o_fox = attend(fq, k_all, v_all, fox_bias, FOX_SCALE)
    ckv_all = jnp.concatenate([cckv.astype(ckv_n.dtype), ckv_n], axis=1)
    kr_all = jnp.concatenate([ckr.astype(k_rope.dtype), k_rope], axis=1)
    k_mla, v_mla = mla_keys_values(ckv_all, kr_all, lw['w_ukv'])
    o_mla = attend(mq, k_mla, v_mla, 0.0, MLA_SCALE)
    h = h + merge_branches(o_fox, o_mla, ga, gb, lw)
    f, conv_state = conv_ffn(rmsnorm(h, lw['norm_ffn_g']), cconv, lw)
    h = h + f
    return h, (fk, fv, flogf, ckv_n, k_rope, conv_state)


def setup_inputs(seed: int = 0) -> dict:
    key = jax.random.key(seed)
    ks = jax.random.split(key, 32)
    nrm = lambda k, shape, s=1.0: jax.random.normal(k, shape, jnp.float32) * s
    gain = lambda k, shape: 1.0 + 0.1 * jax.random.normal(k, shape, jnp.float32)
    return {
        'x_prompt': nrm(ks[0], (BATCH, SEQ, D_MODEL)),
        'x_sample': nrm(ks[1], (DEC_BATCH, DEC_SEQ, D_MODEL)),
        'cache_fox_k': nrm(ks[2], (DEPTH, DEC_BATCH, PAST_LEN, FOX_HEADS, FOX_HEAD_DIM)),
        'cache_fox_v': nrm(ks[3], (DEPTH, DEC_BATCH, PAST_LEN, FOX_HEADS, FOX_HEAD_DIM)),
        'cache_fox_logf': jax.nn.log_sigmoid(4.0 + nrm(ks[4], (DEPTH, DEC_BATCH, PAST_LEN, FOX_HEADS))),
        'cache_mla_ckv': nrm(ks[5], (DEPTH, DEC_BATCH, PAST_LEN, MLA_KV_LORA)),
        'cache_mla_krope': nrm(ks[6], (DEPTH, DEC_BATCH, PAST_LEN, MLA_ROPE)),
        'state_ffn_conv': nrm(ks[7], (DEPTH, DEC_BATCH, CONV_W - 1, 2 * D_FF)),
        'meta_tokens': nrm(ks[8], (N_META, D_MODEL)),
        'norm_mix_g': gain(ks[9], (DEPTH, D_MODEL)),
        'w_in': nrm(ks[10], (DEPTH, D_MODEL, IN_WIDTH), D_MODEL ** -0.5),
        'b_forget': 4.0 + nrm(ks[11], (DEPTH, FOX_HEADS)),
        'mla_q_norm_g': gain(ks[12], (DEPTH, MLA_Q_LORA)),
        'w_uq': nrm(ks[13], (DEPTH, MLA_Q_LORA, MLA_HEADS * MLA_QK), MLA_Q_LORA ** -0.5),
        'mla_kv_norm_g': gain(ks[14], (DEPTH, MLA_KV_LORA)),
        'w_ukv': nrm(ks[15], (DEPTH, MLA_KV_LORA, MLA_HEADS * (MLA_NOPE + MLA_V)), MLA_KV_LORA ** -0.5),
        'w_o_fox': nrm(ks[16], (DEPTH, FOX_W, D_MODEL), FOX_W ** -0.5),
        'w_o_mla': nrm(ks[17], (DEPTH, MLA_HEADS * MLA_V, D_MODEL), (MLA_HEADS * MLA_V) ** -0.5),
        'w_out': nrm(ks[18], (DEPTH, D_MODEL, D_MODEL), D_MODEL ** -0.5),
        'norm_ffn_g': gain(ks[19], (DEPTH, D_MODEL)),
        'w_up': nrm(ks[20], (DEPTH, D_MODEL, 2 * D_FF), D_MODEL ** -0.5),
        'conv_w': nrm(ks[21], (DEPTH, CONV_W, 2 * D_FF), CONV_W ** -0.5),
        'conv_b': nrm(ks[22], (DEPTH, 2 * D_FF), 0.02),
        'w_down': nrm(ks[23], (DEPTH, D_FF, D_MODEL), D_FF ** -0.5),
        'norm_final_g': gain(ks[24], (D_MODEL,)),
    }


def reference(x_prompt, x_sample, cache_fox_k, cache_fox_v, cache_fox_logf, cache_mla_ckv, cache_mla_krope,
              state_ffn_conv, meta_tokens, norm_mix_g, w_in, b_forget, mla_q_norm_g, w_uq, mla_kv_norm_g, w_ukv,
              w_o_fox, w_o_mla, w_out, norm_ffn_g, w_up, conv_w, conv_b, w_down, norm_final_g):
    B = x_prompt.shape[0]
    meta = jnp.broadcast_to(meta_tokens[None].astype(x_prompt.dtype), (B, N_META, D_MODEL))
    h_p = jnp.concatenate([meta, x_prompt], axis=1)
    h_s = x_sample
    new_p = []
    new_s = []
    for layer in range(DEPTH):
        lw = {
            'norm_mix_g': norm_mix_g[layer], 'w_in': w_in[layer], 'b_forget': b_forget[layer],
            'mla_q_norm_g': mla_q_norm_g[layer], 'w_uq': w_uq[layer], 'mla_kv_norm_g': mla_kv_norm_g[layer],
            'w_ukv': w_ukv[layer], 'w_o_fox': w_o_fox[layer], 'w_o_mla': w_o_mla[layer], 'w_out': w_out[layer],
            'norm_ffn_g': norm_ffn_g[layer], 'w_up': w_up[layer], 'conv_w': conv_w[layer],
            'conv_b': conv_b[layer], 'w_down': w_down[layer],
        }
        h_p, st_p = prompt_layer(h_p, lw)
        h_s, st_s = sample_layer(h_s, cache_fox_k[layer], cache_fox_v[layer], cache_fox_logf[layer],
                                 cache_mla_ckv[layer], cache_mla_krope[layer], state_ffn_conv[layer], lw)
        new_p.append(st_p)
        new_s.append(st_s)
    stk = lambda states, i: jnp.stack([s[i] for s in states], axis=0)
    y_prompt = rmsnorm(h_p, norm_final_g)[:, N_META:]
    y_sample = rmsnorm(h_s, norm_final_g)
    return (y_prompt, y_sample,
            stk(new_p, 0), stk(new_p, 1), stk(new_p, 2), stk(new_p, 3), stk(new_p, 4), stk(new_p, 5),
            stk(new_s, 0), stk(new_s, 1), stk(new_s, 2), stk(new_s, 3), stk(new_s, 4), stk(new_s, 5))
```

```python
import math
import os
import numpy as np
import concourse.bass as bass
import concourse.mybir as mybir
from concourse.bass_utils import run_bass_kernel_spmd

F32 = mybir.dt.float32
BF16 = mybir.dt.bfloat16
AF = mybir.ActivationFunctionType
ALU = mybir.AluOpType

N_CORES = 8
D = 1024
SEQ = 2048
NMETA = 16
L = NMETA + SEQ
PB = 4
SB = 2
DSEQ = 16
PAST = 1024
H = 8
HD = 64
FOXW = 512
QL = 384
KVL = 256
ROPE = 32
NOPE = 64
MQK = 96
DFF = 2816
UPW = 2 * DFF
NCH = UPW // 128
NGC = DFF // 128
OFF_FK = 512
OFF_FV = 1024
OFF_FF = 1536
OFF_CQ = 1544
OFF_CKV = 1928
OFF_KR = 2184
OFF_GATE = 2216
INW = OFF_GATE + 2 * D
EPS = 1e-6
FOX_SCALE = 1.0 / math.sqrt(HD)
MLA_SCALE = 1.0 / math.sqrt(MQK)
LK = L

SBUF_BASE = 16640
SBUF_TOP = 229344


class Buf:
    __slots__ = ("name", "last_w", "rd_eng", "rd_dma", "excl")

    def __init__(self, name, excl=False):
        self.name = name
        self.excl = excl
        self.last_w = None
        self.rd_eng = {}
        self.rd_dma = []


class Op:
    __slots__ = ("eng", "fn", "dma", "deps", "signal", "sem", "semval", "prev")

    def __init__(self, eng, fn, dma):
        self.eng = eng
        self.fn = fn
        self.dma = dma
        self.deps = []
        self.signal = False
        self.sem = None
        self.semval = 0
        self.prev = 0


ENGS = ("pe", "act", "dve", "pool", "sp")
NDMASEM = {"sp": 20, "pool": 12, "act": 6}


class Prog:
    def __init__(self):
        self.ops = {e: [] for e in ENGS}
        self.last_op = {e: None for e in ENGS}
        self.last_real = {}
        self.dma_since_fence = {e: [] for e in ENGS}
        self.nops = 0

    def op(self, eng, fn, reads=(), writes=(), dma=False, extra=(), real=True):
        o = Op(eng, fn, dma)
        deps = {}
        xr = [b for b in reads if b.excl]
        if xr:
            reads = [b for b in reads if not b.excl]
            writes = list(writes) + [b for b in xr if b not in writes]
        for b in reads:
            w = b.last_w
            if w is not None:
                deps[id(w)] = (w, True)
        for b in writes:
            w = b.last_w
            if w is not None and id(w) not in deps:
                deps[id(w)] = (w, False)
            for r in b.rd_eng.values():
                if id(r) not in deps:
                    deps[id(r)] = (r, False)
            for r in b.rd_dma:
                if id(r) not in deps:
                    deps[id(r)] = (r, False)
        for d in extra:
            deps[id(d)] = (d, True)
        for d, raw in deps.values():
            if d is o:
                continue
            if d.dma or o.dma or d.eng != o.eng:
                need = True
            elif o.eng == "pe":
                need = False
            else:
                need = True
            if need:
                o.deps.append(d)
                d.signal = True
        for b in reads:
            if dma:
                b.rd_dma.append(o)
            else:
                b.rd_eng[eng] = o
        for b in writes:
            b.last_w = o
            b.rd_eng = {}
            b.rd_dma = []
        self.ops[eng].append(o)
        self.last_op[eng] = o
        if real and not dma:
            self.last_real[eng] = o
        if dma:
            self.dma_since_fence[eng].append(o)
        self.nops += 1
        return o

    def fence(self):
        dmas = []
        for e in ENGS:
            dmas.extend(self.dma_since_fence[e])
            self.dma_since_fence[e] = []
        lasts = [self.last_real[e] for e in ENGS if self.last_real.get(e) is not None]
        for e in ENGS:
            extra = [d for d in dmas] + [l for l in lasts if l.eng != e]
            self.op(e, lambda en: en.nop(nofuse=True), extra=extra, real=False)

    def emit(self, nc):
        sems = {}
        for e in ENGS:
            sems[e] = nc.alloc_semaphore("s_" + e)
        dsem = {e: [nc.alloc_semaphore("d_%s%d" % (e, i)) for i in range(n)] for e, n in NDMASEM.items()}
        for e in ENGS:
            cnt = 0
            di = 0
            dvals = [0] * NDMASEM.get(e, 1)
            for o in self.ops[e]:
                if o.dma:
                    k = di % NDMASEM[e]
                    di += 1
                    o.sem = dsem[e][k]
                    o.prev = dvals[k]
                    dvals[k] += 16
                    o.semval = dvals[k]
                elif o.signal:
                    cnt += 1
                    o.sem = sems[e]
                    o.semval = cnt
        engobj = {"pe": "tensor", "act": "scalar", "dve": "vector", "pool": "gpsimd", "sp": "sync"}
        ops = self.ops

        def make(e):
            def body(en):
                waited = {}
                for o in ops[e]:
                    for d in o.deps:
                        k = id(d.sem)
                        if waited.get(k, 0) < d.semval:
                            en.wait_ge(d.sem, d.semval)
                            waited[k] = d.semval
                    if o.dma and o.prev > 0:
                        k = id(o.sem)
                        if waited.get(k, 0) < o.prev:
                            en.wait_ge(o.sem, o.prev)
                            waited[k] = o.prev
                    inst = o.fn(en)
                    if o.dma:
                        inst.then_inc(o.sem, 16)
                    elif o.signal:
                        inst.then_inc(o.sem, 1)
            return body

        with nc.Block() as block:
            for e in ENGS:
                if ops[e]:
                    getattr(block, engobj[e])(make(e))


class Arena:
    def __init__(self, nc, base, top, tag):
        self.nc = nc
        self.base = base
        self.cur = base
        self.top = top
        self.tag = tag
        self.n = 0

    def alloc(self, shape, dtype, name):
        per = 1
        for s in shape[1:]:
            per *= s
        nbytes = per * (4 if dtype == F32 else 2)
        off = (self.cur + 31) // 32 * 32
        if off + nbytes > self.top:
            raise RuntimeError("SBUF arena %s overflow allocating %s %s (%d > %d)" % (self.tag, name, shape, off + nbytes, self.top))
        self.cur = off + nbytes
        self.n += 1
        return self.nc.alloc_sbuf_tensor_at("%s_%s_%d" % (self.tag, name, self.n), list(shape), dtype, offset=off)


class Seq:
    pass


def make_prompt_seq(b):
    s = Seq()
    s.kind = "p"
    s.b = b
    s.ncache = 0
    s.nt = [(0, NMETA)] + [(NMETA + 128 * i, 128) for i in range(16)]
    s.kt = list(s.nt)
    s.qb = [(0, NMETA)] + [(NMETA + 512 * j, 512) for j in range(4)]
    s.lk = L
    return s


def make_sample_seq(b):
    s = Seq()
    s.kind = "s"
    s.b = b
    s.ncache = PAST
    s.nt = [(PAST, DSEQ)]
    s.kt = [(128 * i, 128) for i in range(8)] + [(PAST, DSEQ)]
    s.qb = [(PAST, DSEQ)]
    s.lk = PAST + DSEQ
    return s


def vis(seq, j, i, kind):
    if seq.kind == "s":
        if i < 8:
            return ("full", 0, None)
        return ("diag", 0, "tri") if kind == "fox" else ("full", 0, None)
    if j == 0:
        if i != 0:
            return None
        return ("diag", 0, "tri") if kind == "fox" else ("full", 0, None)
    if i == 0:
        return ("full", 0, None)
    first = 4 * (j - 1) + 1
    if i < first:
        return ("full", 0, None)
    d = i - first
    if d > 3:
        return None
    return ("diag", 128 * d, "tri" if kind == "fox" else "chunk")


def build_program(n_prompt=PB, n_sample=SB, phases="ABCD"):
    nc = bass.Bass("TRN2", target_bir_lowering=False)
    P = Prog()

    def din(name, shape):
        return nc.dram_tensor(name, list(shape), F32, kind="ExternalInput").ap()

    def dout(name, shape):
        return nc.dram_tensor(name, list(shape), F32, kind="ExternalOutput").ap()

    x_p = din("x_prompt", [PB, SEQ, D])
    x_s = din("x_sample", [SB, DSEQ, D])
    c_fk = din("cache_fox_k", [SB, PAST, FOXW])
    c_fv = din("cache_fox_v", [SB, PAST, FOXW])
    c_lf = din("cache_fox_logf", [SB, PAST, H])
    c_ckv = din("cache_mla_ckv", [SB, PAST, KVL])
    c_kr = din("cache_mla_krope", [SB, PAST, ROPE])
    c_conv = din("state_ffn_conv", [SB, 2, UPW])
    meta = din("meta_tokens", [NMETA, D])
    w_in = din("w_in", [D, INW])
    w_uq = din("w_uq", [QL, H * MQK])
    w_ukv = din("w_ukv", [KVL, H * 128])
    w_ofox = din("w_o_fox", [FOXW, D])
    w_omla = din("w_o_mla", [FOXW, D])
    w_out = din("w_out", [D, D])
    w_up = din("w_up", [D, UPW])
    w_down = din("w_down", [DFF, D])
    convwb = din("convwb", [128, NCH, 4])
    g_mix = din("g_mix_bc", [128, D])
    g_ffn = din("g_ffn_bc", [128, D])
    g_fin = din("g_fin_bc", [128, D])
    g_q = din("g_q_bc", [128, QL])
    g_kv = din("g_kv_bc", [128, KVL])
    b_fg = din("b_forget_bc", [128, H])
    c_ident = din("c_ident", [128, 128])
    c_utri = din("c_utri", [128, 128])
    c_ones = din("c_ones", [128, 128])
    c_mtri = din("c_mtri", [128, 128])
    c_mchk = din("c_mchk", [128, 128])
    c_cs_tm = din("c_cs_tm", [L, 64])
    c_cs_fm = din("c_cs_fm", [2, ROPE, L])

    o_y_p = dout("y_prompt", [PB, SEQ, D])
    o_y_s = dout("y_sample", [SB, DSEQ, D])
    o_fk_p = dout("new_fox_k_p", [PB, L, FOXW])
    o_fv_p = dout("new_fox_v_p", [PB, L, FOXW])
    o_lf_p = dout("new_fox_logf_p", [PB, L, H])
    o_ckv_p = dout("new_mla_ckv_p", [PB, L, KVL])
    o_kr_p = dout("new_mla_krope_p", [PB, L, ROPE])
    o_cv_p = dout("new_ffn_conv_p", [PB, 2, UPW])
    o_fk_s = dout("new_fox_k_s", [SB, DSEQ, FOXW])
    o_fv_s = dout("new_fox_v_s", [SB, DSEQ, FOXW])
    o_lf_s = dout("new_fox_logf_s", [SB, DSEQ, H])
    o_ckv_s = dout("new_mla_ckv_s", [SB, DSEQ, KVL])
    o_kr_s = dout("new_mla_krope_s", [SB, DSEQ, ROPE])
    o_cv_s = dout("new_ffn_conv_s", [SB, 2, UPW])
    h2_scr = nc.dram_tensor("h2_scratch", [L, D], F32, kind="Internal").ap()

    ps_mm = [nc.alloc_psum_tensor("ps_mm%d" % i, [128, 512], F32) for i in range(3)]
    ps_s = [nc.alloc_psum_tensor("ps_s%d" % i, [128, 512], F32) for i in range(2)]
    ps_acc = [nc.alloc_psum_tensor("ps_acc%d" % i, [128, 512], F32) for i in range(2)]
    ps_t = nc.alloc_psum_tensor("ps_t", [128, 1024], BF16)
    b_ps_mm = [Buf("ps_mm%d" % i, True) for i in range(3)]
    b_ps_s = [Buf("ps_s%d" % i, True) for i in range(2)]
    b_ps_acc = [Buf("ps_acc%d" % i, True) for i in range(2)]
    b_ps_t = Buf("ps_t", True)
    rr = {"mm": 0, "s": 0, "acc": 0}

    mm_all = [(ps_mm[i], b_ps_mm[i]) for i in range(3)]
    mm_wide7 = mm_all + [(ps_s[i], b_ps_s[i]) for i in range(2)] + [(ps_acc[i], b_ps_acc[i]) for i in range(2)]
    mm_wide5 = mm_all + [(ps_acc[i], b_ps_acc[i]) for i in range(2)]
    mmset = {"banks": mm_all}

    def next_mm():
        bk = mmset["banks"]
        i = rr["mm"] % len(bk)
        rr["mm"] += 1
        return bk[i]

    def next_s():
        i = rr["s"] % 2
        rr["s"] += 1
        return ps_s[i], b_ps_s[i]

    def next_acc():
        i = rr["acc"] % 2
        rr["acc"] += 1
        return ps_acc[i], b_ps_acc[i]

    A0 = Arena(nc, SBUF_BASE, SBUF_TOP, "pers")
    ident_f = A0.alloc([128, 128], F32, "identf")
    utri_f = A0.alloc([128, 128], F32, "utri")
    ones_f = A0.alloc([128, 128], F32, "ones")
    ident_b = A0.alloc([128, 128], BF16, "identb")
    mtri_b = A0.alloc([128, 128], BF16, "mtri")
    mchk_b = A0.alloc([128, 128], BF16, "mchk")
    gmix_t = A0.alloc([128, D], F32, "gmix")
    gffn_t = A0.alloc([128, D], F32, "gffn")
    gfin_t = A0.alloc([128, D], F32, "gfin")
    gq_t = A0.alloc([128, QL], F32, "gq")
    gkv_t = A0.alloc([128, KVL], F32, "gkv")
    bfg_t = A0.alloc([128, H], F32, "bfg")
    cwb_t = A0.alloc([128, NCH, 4], F32, "cwb")
    wuq_t = A0.alloc([128, 3, H * MQK], BF16, "wuq")
    wuqr_t = A0.alloc([128, 3, H * MQK], BF16, "wuqr")
    wukvk_t = A0.alloc([128, 2, FOXW], BF16, "wukvk")
    wukvv_t = A0.alloc([128, 2, FOXW], BF16, "wukvv")
    cst_t = A0.alloc([128, 4], F32, "cst")
    b_const = Buf("const")
    PERS_END = A0.cur

    A1 = Arena(nc, PERS_END, SBUF_TOP, "mid")
    XT = A1.alloc([128, 8, L], BF16, "XT")
    OTF = A1.alloc([128, 4, L], BF16, "OTF")
    OTM = A1.alloc([128, 4, L], BF16, "OTM")
    PH_BASE = A1.cur
    OT_BASE = PH_BASE - 2 * (4 * L * 2)
    b_h2scr = Buf("h2scr")
    b_XT = {}
    b_OTF = {}
    b_OTM = {}

    def bXT(c0):
        return b_XT.setdefault(c0, Buf("XT%d" % c0))

    def bOTF(c0):
        return b_OTF.setdefault(c0, Buf("OTF%d" % c0))

    def bOTM(c0):
        return b_OTM.setdefault(c0, Buf("OTM%d" % c0))

    def dma(eng, out, in_, reads, writes):
        return P.op(eng, lambda en: en.dma_start(out=out, in_=in_), reads=reads, writes=writes, dma=True)

    def mm(out, lhsT, rhs, start, stop, reads, writes):
        return P.op("pe", lambda en: en.matmul(out, lhsT=lhsT, rhs=rhs, start=start, stop=stop),
                    reads=reads, writes=writes)

    def tr(out, in_, ident, reads, writes):
        return P.op("pe", lambda en: en.transpose(out, in_, ident), reads=reads, writes=writes)

    def act(out, in_, func, reads, writes, bias=None, scale=None, accum_out=None):
        kw = {}
        if bias is not None:
            kw["bias"] = bias
        if scale is not None:
            kw["scale"] = scale
        if accum_out is not None:
            kw["accum_out"] = accum_out
        return P.op("act", lambda en: en.activation(out=out, in_=in_, func=func, **kw), reads=reads, writes=writes)

    def vcopy(out, in_, reads, writes, eng="dve"):
        return P.op(eng, lambda en: en.tensor_copy(out=out, in_=in_), reads=reads, writes=writes)

    def vtt(out, in0, in1, op, reads, writes, eng="dve"):
        return P.op(eng, lambda en: en.tensor_tensor(out=out, in0=in0, in1=in1, op=op), reads=reads, writes=writes)

    def vts(out, in0, s1, s2, op0, op1, reads, writes, eng="dve"):
        if op1 is None:
            return P.op(eng, lambda en: en.tensor_scalar(out=out, in0=in0, scalar1=s1, scalar2=None, op0=op0),
                        reads=reads, writes=writes)
        return P.op(eng, lambda en: en.tensor_scalar(out=out, in0=in0, scalar1=s1, scalar2=s2, op0=op0, op1=op1),
                    reads=reads, writes=writes)

    def vstt(out, in0, scalar, in1, op0, op1, reads, writes):
        return P.op("dve", lambda en: en.scalar_tensor_tensor(out=out, in0=in0, scalar=scalar, in1=in1, op0=op0, op1=op1),
                    reads=reads, writes=writes)

    def vmemset(ap, val, writes, eng="dve"):
        return P.op(eng, lambda en: en.memset(ap, val), writes=writes)

    def vrecip(out, in_, reads, writes):
        return P.op("dve", lambda en: en.reciprocal(out=out, in_=in_), reads=reads, writes=writes)

    AS = Arena(nc, PH_BASE, SBUF_TOP, "setup")
    vmemset(cst_t[:, 0:1], EPS, [b_const])
    vmemset(cst_t[:, 1:2], 1.0, [b_const])
    vmemset(cst_t[:, 2:3], 0.0, [b_const])
    b_stage = Buf("setup_stage")
    for t, src in ((ident_f, c_ident), (utri_f, c_utri), (ones_f, c_ones), (gmix_t, g_mix), (gffn_t, g_ffn),
                   (gfin_t, g_fin), (gq_t, g_q), (gkv_t, g_kv), (bfg_t, b_fg)):
        dma("sp", t[:], src[:, :], [], [b_const])
    STAGE = int(os.environ.get("KSTAGE", "9"))
    for t, src in ((ident_b, c_ident), (mtri_b, c_mtri), (mchk_b, c_mchk)):
        if STAGE >= 2:
            dma("pool", t[:], src[:, :], [], [b_const])
    if STAGE >= 3:
        dma("pool", wuq_t[:], w_uq.rearrange("(kc p) f -> p kc f", p=128), [], [b_const])
    for kc in range(2 if STAGE >= 4 else 0):
        src = w_ukv[kc * 128:(kc + 1) * 128, :].rearrange("p (h x) -> p h x", x=128)
        dma("pool", wukvk_t[:, kc, :].rearrange("p (h x) -> p h x", x=64), src[:, :, 0:64], [], [b_const])
        dma("pool", wukvv_t[:, kc, :].rearrange("p (h x) -> p h x", x=64), src[:, :, 64:128], [], [b_const])
    wq4 = wuq_t[:].rearrange("p kc (h x) -> p kc h x", x=MQK)
    wr4 = wuqr_t[:].rearrange("p kc (h x) -> p kc h x", x=MQK)
    vmemset(wuqr_t[:], 0.0, [b_const])
    for kc in range(3 if STAGE >= 5 else 0):
        P.op("dve", (lambda kc: lambda en: en.tensor_scalar(out=wr4[:, kc, :, 64:80], in0=wq4[:, kc, :, 80:96], scalar1=-1.0,
                                                             scalar2=None, op0=ALU.mult))(kc), reads=[b_const], writes=[b_const])
        vcopy(wr4[:, kc, :, 80:96], wq4[:, kc, :, 64:80], [b_const], [b_const])
    dma("sp", cwb_t[:], convwb[:, :, :], [], [b_const])
    P.fence()

    seqs = [make_prompt_seq(b) for b in range(n_prompt)] + [make_sample_seq(b) for b in range(n_sample)]
    if phases.startswith('S'):
        seqs = []

    wb_state = {"i": 0}

    for seq in seqs:
        isP = seq.kind == "p"
        b = seq.b
        if isP:
            out_fk, out_fv, out_lf, out_ckv, out_kr, out_cv, out_y = (o_fk_p[b], o_fv_p[b], o_lf_p[b], o_ckv_p[b],
                                                                     o_kr_p[b], o_cv_p[b], o_y_p[b])
        else:
            out_fk, out_fv, out_lf, out_ckv, out_kr, out_cv, out_y = (o_fk_s[b], o_fv_s[b], o_lf_s[b], o_ckv_s[b],
                                                                     o_kr_s[b], o_cv_s[b], o_y_s[b])
        nc0 = seq.ncache
        NT = seq.nt
        KT = seq.kt
        QB = seq.qb
        NKT = len(KT)
        NQB = len(QB)

        def src_rows(c0, n):
            if not isP:
                return x_s[b, c0 - nc0:c0 - nc0 + n, :]
            if c0 == 0:
                return meta[0:n, :]
            return x_p[b, c0 - NMETA:c0 - NMETA + n, :]

        def out_rows(ap, c0, n):
            return ap[c0 - nc0:c0 - nc0 + n, :]

        AB = Arena(nc, PH_BASE, SBUF_TOP, "ab%s%d" % (seq.kind, b))
        WB = [AB.alloc([128, 8, 512], BF16, "wb%d" % i) for i in range(3)]
        b_WB = [Buf("wb%d" % i) for i in range(3)]

        def load_w_group(col0, ncols):
            i = wb_state["i"] % 3
            wb_state["i"] += 1
            dma("pool", WB[i][:, :, 0:ncols], w_in.rearrange("(kc p) f -> p kc f", p=128)[:, :, col0:col0 + ncols],
                [], [b_WB[i]])
            return WB[i], b_WB[i]

        xin = [AB.alloc([128, D], F32, "xin%d" % i) for i in range(2)]
        b_xin = [Buf("xin%d" % i) for i in range(2)]
        junk = AB.alloc([128, D], BF16, "junk")
        b_junk = Buf("junk")
        xnb = [AB.alloc([128, D], BF16, "xnb%d" % i) for i in range(3)]
        b_xnb = [Buf("xnb%d" % i) for i in range(3)]
        stat = [AB.alloc([128, 4], F32, "stat%d" % i) for i in range(3)]
        b_stat = [Buf("stat%d" % i) for i in range(3)]
        ost = [AB.alloc([128, 512], F32, "ost%d" % i) for i in range(3)]
        b_ost = [Buf("ost%d" % i) for i in range(3)]
        LOGF = AB.alloc([128, NKT, H], F32, "logf")
        b_LOGF = [Buf("logf%d" % i) for i in range(NKT)]
        FCUM = AB.alloc([128, NKT, H], F32, "fcum")
        b_FCUM = [Buf("fcum%d" % i) for i in range(NKT)]
        CREF = AB.alloc([128, NQB, H], F32, "cref")
        b_CREF = [Buf("cref%d" % j) for j in range(NQB)]
        BIAS = AB.alloc([128, NQB, NKT, H], F32, "bias")
        b_BIAS = [Buf("bias%d" % j) for j in range(NQB)]
        PT = [AB.alloc([128, 512], BF16, "pt%d" % i) for i in range(4)]
        b_PT = [Buf("pt%d" % i) for i in range(4)]
        OSB = [AB.alloc([128, 512], F32, "osb%d" % i) for i in range(2)]
        b_OSB = [Buf("osb%d" % i) for i in range(2)]
        ATT_BASE = AB.cur
        AF_ = Arena(nc, ATT_BASE, SBUF_TOP, "fox%s%d" % (seq.kind, b))
        FQT = AF_.alloc([128, 4, LK], BF16, "fqt")
        FKT = AF_.alloc([128, 4, LK], BF16, "fkt")
        VF = AF_.alloc([128, NKT, H, 65], BF16, "vf")
        b_FQT = [Buf("fqt%d" % j) for j in range(NQB)]
        b_FKT = [Buf("fkt%d" % i) for i in range(NKT)]
        b_VF = [Buf("vf%d" % i) for i in range(NKT)]
        b_VF1 = Buf("vf_ones")

        def tiles_in(lst, c0, n):
            return [i for i, (t0, tn) in enumerate(lst) if t0 < c0 + n and c0 < t0 + tn]

        def xt_bufs(c0, n):
            return [bXT(NT[i][0]) for i in tiles_in(NT, c0, n)]

        def pipelined(items, head, tail, depth=2):
            q = []
            for it in items:
                head(it)
                q.append(it)
                if len(q) > depth:
                    tail(q.pop(0))
            while q:
                tail(q.pop(0))

        ev = {"i": 0}

        def evac(out, in_, reads, writes):
            ev["i"] += 1
            if ev["i"] % 2:
                return act(out, in_, AF.Copy, reads, writes)
            return vcopy(out, in_, reads, writes)

        def tm_proj(Wt, bW, c0, n, wcol0, ncols):
            pm, bpm = next_mm()
            for kc in range(8):
                mm(pm[0:n, 0:ncols], XT[:, kc, c0:c0 + n], Wt[:, kc, wcol0:wcol0 + ncols], kc == 0, kc == 7,
                   [bXT(c0), bW], [bpm])
            return pm, bpm

        def fm_proj(Wt, bW, wcol0, m, qc0, nq):
            pm, bpm = next_mm()
            xb = xt_bufs(qc0, nq)
            for kc in range(8):
                mm(pm[0:m, 0:nq], Wt[:, kc, wcol0:wcol0 + m], XT[:, kc, qc0:qc0 + nq], kc == 0, kc == 7,
                   xb + [bW], [bpm])
            return pm, bpm

        KI0 = NKT - len(NT)
        mmset["banks"] = mm_wide7
        Wt, bW = load_w_group(OFF_FK, 512)

        def fk_tm(ti):
            c0, n = NT[ti]
            s = ti % 3
            pm, bpm = tm_proj(Wt, bW, c0, n, 0, 512)
            evac(ost[s][0:n, :], pm[0:n, 0:512], [bpm], [b_ost[s]])
            dma("sp", out_rows(out_fk, c0, n), ost[s][0:n, :], [b_ost[s]], [])

        for ti, (c0, n) in enumerate(NT):
            s = ti % 2
            if ti >= 1:
                fk_tm(ti - 1)
            dma("sp", xin[s][0:n, :], src_rows(c0, n), [], [b_xin[s]])
            vmemset(stat[s][0:n, 0:1], 0.0, [b_stat[s]])
            act(junk[0:n, :], xin[s][0:n, :], AF.Square, [b_xin[s]], [b_junk, b_stat[s]], accum_out=stat[s][0:n, 0:1])
            act(stat[s][0:n, 1:2], stat[s][0:n, 0:1], AF.Sqrt, [b_stat[s], b_const], [b_stat[s]],
                bias=cst_t[0:n, 0:1], scale=1.0 / D)
            vrecip(stat[s][0:n, 2:3], stat[s][0:n, 1:2], [b_stat[s]], [b_stat[s]])
            vstt(xnb[s][0:n, :], xin[s][0:n, :], stat[s][0:n, 2:3], gmix_t[0:n, :], ALU.mult, ALU.mult,
                 [b_xin[s], b_stat[s], b_const], [b_xnb[s]])
            for kc in range(8):
                tr(ps_t[:, kc * 128:kc * 128 + n], xnb[s][0:n, kc * 128:(kc + 1) * 128], ident_b[0:n, 0:n],
                   [b_xnb[s], b_const], [b_ps_t])
            vcopy(XT[:, :, c0:c0 + n], ps_t[:].rearrange("p (kc t) -> p kc t", t=128)[:, :, 0:n], [b_ps_t], [bXT(c0)])
        fk_tm(len(NT) - 1)

        vmemset(VF[:, :, :, 64:65], 1.0, b_VF + [b_VF1])
        if not isP:
            for i in range(8):
                s = i % 2
                dma("pool", xnb[s][:, 0:512], c_fk[b, i * 128:(i + 1) * 128, :], [], [b_xnb[s]])
                for g in range(4):
                    tr(ps_t[:, g * 128:(g + 1) * 128], xnb[s][:, g * 128:(g + 1) * 128], ident_b[:, :],
                       [b_xnb[s], b_const], [b_ps_t])
                vcopy(FKT[:, :, i * 128:(i + 1) * 128], ps_t[:, 0:512].rearrange("p (g t) -> p g t", t=128),
                      [b_ps_t], [b_FKT[i]])
                dma("pool", VF[:, i, :, 0:64], c_fv[b, i * 128:(i + 1) * 128, :].rearrange("p (h x) -> p h x", x=64),
                    [], [b_VF[i]])
            dma("sp", LOGF[:, 0:8, :], c_lf[b].rearrange("(i p) h -> p i h", p=128), [], b_LOGF[0:8])

        mmset["banks"] = mm_wide7
        Wv, bWv = load_w_group(OFF_FV, 512)
        Wq, bWq = load_w_group(0, 512)
        for j, (qc0, nq) in enumerate(QB):
            kts = tiles_in(KT, qc0, nq)
            for g in range(4):
                pm, bpm = fm_proj(Wt, bW, g * 128, 128, qc0, nq)
                evac(FKT[:, g, qc0:qc0 + nq], pm[:, 0:nq], [bpm], [b_FKT[i] for i in kts])
        for ti, (c0, n) in enumerate(NT):
            s = ti % 2
            ki = KI0 + ti
            pm, bpm = tm_proj(Wv, bWv, c0, n, 0, 512)
            act(ost[s][0:n, :], pm[0:n, 0:512], AF.Copy, [bpm], [b_ost[s]])
            vcopy(VF[0:n, ki, :, 0:64], pm[0:n, 0:512].rearrange("p (h x) -> p h x", x=64), [bpm], [b_VF[ki]])
            dma("sp", out_rows(out_fv, c0, n), ost[s][0:n, :], [b_ost[s]], [])
        for j, (qc0, nq) in enumerate(QB):
            for g in range(4):
                pm, bpm = fm_proj(Wq, bWq, g * 128, 128, qc0, nq)
                evac(FQT[:, g, qc0:qc0 + nq], pm[:, 0:nq], [bpm], [b_FQT[j]])
        Wf, bWf = load_w_group(OFF_FF, 8)
        for ti, (c0, n) in enumerate(NT):
            s = ti % 2
            ki = KI0 + ti
            pm, bpm = tm_proj(Wf, bWf, c0, n, 0, 8)
            z = ost[s]
            vtt(z[0:n, 0:8], pm[0:n, 0:8], bfg_t[0:n, :], ALU.add, [bpm, b_const], [b_ost[s]])
            act(z[0:n, 8:16], z[0:n, 0:8], AF.Exp, [b_ost[s]], [b_ost[s]], scale=-1.0)
            act(z[0:n, 16:24], z[0:n, 8:16], AF.Ln, [b_ost[s], b_const], [b_ost[s]], bias=cst_t[0:n, 1:2])
            vts(LOGF[0:n, ki, :], z[0:n, 16:24], -1.0, None, ALU.mult, None, [b_ost[s]], [b_LOGF[ki]])
            dma("sp", out_rows(out_lf, c0, n), LOGF[0:n, ki, :], [b_LOGF[ki]], [])
        pm, bpm = next_mm()
        vmemset(pm[:, 0:NKT * 8], 0.0, [bpm])
        for i, (kc0, nk) in enumerate(KT):
            for jj in range(i):
                nj = KT[jj][1]
                mm(pm[0:nk, i * 8:(i + 1) * 8], ones_f[0:nj, 0:nk], LOGF[0:nj, jj, :], jj == 0, False,
                   [b_LOGF[jj], b_const], [bpm])
            mm(pm[0:nk, i * 8:(i + 1) * 8], utri_f[0:nk, 0:nk], LOGF[0:nk, i, :], i == 0, True,
               [b_LOGF[i], b_const], [bpm])
        vcopy(FCUM[:].rearrange("p i h -> p (i h)"), pm[:, 0:NKT * 8], [bpm], b_FCUM)
        pm, bpm = next_mm()
        vmemset(pm[:, 0:NQB * 8], 0.0, [bpm])
        for j in range(NQB):
            if isP:
                upto = 0 if j == 0 else 4 * (j - 1) + 3
            else:
                upto = 8
            if upto == 0:
                continue
            for jj in range(upto):
                nj = KT[jj][1]
                mm(pm[:, j * 8:(j + 1) * 8], ones_f[0:nj, :], LOGF[0:nj, jj, :], jj == 0, jj == upto - 1,
                   [b_LOGF[jj], b_const], [bpm])
        vcopy(CREF[:].rearrange("p j h -> p (j h)"), pm[:, 0:NQB * 8], [bpm], b_CREF)
        for j in range(NQB):
            zero_ref = isP and j == 0
            for i, (kc0, nk) in enumerate(KT):
                if vis(seq, j, i, "fox") is None:
                    continue
                if zero_ref:
                    vts(BIAS[0:nk, j, i, :], FCUM[0:nk, i, :], -1.0, None, ALU.mult, None, [b_FCUM[i]], [b_BIAS[j]])
                else:
                    vtt(BIAS[0:nk, j, i, :], CREF[0:nk, j, :], FCUM[0:nk, i, :], ALU.subtract,
                        [b_CREF[j], b_FCUM[i]], [b_BIAS[j]])

        mmset["banks"] = mm_all
        ptc = {"i": 0, "o": 0}

        def attention(kind, kdim, Kap, bK, Qap, bQ, Vap, bV, OT, bOT, scale, heads):
            groups = []
            for h in heads:
                for j in range(NQB):
                    vl = [(i, vis(seq, j, i, kind)) for i in range(NKT)]
                    vl = [(i, v) for i, v in vl if v is not None]
                    groups.append((h, j, vl))
            pairs = []
            for gi, (h, j, vl) in enumerate(groups):
                for idx, (i, v) in enumerate(vl):
                    pairs.append((gi, h, j, idx, i, v, idx == len(vl) - 1))
            sinfo = {}
            ginfo = {}

            def emit_S(p):
                gi, h, j, idx, i, v, last = pairs[p]
                qc0, nq = QB[j]
                kc0, nk = KT[i]
                c0 = v[1]
                w = nq - c0
                ps, bps = next_s()
                mm(ps[0:nk, 0:w], Kap(i, h), Qap(j, h, c0), True, True, [bK(i, h), bQ(j, h)], [bps])
                sinfo[p] = (ps, bps)

            def emit_rest(p):
                gi, h, j, idx, i, v, last = pairs[p]
                qc0, nq = QB[j]
                kc0, nk = KT[i]
                c0 = v[1]
                w = nq - c0
                ps, bps = sinfo.pop(p)
                if idx == 0:
                    ginfo[gi] = next_acc()
                acc, bacc = ginfo[gi]
                k = ptc["i"] % 4
                ptc["i"] += 1
                if kind == "fox":
                    act(PT[k][0:nk, 0:w], ps[0:nk, 0:w], AF.Exp, [bps, b_BIAS[j]], [b_PT[k]],
                        bias=BIAS[0:nk, j, i, h:h + 1], scale=scale)
                else:
                    act(PT[k][0:nk, 0:w], ps[0:nk, 0:w], AF.Exp, [bps, b_const], [b_PT[k]],
                        bias=cst_t[0:nk, 2:3], scale=scale)
                if v[0] == "diag":
                    mw = min(128, w)
                    mt = mtri_b if v[2] == "tri" else mchk_b
                    vtt(PT[k][0:nk, 0:mw], PT[k][0:nk, 0:mw], mt[0:nk, 0:mw], ALU.mult,
                        [b_PT[k], b_const], [b_PT[k]])
                mm(acc[0:65, c0:nq], Vap(i, h), PT[k][0:nk, 0:w], idx == 0, last,
                   [bV(i, h), b_PT[k]], [bacc])

            def emit_norm(gi):
                h, j, vl = groups[gi]
                g, sl = h // 2, h % 2
                qc0, nq = QB[j]
                acc, bacc = ginfo.pop(gi)
                o = ptc["o"] % 2
                ptc["o"] += 1
                vcopy(OSB[o][0:65, 0:nq], acc[0:65, 0:nq], [bacc], [b_OSB[o]])
                vrecip(OSB[o][64:65, 0:nq], OSB[o][64:65, 0:nq], [b_OSB[o]], [b_OSB[o]])
                pm, bpm = next_mm()
                mm(pm[0:64, 0:nq], ones_f[64:65, 0:64], OSB[o][64:65, 0:nq], True, True, [b_OSB[o], b_const], [bpm])
                vtt(OT[sl * 64:(sl + 1) * 64, g, qc0:qc0 + nq], OSB[o][0:64, 0:nq], pm[0:64, 0:nq], ALU.mult,
                    [b_OSB[o], bpm], [bOT(qc0)])

            pending = None
            emit_S(0)
            for p in range(len(pairs)):
                if p + 1 < len(pairs):
                    emit_S(p + 1)
                emit_rest(p)
                gi, h, j, idx, i, v, last = pairs[p]
                if pending is not None and (idx >= 3 or last):
                    emit_norm(pending)
                    pending = None
                if last:
                    pending = gi
            if pending is not None:
                emit_norm(pending)

        if "B" in phases:
            attention(
                "fox", 64,
                lambda i, h: FKT[(h % 2) * 64:(h % 2) * 64 + 64, h // 2, KT[i][0]:KT[i][0] + KT[i][1]],
                lambda i, h: b_FKT[i],
                lambda j, h, c0: FQT[(h % 2) * 64:(h % 2) * 64 + 64, h // 2, QB[j][0] + c0:QB[j][0] + QB[j][1]],
                lambda j, h: b_FQT[j],
                lambda i, h: VF[0:KT[i][1], i, h, :],
                lambda i, h: b_VF[i],
                OTF, bOTF, FOX_SCALE, range(H))
        P.fence()
        if "M" in phases:
            mmset["banks"] = mm_wide7
            AM = Arena(nc, ATT_BASE, SBUF_TOP, "mla%s%d" % (seq.kind, b))
            CQT = AM.alloc([128, 3, LK], BF16, "cqt")
            CKVT = AM.alloc([128, 2, LK], BF16, "ckvt")
            KRT = AM.alloc([128, LK], BF16, "krt")
            QP = AM.alloc([128, 2, LK], BF16, "qp")
            KP = AM.alloc([128, 2, LK], BF16, "kp")
            VM = AM.alloc([128, NKT, 2, 65], BF16, "vm")
            CSF = AM.alloc([128, 2, 512], F32, "csf")
            RT = AM.alloc([128, 2, 512], F32, "rt")
            CST = [AM.alloc([128, 64], F32, "cst%d" % i) for i in range(3)]
            b_CQT = [Buf("cqt%d" % j) for j in range(NQB)]
            b_CKVT = [Buf("ckvt%d" % i) for i in range(NKT)]
            b_KRT = [Buf("krt%d" % i) for i in range(NKT)]
            b_QP = [Buf("qp%d" % j) for j in range(NQB)]
            b_KP = [Buf("kp%d" % i) for i in range(NKT)]
            b_VM = [Buf("vm%d" % i) for i in range(NKT)]
            b_CSF = Buf("csf")
            b_RT = Buf("rt")
            b_CST = [Buf("cst%d" % i) for i in range(3)]
            vmemset(VM[:, :, :, 64:65], 1.0, b_VM)

            def qb_of(c0):
                return [j for j, (q0, qn) in enumerate(QB) if q0 <= c0 < q0 + qn][0]

            if not isP:
                for i in range(8):
                    s = i % 2
                    dma("pool", xnb[s][:, 0:256], c_ckv[b, i * 128:(i + 1) * 128, :], [], [b_xnb[s]])
                    dma("pool", xnb[s][:, 256:288], c_kr[b, i * 128:(i + 1) * 128, :], [], [b_xnb[s]])
                    for kc in range(2):
                        tr(ps_t[:, kc * 128:(kc + 1) * 128], xnb[s][:, kc * 128:(kc + 1) * 128], ident_b[:, :],
                           [b_xnb[s], b_const], [b_ps_t])
                    tr(ps_t[0:32, 256:384], xnb[s][:, 256:288], ident_b[:, :], [b_xnb[s], b_const], [b_ps_t])
                    vcopy(CKVT[:, :, i * 128:(i + 1) * 128], ps_t[:, 0:256].rearrange("p (g t) -> p g t", t=128),
                          [b_ps_t], [b_CKVT[i]])
                    vcopy(KRT[64:96, i * 128:(i + 1) * 128], ps_t[0:32, 256:384], [b_ps_t], [b_KRT[i]])
            Wc, bWc = load_w_group(OFF_CQ, QL)
            Wk, bWk = load_w_group(OFF_CKV, KVL + ROPE)
            def cq_head(ti):
                c0, n = NT[ti]
                s = ti % 3
                pm, bpm = tm_proj(Wc, bWc, c0, n, 0, QL)
                vmemset(stat[s][0:n, 0:1], 0.0, [b_stat[s]])
                act(junk[0:n, 0:QL], pm[0:n, 0:QL], AF.Square, [bpm], [b_junk, b_stat[s]], accum_out=stat[s][0:n, 0:1])
                act(stat[s][0:n, 1:2], stat[s][0:n, 0:1], AF.Sqrt, [b_stat[s], b_const], [b_stat[s]],
                    bias=cst_t[0:n, 0:1], scale=1.0 / QL)
                vrecip(stat[s][0:n, 2:3], stat[s][0:n, 1:2], [b_stat[s]], [b_stat[s]])
                vstt(xnb[s][0:n, 0:QL], pm[0:n, 0:QL], stat[s][0:n, 2:3], gq_t[0:n, :], ALU.mult, ALU.mult,
                     [bpm, b_stat[s], b_const], [b_xnb[s]])

            def cq_tail(ti):
                c0, n = NT[ti]
                s = ti % 3
                for kc in range(3):
                    tr(ps_t[:, kc * 128:kc * 128 + n], xnb[s][0:n, kc * 128:(kc + 1) * 128], ident_b[0:n, 0:n],
                       [b_xnb[s], b_const], [b_ps_t])
                vcopy(CQT[:, :, c0:c0 + n], ps_t[:, 0:384].rearrange("p (kc t) -> p kc t", t=128)[:, :, 0:n],
                      [b_ps_t], [b_CQT[qb_of(c0)]])

            pipelined(range(len(NT)), cq_head, cq_tail)
            def kv_head(ti):
                c0, n = NT[ti]
                s = ti % 3
                ki = KI0 + ti
                dma("sp", CST[s][0:n, :], c_cs_tm[c0:c0 + n, :], [], [b_CST[s]])
                pm, bpm = tm_proj(Wk, bWk, c0, n, 0, KVL + ROPE)
                vmemset(stat[s][0:n, 0:1], 0.0, [b_stat[s]])
                act(junk[0:n, 0:KVL], pm[0:n, 0:KVL], AF.Square, [bpm], [b_junk, b_stat[s]], accum_out=stat[s][0:n, 0:1])
                act(stat[s][0:n, 1:2], stat[s][0:n, 0:1], AF.Sqrt, [b_stat[s], b_const], [b_stat[s]],
                    bias=cst_t[0:n, 0:1], scale=1.0 / KVL)
                vrecip(stat[s][0:n, 2:3], stat[s][0:n, 1:2], [b_stat[s]], [b_stat[s]])
                o = ost[s]
                vstt(o[0:n, 0:KVL], pm[0:n, 0:KVL], stat[s][0:n, 2:3], gkv_t[0:n, :], ALU.mult, ALU.mult,
                     [bpm, b_stat[s], b_const], [b_ost[s]])
                vtt(o[0:n, 256:288], pm[0:n, 256:288], CST[s][0:n, 0:32], ALU.mult, [bpm, b_CST[s]], [b_ost[s]])
                vtt(o[0:n, 288:304], pm[0:n, 272:288], CST[s][0:n, 32:48], ALU.mult, [bpm, b_CST[s]], [b_ost[s]])
                vtt(o[0:n, 304:320], pm[0:n, 256:272], CST[s][0:n, 48:64], ALU.mult, [bpm, b_CST[s]], [b_ost[s]])
                vtt(o[0:n, 256:288], o[0:n, 256:288], o[0:n, 288:320], ALU.add, [b_ost[s]], [b_ost[s]])
                dma("sp", out_rows(out_ckv, c0, n), o[0:n, 0:KVL], [b_ost[s]], [])
                dma("sp", out_rows(out_kr, c0, n), o[0:n, 256:288], [b_ost[s]], [])
                vcopy(xnb[s][0:n, 0:288], o[0:n, 0:288], [b_ost[s]], [b_xnb[s]])

            def kv_tail(ti):
                c0, n = NT[ti]
                s = ti % 3
                ki = KI0 + ti
                for kc in range(2):
                    tr(ps_t[:, kc * 128:kc * 128 + n], xnb[s][0:n, kc * 128:(kc + 1) * 128], ident_b[0:n, 0:n],
                       [b_xnb[s], b_const], [b_ps_t])
                tr(ps_t[0:32, 256:256 + n], xnb[s][0:n, 256:288], ident_b[0:n, 0:n], [b_xnb[s], b_const], [b_ps_t])
                vcopy(CKVT[:, :, c0:c0 + n], ps_t[:, 0:256].rearrange("p (g t) -> p g t", t=128)[:, :, 0:n],
                      [b_ps_t], [b_CKVT[ki]])
                vcopy(KRT[64:96, c0:c0 + n], ps_t[0:32, 256:256 + n], [b_ps_t], [b_KRT[ki]])

            pipelined(range(len(NT)), kv_head, kv_tail)
            if isP:
                KB = list(QB)
            else:
                KB = [(0, 512), (512, 512), (PAST, DSEQ)]
            mmset["banks"] = mm_all
            for g in range(4):
                for sl in range(2):
                    h = 2 * g + sl
                    for (k0, kw) in KB:
                        kts = tiles_in(KT, k0, kw)
                        pm, bpm = next_mm()
                        for kc in range(2):
                            mm(pm[0:64, 0:kw], wukvk_t[:, kc, h * 64:(h + 1) * 64], CKVT[:, kc, k0:k0 + kw], kc == 0, kc == 1,
                               [b_CKVT[i] for i in kts] + [b_const], [bpm])
                        evac(KP[0:64, sl, k0:k0 + kw], pm[0:64, 0:kw], [bpm], [b_KP[i] for i in kts])
                        vcopy(KP[64:96, sl, k0:k0 + kw], KRT[64:96, k0:k0 + kw], [b_KRT[i] for i in kts],
                              [b_KP[i] for i in kts])
                for i, (k0, nk) in enumerate(KT):
                    pm, bpm = next_mm()
                    for kc in range(2):
                        mm(pm[0:nk, 0:128], CKVT[:, kc, k0:k0 + nk], wukvv_t[:, kc, g * 128:(g + 1) * 128], kc == 0, kc == 1,
                           [b_CKVT[i], b_const], [bpm])
                    evac(VM[0:nk, i, :, 0:64], pm[0:nk, 0:128].rearrange("p (s x) -> p s x", x=64), [bpm], [b_VM[i]])
                for j, (qc0, nq) in enumerate(QB):
                    dma("sp", CSF[64:96, 0, 0:nq], c_cs_fm[0, :, qc0:qc0 + nq], [], [b_CSF])
                    dma("sp", CSF[64:96, 1, 0:nq], c_cs_fm[1, :, qc0:qc0 + nq], [], [b_CSF])
                    for sl in range(2):
                        h = 2 * g + sl
                        pm1, bpm1 = next_mm()
                        for kc in range(3):
                            mm(pm1[0:96, 0:nq], wuq_t[:, kc, h * 96:(h + 1) * 96], CQT[:, kc, qc0:qc0 + nq], kc == 0, kc == 2,
                               [b_CQT[j], b_const], [bpm1])
                        pm2, bpm2 = next_mm()
                        for kc in range(3):
                            mm(pm2[0:96, 0:nq], wuqr_t[:, kc, h * 96:(h + 1) * 96], CQT[:, kc, qc0:qc0 + nq], kc == 0, kc == 2,
                               [b_CQT[j], b_const], [bpm2])
                        act(QP[0:64, sl, qc0:qc0 + nq], pm1[0:64, 0:nq], AF.Copy, [bpm1], [b_QP[j]])
                        vtt(RT[64:96, 0, 0:nq], pm1[64:96, 0:nq], CSF[64:96, 0, 0:nq], ALU.mult, [bpm1, b_CSF], [b_RT])
                        vtt(RT[64:96, 1, 0:nq], pm2[64:96, 0:nq], CSF[64:96, 1, 0:nq], ALU.mult, [bpm2, b_CSF], [b_RT])
                        vtt(QP[64:96, sl, qc0:qc0 + nq], RT[64:96, 0, 0:nq], RT[64:96, 1, 0:nq], ALU.add, [b_RT], [b_QP[j]])
                attention(
                    "mla", 96,
                    lambda i, h: KP[0:96, h % 2, KT[i][0]:KT[i][0] + KT[i][1]],
                    lambda i, h: b_KP[i],
                    lambda j, h, c0: QP[0:96, h % 2, QB[j][0] + c0:QB[j][0] + QB[j][1]],
                    lambda j, h: b_QP[j],
                    lambda i, h: VM[0:KT[i][1], i, h % 2, :],
                    lambda i, h: b_VM[i],
                    OTM, bOTM, MLA_SCALE, [2 * g, 2 * g + 1])
            P.fence()
        if "C" in phases:
            mmset["banks"] = mm_wide7
            AC = Arena(nc, PH_BASE, SBUF_TOP, "c%s%d" % (seq.kind, b))
            WOF = AC.alloc([128, 4, D], BF16, "wof")
            WOM = AC.alloc([128, 4, D], BF16, "wom")
            WG = AC.alloc([128, 8, 2 * D], BF16, "wg")
            WO = AC.alloc([128, 8, D], BF16, "wo")
            b_WC = Buf("wc")
            MT = AC.alloc([128, 8, 512], BF16, "mt")
            b_MT = Buf("mt")
            G0 = [AC.alloc([128, 512], F32, "g0%d" % i) for i in range(2)]
            G1 = [AC.alloc([128, 512], F32, "g1%d" % i) for i in range(2)]
            M0 = [AC.alloc([128, 512], F32, "m0%d" % i) for i in range(2)]
            b_G0 = [Buf("g0%d" % i) for i in range(2)]
            b_G1 = [Buf("g1%d" % i) for i in range(2)]
            b_M0 = [Buf("m0%d" % i) for i in range(2)]
            cxin = [AC.alloc([128, D], F32, "cxin%d" % i) for i in range(3)]
            b_cxin = [Buf("cxin%d" % i) for i in range(3)]
            cxnb = [AC.alloc([128, D], BF16, "cxnb%d" % i) for i in range(3)]
            b_cxnb = [Buf("cxnb%d" % i) for i in range(3)]
            cjunk = AC.alloc([128, D], BF16, "cjunk")
            b_cjunk = Buf("cjunk")
            cstat = [AC.alloc([128, 4], F32, "cstat%d" % i) for i in range(3)]
            b_cstat = [Buf("cstat%d" % i) for i in range(3)]
            b_WOF = Buf("wof")
            b_WOM = Buf("wom")
            b_WG = [Buf("wg%d" % i) for i in range(4)]
            b_WO = [Buf("wo%d" % i) for i in range(2)]
            w_in_v = w_in.rearrange("(kc p) f -> p kc f", p=128)

            def ld_wg(hh):
                dma("pool", WG[:, :, hh * 512:(hh + 1) * 512],
                    w_in_v[:, :, OFF_GATE + hh * 512:OFF_GATE + (hh + 1) * 512], [], [b_WG[hh]])

            dma("pool", WOF[:], w_ofox.rearrange("(kc p) f -> p kc f", p=128), [], [b_WOF])
            ld_wg(0)
            dma("pool", WOM[:], w_omla.rearrange("(kc p) f -> p kc f", p=128), [], [b_WOM])
            ld_wg(2)
            ld_wg(1)
            ld_wg(3)
            for hh in range(2):
                dma("pool", WO[:, :, hh * 512:(hh + 1) * 512],
                    w_out.rearrange("(kc p) f -> p kc f", p=128)[:, :, hh * 512:(hh + 1) * 512], [], [b_WO[hh]])
            tcount = 0
            for j, (qc0, nq) in enumerate(QB):
                xb = xt_bufs(qc0, nq)
                for m in range(8):
                    s = m % 2
                    pa, bpa = next_mm()
                    for kc in range(4):
                        mm(pa[:, 0:nq], WOF[:, kc, m * 128:(m + 1) * 128], OTF[:, kc, qc0:qc0 + nq], kc == 0, kc == 3,
                           [b_WOF, bOTF(qc0)], [bpa])
                    pg, bpg = next_mm()
                    for kc in range(8):
                        mm(pg[:, 0:nq], WG[:, kc, m * 128:(m + 1) * 128], XT[:, kc, qc0:qc0 + nq], kc == 0, kc == 7,
                           [b_WG[m // 4]] + xb, [bpg])
                    act(G0[s][:, 0:nq], pg[:, 0:nq], AF.Sigmoid, [bpg], [b_G0[s]])
                    vtt(M0[s][:, 0:nq], G0[s][:, 0:nq], pa[:, 0:nq], ALU.mult, [b_G0[s], bpa], [b_M0[s]])
                    pb, bpb = next_mm()
                    for kc in range(4):
                        mm(pb[:, 0:nq], WOM[:, kc, m * 128:(m + 1) * 128], OTM[:, kc, qc0:qc0 + nq], kc == 0, kc == 3,
                           [b_WOM, bOTM(qc0)], [bpb])
                    pg2, bpg2 = next_mm()
                    for kc in range(8):
                        mm(pg2[:, 0:nq], WG[:, kc, D + m * 128:D + (m + 1) * 128], XT[:, kc, qc0:qc0 + nq], kc == 0, kc == 7,
                           [b_WG[2 + m // 4]] + xb, [bpg2])
                    act(G1[s][:, 0:nq], pg2[:, 0:nq], AF.Sigmoid, [bpg2], [b_G1[s]])
                    vtt(G1[s][:, 0:nq], G1[s][:, 0:nq], pb[:, 0:nq], ALU.mult, [b_G1[s], bpb], [b_G1[s]])
                    vtt(MT[:, m, 0:nq], M0[s][:, 0:nq], G1[s][:, 0:nq], ALU.add, [b_M0[s], b_G1[s]], [b_MT])
                def c_head(arg):
                    ti, s = arg
                    c0, n = NT[ti]
                    o = c0 - qc0
                    dma("sp", cxin[s][0:n, :], src_rows(c0, n), [], [b_cxin[s]])
                    for hw in range(2):
                        pm, bpm = next_mm()
                        for kc in range(8):
                            mm(pm[0:n, :], MT[:, kc, o:o + n], WO[:, kc, hw * 512:(hw + 1) * 512], kc == 0, kc == 7,
                               [b_MT, b_WO[hw]], [bpm])
                        vtt(cxin[s][0:n, hw * 512:(hw + 1) * 512], cxin[s][0:n, hw * 512:(hw + 1) * 512], pm[0:n, :], ALU.add,
                            [b_cxin[s], bpm], [b_cxin[s]])
                    dma("sp", h2_scr[c0 - nc0:c0 - nc0 + n, :], cxin[s][0:n, :], [b_cxin[s]], [b_h2scr])
                    vmemset(cstat[s][0:n, 0:1], 0.0, [b_cstat[s]])
                    act(cjunk[0:n, :], cxin[s][0:n, :], AF.Square, [b_cxin[s]], [b_cjunk, b_cstat[s]],
                        accum_out=cstat[s][0:n, 0:1])
                    act(cstat[s][0:n, 1:2], cstat[s][0:n, 0:1], AF.Sqrt, [b_cstat[s], b_const], [b_cstat[s]],
                        bias=cst_t[0:n, 0:1], scale=1.0 / D)
                    vrecip(cstat[s][0:n, 2:3], cstat[s][0:n, 1:2], [b_cstat[s]], [b_cstat[s]])
                    vstt(cxnb[s][0:n, :], cxin[s][0:n, :], cstat[s][0:n, 2:3], gffn_t[0:n, :], ALU.mult, ALU.mult,
                         [b_cxin[s], b_cstat[s], b_const], [b_cxnb[s]])

                def c_tail(arg):
                    ti, s = arg
                    c0, n = NT[ti]
                    for kc in range(8):
                        tr(ps_t[:, kc * 128:kc * 128 + n], cxnb[s][0:n, kc * 128:(kc + 1) * 128], ident_b[0:n, 0:n],
                           [b_cxnb[s], b_const], [b_ps_t])
                    vcopy(XT[:, :, c0:c0 + n], ps_t[:].rearrange("p (kc t) -> p kc t", t=128)[:, :, 0:n], [b_ps_t], [bXT(c0)])

                targs = []
                for ti in tiles_in(NT, qc0, nq):
                    targs.append((ti, tcount % 3))
                    tcount += 1
                pipelined(targs, c_head, c_tail)
            P.fence()
        if "D" in phases:
            mmset["banks"] = mm_wide5
            AD = Arena(nc, OT_BASE, SBUF_TOP, "d%s%d" % (seq.kind, b))
            if isP:
                halves = [[0, 1, 2], [3, 4]]
            else:
                halves = [[0]]
            HW_MAX = max(sum(QB[j][1] for j in blocks) for blocks in halves)
            WD = AD.alloc([128, NGC, D], BF16, "wd")
            b_WD = Buf("wd")
            AT = AD.alloc([128, NGC, HW_MAX], BF16, "at")
            b_AT = Buf("at")
            UFG = [AD.alloc([128, 2 + HW_MAX], BF16, "ufg%d" % i) for i in range(2)]
            UFV = [AD.alloc([128, 2 + HW_MAX], BF16, "ufv%d" % i) for i in range(2)]
            b_UFG = [Buf("ufg%d" % i) for i in range(2)]
            b_UFV = [Buf("ufv%d" % i) for i in range(2)]
            WUP = [AD.alloc([128, 8, 256], BF16, "wup%d" % i) for i in range(3)]
            b_WUP = [Buf("wup%d" % i) for i in range(3)]
            DG = [AD.alloc([128, 6, 128], BF16, "dg%d" % i) for i in range(2)]
            b_DG = [Buf("dg%d" % i) for i in range(2)]
            SG = [AD.alloc([128, 512], F32, "sg%d" % i) for i in range(2)]
            b_SG = [Buf("sg%d" % i) for i in range(2)]
            ULAST = AD.alloc([128, NCH, 2], BF16, "ulast")
            b_UL = Buf("ulast")
            CSO = [AD.alloc([2, 256], F32, "cso%d" % i) for i in range(2)]
            b_CSO = [Buf("cso%d" % i) for i in range(2)]
            dh2 = [AD.alloc([128, D], F32, "dh2%d" % i) for i in range(2)]
            b_dh2 = [Buf("dh2%d" % i) for i in range(2)]
            dy = [AD.alloc([128, D], F32, "dy%d" % i) for i in range(2)]
            b_dy = [Buf("dy%d" % i) for i in range(2)]
            djunk = AD.alloc([128, D], BF16, "djunk")
            b_djunk = Buf("djunk")
            dstat = [AD.alloc([128, 4], F32, "dstat%d" % i) for i in range(2)]
            b_dstat = [Buf("dstat%d" % i) for i in range(2)]
            if isP:
                vmemset(ULAST[:], 0.0, [b_UL])
            else:
                ccv = AD.alloc([2, UPW], BF16, "ccv")
                b_ccv = Buf("ccv")
                dma("pool", ccv[:], c_conv[b], [], [b_ccv])
                for c in range(NCH):
                    tr(ps_t[:, c * 2:c * 2 + 2], ccv[0:2, c * 128:(c + 1) * 128], ident_b[0:2, 0:2], [b_ccv, b_const], [b_ps_t])
                vcopy(ULAST[:].rearrange("p c j -> p (c j)"), ps_t[:, 0:NCH * 2], [b_ps_t], [b_UL])
            wupc = 0
            tcount = 0
            for hi, blocks in enumerate(halves):
                hc0 = QB[blocks[0]][0]
                hw_ = sum(QB[j][1] for j in blocks)
                last_half = hi == len(halves) - 1
                for c in range(NGC):
                    ws = wupc % 3
                    us = wupc % 2
                    wupc += 1
                    wv = w_up.rearrange("(kc p) f -> p kc f", p=128)
                    dma("pool", WUP[ws][:, :, 0:128], wv[:, :, c * 128:(c + 1) * 128], [], [b_WUP[ws]])
                    dma("pool", WUP[ws][:, :, 128:256], wv[:, :, DFF + c * 128:DFF + (c + 1) * 128], [], [b_WUP[ws]])
                    if hi == 0 and c == 2:
                        for hh in range(2):
                            dma("pool", WD[:, :, hh * 512:(hh + 1) * 512],
                                w_down.rearrange("(c p) f -> p c f", p=128)[:, :, hh * 512:(hh + 1) * 512], [], [b_WD])
                    def d_prep(cc, uu):
                        for t in range(3):
                            vts(DG[uu][:, t, :], ident_b[:, :], cwb_t[:, cc, t:t + 1], None, ALU.mult, None, [b_const], [b_DG[uu]])
                            vts(DG[uu][:, 3 + t, :], ident_b[:, :], cwb_t[:, NGC + cc, t:t + 1], None, ALU.mult, None,
                                [b_const], [b_DG[uu]])
                        vcopy(UFG[uu][:, 0:2], ULAST[:, cc, :], [b_UL], [b_UFG[uu]])
                        vcopy(UFV[uu][:, 0:2], ULAST[:, NGC + cc, :], [b_UL], [b_UFV[uu]])

                    if c == 0:
                        d_prep(0, us)
                    for bi, j in enumerate(blocks):
                        if bi == 1 or (bi == 0 and len(blocks) == 1):
                            if c + 1 < NGC:
                                d_prep(c + 1, (us + 1) % 2)
                        qc0, nq = QB[j]
                        o = qc0 - hc0
                        xb = xt_bufs(qc0, nq)
                        pg, bpg = next_mm()
                        for kc in range(8):
                            mm(pg[:, 0:nq], WUP[ws][:, kc, 0:128], XT[:, kc, qc0:qc0 + nq], kc == 0, kc == 7,
                               [b_WUP[ws]] + xb, [bpg])
                        act(UFG[us][:, 2 + o:2 + o + nq], pg[:, 0:nq], AF.Copy, [bpg], [b_UFG[us]])
                        pv, bpv = next_mm()
                        for kc in range(8):
                            mm(pv[:, 0:nq], WUP[ws][:, kc, 128:256], XT[:, kc, qc0:qc0 + nq], kc == 0, kc == 7,
                               [b_WUP[ws]] + xb, [bpv])
                        vcopy(UFV[us][:, 2 + o:2 + o + nq], pv[:, 0:nq], [bpv], [b_UFV[us]])
                        pcg, bpcg = next_s()
                        for t in range(3):
                            mm(pcg[:, 0:nq], DG[us][:, t, :], UFG[us][:, o + t:o + t + nq], t == 0, t == 2,
                               [b_DG[us], b_UFG[us]], [bpcg])
                        pcv, bpcv = next_s()
                        for t in range(3):
                            mm(pcv[:, 0:nq], DG[us][:, 3 + t, :], UFV[us][:, o + t:o + t + nq], t == 0, t == 2,
                               [b_DG[us], b_UFV[us]], [bpcv])
                        act(SG[us][:, 0:nq], pcg[:, 0:nq], AF.Silu, [bpcg, b_const], [b_SG[us]], bias=cwb_t[:, c, 3:4])
                        vstt(AT[:, c, o:o + nq], pcv[:, 0:nq], cwb_t[:, NGC + c, 3:4], SG[us][:, 0:nq], ALU.add, ALU.mult,
                             [bpcv, b_const, b_SG[us]], [b_AT])
                    if not last_half:
                        vcopy(ULAST[:, c, :], UFG[us][:, hw_:hw_ + 2], [b_UFG[us]], [b_UL])
                        vcopy(ULAST[:, NGC + c, :], UFV[us][:, hw_:hw_ + 2], [b_UFV[us]], [b_UL])
                    else:
                        lc = seq.lk - 2
                        pm, bpm = next_mm()
                        for kc in range(8):
                            mm(pm[0:2, 0:256], XT[:, kc, lc:lc + 2], WUP[ws][:, kc, :], kc == 0, kc == 7,
                               [b_WUP[ws]] + xt_bufs(lc, 2), [bpm])
                        vcopy(CSO[us][0:2, 0:256], pm[0:2, 0:256], [bpm], [b_CSO[us]])
                        dma("sp", out_cv[:, c * 128:(c + 1) * 128], CSO[us][0:2, 0:128], [b_CSO[us]], [])
                        dma("sp", out_cv[:, DFF + c * 128:DFF + (c + 1) * 128], CSO[us][0:2, 128:256], [b_CSO[us]], [])
                for ti in tiles_in(NT, hc0, hw_):
                    c0, n = NT[ti]
                    s = tcount % 2
                    tcount += 1
                    o = c0 - hc0
                    dma("sp", dh2[s][0:n, :], h2_scr[c0 - nc0:c0 - nc0 + n, :], [b_h2scr], [b_dh2[s]])
                    for hw in range(2):
                        pm, bpm = next_mm()
                        for c in range(NGC):
                            mm(pm[0:n, :], AT[:, c, o:o + n], WD[:, c, hw * 512:(hw + 1) * 512], c == 0, c == NGC - 1,
                               [b_AT, b_WD], [bpm])
                        vtt(dh2[s][0:n, hw * 512:(hw + 1) * 512], dh2[s][0:n, hw * 512:(hw + 1) * 512], pm[0:n, :], ALU.add,
                            [b_dh2[s], bpm], [b_dh2[s]])
                    vmemset(dstat[s][0:n, 0:1], 0.0, [b_dstat[s]])
                    act(djunk[0:n, :], dh2[s][0:n, :], AF.Square, [b_dh2[s]], [b_djunk, b_dstat[s]],
                        accum_out=dstat[s][0:n, 0:1])
                    act(dstat[s][0:n, 1:2], dstat[s][0:n, 0:1], AF.Sqrt, [b_dstat[s], b_const], [b_dstat[s]],
                        bias=cst_t[0:n, 0:1], scale=1.0 / D)
                    vrecip(dstat[s][0:n, 2:3], dstat[s][0:n, 1:2], [b_dstat[s]], [b_dstat[s]])
                    vstt(dy[s][0:n, :], dh2[s][0:n, :], dstat[s][0:n, 2:3], gfin_t[0:n, :], ALU.mult, ALU.mult,
                         [b_dh2[s], b_dstat[s], b_const], [b_dy[s]])
                    if isP:
                        if c0 >= NMETA:
                            dma("sp", out_y[c0 - NMETA:c0 - NMETA + n, :], dy[s][0:n, :], [b_dy[s]], [])
                    else:
                        dma("sp", out_y[c0 - nc0:c0 - nc0 + n, :], dy[s][0:n, :], [b_dy[s]], [])
            P.fence()

    P.fence()
    P.emit(nc)
    return nc


_CFG = {"n_prompt": PB, "n_sample": SB, "phases": "ABMCD"}
_NC_CACHE = {}


def _constants():
    c = {}
    c["c_ident"] = np.eye(128, dtype=np.float32)
    p = np.arange(128)
    c["c_utri"] = (p[:, None] <= p[None, :]).astype(np.float32)
    c["c_ones"] = np.ones((128, 128), np.float32)
    c["c_mtri"] = (p[:, None] <= p[None, :]).astype(np.float32)
    c["c_mchk"] = ((p[:, None] // 64) <= (p[None, :] // 64)).astype(np.float32)
    inv = (10000.0 ** (-np.arange(0, ROPE, 2, dtype=np.float32) / np.float32(ROPE))).astype(np.float32)
    pos = np.arange(L, dtype=np.float32)
    ang = (pos[:, None] * inv[None, :]).astype(np.float32).astype(np.float64)
    cos = np.cos(ang).astype(np.float32)
    sin = np.sin(ang).astype(np.float32)
    c["c_cs_tm"] = np.ascontiguousarray(np.concatenate([cos, cos, -sin, sin], axis=1))
    c["c_cs_fm"] = np.ascontiguousarray(np.stack([np.concatenate([cos, cos], axis=1).T,
                                                  np.concatenate([sin, sin], axis=1).T], axis=0))
    return c


def kernel(x_prompt, x_sample, cache_fox_k, cache_fox_v, cache_fox_logf, cache_mla_ckv, cache_mla_krope,
           state_ffn_conv, meta_tokens, norm_mix_g, w_in, b_forget, mla_q_norm_g, w_uq, mla_kv_norm_g, w_ukv,
           w_o_fox, w_o_mla, w_out, norm_ffn_g, w_up, conv_w, conv_b, w_down, norm_final_g):
    f = lambda a: np.ascontiguousarray(np.asarray(a, dtype=np.float32))
    key = (_CFG["n_prompt"], _CFG["n_sample"], _CFG["phases"])
    if key not in _NC_CACHE:
        _NC_CACHE[key] = build_program(*key)
    nc = _NC_CACHE[key]
    rep = lambda v: np.ascontiguousarray(np.broadcast_to(f(v).reshape(1, -1), (128, f(v).size)))
    shared = {
        "meta_tokens": f(meta_tokens), "w_in": f(w_in)[0], "w_uq": f(w_uq)[0], "w_ukv": f(w_ukv)[0],
        "w_o_fox": f(w_o_fox)[0], "w_o_mla": f(w_o_mla)[0], "w_out": f(w_out)[0], "w_up": f(w_up)[0],
        "w_down": f(w_down)[0],
        "convwb": np.ascontiguousarray(np.concatenate([f(conv_w)[0], f(conv_b)[0][None, :]], axis=0)
                                       .reshape(4, NCH, 128).transpose(2, 1, 0)),
        "g_mix_bc": rep(norm_mix_g), "g_ffn_bc": rep(norm_ffn_g), "g_fin_bc": rep(norm_final_g),
        "g_q_bc": rep(mla_q_norm_g), "g_kv_bc": rep(mla_kv_norm_g), "b_forget_bc": rep(b_forget),
    }
    shared.update(_constants())
    xp = f(x_prompt)
    xs = f(x_sample)
    in_maps = []
    for c in range(N_CORES):
        m = dict(shared)
        m["x_prompt"] = xp[c * PB:(c + 1) * PB]
        m["x_sample"] = xs[c * SB:(c + 1) * SB]
        m["cache_fox_k"] = f(cache_fox_k)[0, c * SB:(c + 1) * SB].reshape(SB, PAST, FOXW)
        m["cache_fox_v"] = f(cache_fox_v)[0, c * SB:(c + 1) * SB].reshape(SB, PAST, FOXW)
        m["cache_fox_logf"] = f(cache_fox_logf)[0, c * SB:(c + 1) * SB]
        m["cache_mla_ckv"] = f(cache_mla_ckv)[0, c * SB:(c + 1) * SB]
        m["cache_mla_krope"] = f(cache_mla_krope)[0, c * SB:(c + 1) * SB]
        m["state_ffn_conv"] = f(state_ffn_conv)[0, c * SB:(c + 1) * SB]
        in_maps.append({k: np.ascontiguousarray(v) for k, v in m.items()})
    res = run_bass_kernel_spmd(nc, in_maps, core_ids=list(range(N_CORES)))
    R = res.results
    cat = lambda name: np.concatenate([np.asarray(r[name], dtype=np.float32) for r in R], axis=0)
    B = N_CORES * PB
    S = N_CORES * SB
    return (
        cat("y_prompt"), cat("y_sample"),
        cat("new_fox_k_p").reshape(1, B, L, H, HD), cat("new_fox_v_p").reshape(1, B, L, H, HD),
        cat("new_fox_logf_p").reshape(1, B, L, H), cat("new_mla_ckv_p").reshape(1, B, L, KVL),
        cat("new_mla_krope_p").reshape(1, B, L, ROPE), cat("new_ffn_conv_p").reshape(1, B, 2, UPW),
        cat("new_fox_k_s").reshape(1, S, DSEQ, H, HD), cat("new_fox_v_s").reshape(1, S, DSEQ, H, HD),
        cat("new_fox_logf_s").reshape(1, S, DSEQ, H), cat("new_mla_ckv_s").reshape(1, S, DSEQ, KVL),
        cat("new_mla_krope_s").reshape(1, S, DSEQ, ROPE), cat("new_ffn_conv_s").reshape(1, S, 2, UPW),
    )
```

```python
import math
import os
import numpy as np
import concourse.bass as bass
import concourse.mybir as mybir
from concourse.bass_utils import run_bass_kernel_spmd

F32 = mybir.dt.float32
BF16 = mybir.dt.bfloat16
AF = mybir.ActivationFunctionType
ALU = mybir.AluOpType

N_CORES = 8
D = 1024
SEQ = 2048
NMETA = 16
L = NMETA + SEQ
PB = 4
SB = 2
DSEQ = 16
PAST = 1024
H = 8
HD = 64
FOXW = 512
QL = 384
KVL = 256
ROPE = 32
NOPE = 64
MQK = 96
DFF = 2816
UPW = 2 * DFF
NCH = UPW // 128
NGC = DFF // 128
OFF_FK = 512
OFF_FV = 1024
OFF_FF = 1536
OFF_CQ = 1544
OFF_CKV = 1928
OFF_KR = 2184
OFF_GATE = 2216
INW = OFF_GATE + 2 * D
EPS = 1e-6
FOX_SCALE = 1.0 / math.sqrt(HD)
MLA_SCALE = 1.0 / math.sqrt(MQK)
LK = L

SBUF_BASE = 16640
SBUF_TOP = 229344


class Buf:
    __slots__ = ("name", "last_w", "rd_eng", "rd_dma", "excl")

    def __init__(self, name, excl=False):
        self.name = name
        self.excl = excl
        self.last_w = None
        self.rd_eng = {}
        self.rd_dma = []


class Op:
    __slots__ = ("eng", "fn", "dma", "deps", "signal", "sem", "semval", "prev")

    def __init__(self, eng, fn, dma):
        self.eng = eng
        self.fn = fn
        self.dma = dma
        self.deps = []
        self.signal = False
        self.sem = None
        self.semval = 0
        self.prev = 0


ENGS = ("pe", "act", "dve", "pool", "sp")
NDMASEM = {"sp": 20, "pool": 12, "act": 6}


class Prog:
    def __init__(self):
        self.ops = {e: [] for e in ENGS}
        self.last_op = {e: None for e in ENGS}
        self.last_real = {}
        self.dma_since_fence = {e: [] for e in ENGS}
        self.nops = 0

    def op(self, eng, fn, reads=(), writes=(), dma=False, extra=(), real=True):
        o = Op(eng, fn, dma)
        deps = {}
        xr = [b for b in reads if b.excl]
        if xr:
            reads = [b for b in reads if not b.excl]
            writes = list(writes) + [b for b in xr if b not in writes]
        for b in reads:
            w = b.last_w
            if w is not None:
                deps[id(w)] = (w, True)
        for b in writes:
            w = b.last_w
            if w is not None and id(w) not in deps:
                deps[id(w)] = (w, False)
            for r in b.rd_eng.values():
                if id(r) not in deps:
                    deps[id(r)] = (r, False)
            for r in b.rd_dma:
                if id(r) not in deps:
                    deps[id(r)] = (r, False)
        for d in extra:
            deps[id(d)] = (d, True)
        for d, raw in deps.values():
            if d is o:
                continue
            if d.dma or o.dma or d.eng != o.eng:
                need = True
            elif o.eng == "pe":
                need = False
            else:
                need = True
            if need:
                o.deps.append(d)
                d.signal = True
        for b in reads:
            if dma:
                b.rd_dma.append(o)
            else:
                b.rd_eng[eng] = o
        for b in writes:
            b.last_w = o
            b.rd_eng = {}
            b.rd_dma = []
        self.ops[eng].append(o)
        self.last_op[eng] = o
        if real and not dma:
            self.last_real[eng] = o
        if dma:
            self.dma_since_fence[eng].append(o)
        self.nops += 1
        return o

    def fence(self):
        dmas = []
        for e in ENGS:
            dmas.extend(self.dma_since_fence[e])
            self.dma_since_fence[e] = []
        lasts = [self.last_real[e] for e in ENGS if self.last_real.get(e) is not None]
        for e in ENGS:
            extra = [d for d in dmas] + [l for l in lasts if l.eng != e]
            self.op(e, lambda en: en.nop(nofuse=True), extra=extra, real=False)

    def emit(self, nc):
        sems = {}
        for e in ENGS:
            sems[e] = nc.alloc_semaphore("s_" + e)
        dsem = {e: [nc.alloc_semaphore("d_%s%d" % (e, i)) for i in range(n)] for e, n in NDMASEM.items()}
        for e in ENGS:
            cnt = 0
            di = 0
            dvals = [0] * NDMASEM.get(e, 1)
            for o in self.ops[e]:
                if o.dma:
                    k = di % NDMASEM[e]
                    di += 1
                    o.sem = dsem[e][k]
                    o.prev = dvals[k]
                    dvals[k] += 16
                    o.semval = dvals[k]
                elif o.signal:
                    cnt += 1
                    o.sem = sems[e]
                    o.semval = cnt
        engobj = {"pe": "tensor", "act": "scalar", "dve": "vector", "pool": "gpsimd", "sp": "sync"}
        ops = self.ops

        def make(e):
            def body(en):
                waited = {}
                for o in ops[e]:
                    for d in o.deps:
                        k = id(d.sem)
                        if waited.get(k, 0) < d.semval:
                            en.wait_ge(d.sem, d.semval)
                            waited[k] = d.semval
                    if o.dma and o.prev > 0:
                        k = id(o.sem)
                        if waited.get(k, 0) < o.prev:
                            en.wait_ge(o.sem, o.prev)
                            waited[k] = o.prev
                    inst = o.fn(en)
                    if o.dma:
                        inst.then_inc(o.sem, 16)
                    elif o.signal:
                        inst.then_inc(o.sem, 1)
            return body

        with nc.Block() as block:
            for e in ENGS:
                if ops[e]:
                    getattr(block, engobj[e])(make(e))


class Arena:
    def __init__(self, nc, base, top, tag):
        self.nc = nc
        self.base = base
        self.cur = base
        self.top = top
        self.tag = tag
        self.n = 0

    def alloc(self, shape, dtype, name):
        per = 1
        for s in shape[1:]:
            per *= s
        nbytes = per * (4 if dtype == F32 else 2)
        off = (self.cur + 31) // 32 * 32
        if off + nbytes > self.top:
            raise RuntimeError("SBUF arena %s overflow allocating %s %s (%d > %d)" % (self.tag, name, shape, off + nbytes, self.top))
        self.cur = off + nbytes
        self.n += 1
        return self.nc.alloc_sbuf_tensor_at("%s_%s_%d" % (self.tag, name, self.n), list(shape), dtype, offset=off)


class Seq:
    pass


def make_prompt_seq(b):
    s = Seq()
    s.kind = "p"
    s.b = b
    s.ncache = 0
    s.nt = [(0, NMETA)] + [(NMETA + 128 * i, 128) for i in range(16)]
    s.kt = list(s.nt)
    s.qb = [(0, NMETA)] + [(NMETA + 512 * j, 512) for j in range(4)]
    s.lk = L
    return s


def make_sample_seq(b):
    s = Seq()
    s.kind = "s"
    s.b = b
    s.ncache = PAST
    s.nt = [(PAST, DSEQ)]
    s.kt = [(128 * i, 128) for i in range(8)] + [(PAST, DSEQ)]
    s.qb = [(PAST, DSEQ)]
    s.lk = PAST + DSEQ
    return s


def vis(seq, j, i, kind):
    if seq.kind == "s":
        if i < 8:
            return ("full", 0, None)
        return ("diag", 0, "tri") if kind == "fox" else ("full", 0, None)
    if j == 0:
        if i != 0:
            return None
        return ("diag", 0, "tri") if kind == "fox" else ("full", 0, None)
    if i == 0:
        return ("full", 0, None)
    first = 4 * (j - 1) + 1
    if i < first:
        return ("full", 0, None)
    d = i - first
    if d > 3:
        return None
    return ("diag", 128 * d, "tri" if kind == "fox" else "chunk")


def build_program(n_prompt=PB, n_sample=SB, phases="ABCD"):
    nc = bass.Bass("TRN2", target_bir_lowering=False)
    P = Prog()

    def din(name, shape):
        return nc.dram_tensor(name, list(shape), F32, kind="ExternalInput").ap()

    def dout(name, shape):
        return nc.dram_tensor(name, list(shape), F32, kind="ExternalOutput").ap()

    x_p = din("x_prompt", [PB, SEQ, D])
    x_s = din("x_sample", [SB, DSEQ, D])
    c_fk = din("cache_fox_k", [SB, PAST, FOXW])
    c_fv = din("cache_fox_v", [SB, PAST, FOXW])
    c_lf = din("cache_fox_logf", [SB, PAST, H])
    c_ckv = din("cache_mla_ckv", [SB, PAST, KVL])
    c_kr = din("cache_mla_krope", [SB, PAST, ROPE])
    c_conv = din("state_ffn_conv", [SB, 2, UPW])
    meta = din("meta_tokens", [NMETA, D])
    w_in = din("w_in", [D, INW])
    w_uq = din("w_uq", [QL, H * MQK])
    w_ukv = din("w_ukv", [KVL, H * 128])
    w_ofox = din("w_o_fox", [FOXW, D])
    w_omla = din("w_o_mla", [FOXW, D])
    w_out = din("w_out", [D, D])
    w_up = din("w_up", [D, UPW])
    w_down = din("w_down", [DFF, D])
    convwb = din("convwb", [128, NCH, 4])
    g_mix = din("g_mix_bc", [128, D])
    g_ffn = din("g_ffn_bc", [128, D])
    g_fin = din("g_fin_bc", [128, D])
    g_q = din("g_q_bc", [128, QL])
    g_kv = din("g_kv_bc", [128, KVL])
    b_fg = din("b_forget_bc", [128, H])
    c_ident = din("c_ident", [128, 128])
    c_utri = din("c_utri", [128, 128])
    c_ones = din("c_ones", [128, 128])
    c_mtri = din("c_mtri", [128, 128])
    c_mchk = din("c_mchk", [128, 128])
    c_cs_tm = din("c_cs_tm", [L, 64])
    c_cs_fm = din("c_cs_fm", [2, ROPE, L])

    o_y_p = dout("y_prompt", [PB, SEQ, D])
    o_y_s = dout("y_sample", [SB, DSEQ, D])
    o_fk_p = dout("new_fox_k_p", [PB, L, FOXW])
    o_fv_p = dout("new_fox_v_p", [PB, L, FOXW])
    o_lf_p = dout("new_fox_logf_p", [PB, L, H])
    o_ckv_p = dout("new_mla_ckv_p", [PB, L, KVL])
    o_kr_p = dout("new_mla_krope_p", [PB, L, ROPE])
    o_cv_p = dout("new_ffn_conv_p", [PB, 2, UPW])
    o_fk_s = dout("new_fox_k_s", [SB, DSEQ, FOXW])
    o_fv_s = dout("new_fox_v_s", [SB, DSEQ, FOXW])
    o_lf_s = dout("new_fox_logf_s", [SB, DSEQ, H])
    o_ckv_s = dout("new_mla_ckv_s", [SB, DSEQ, KVL])
    o_kr_s = dout("new_mla_krope_s", [SB, DSEQ, ROPE])
    o_cv_s = dout("new_ffn_conv_s", [SB, 2, UPW])
    h2_scr = nc.dram_tensor("h2_scratch", [L, D], F32, kind="Internal").ap()

    ps_mm = [nc.alloc_psum_tensor("ps_mm%d" % i, [128, 512], F32) for i in range(3)]
    ps_s = [nc.alloc_psum_tensor("ps_s%d" % i, [128, 512], F32) for i in range(2)]
    ps_acc = [nc.alloc_psum_tensor("ps_acc%d" % i, [128, 512], F32) for i in range(2)]
    ps_t = nc.alloc_psum_tensor("ps_t", [128, 1024], BF16)
    b_ps_mm = [Buf("ps_mm%d" % i, True) for i in range(3)]
    b_ps_s = [Buf("ps_s%d" % i, True) for i in range(2)]
    b_ps_acc = [Buf("ps_acc%d" % i, True) for i in range(2)]
    b_ps_t = Buf("ps_t", True)
    rr = {"mm": 0, "s": 0, "acc": 0}

    mm_all = [(ps_mm[i], b_ps_mm[i]) for i in range(3)]
    mm_wide7 = mm_all + [(ps_s[i], b_ps_s[i]) for i in range(2)] + [(ps_acc[i], b_ps_acc[i]) for i in range(2)]
    mm_wide5 = mm_all + [(ps_acc[i], b_ps_acc[i]) for i in range(2)]
    mmset = {"banks": mm_all}

    def next_mm():
        bk = mmset["banks"]
        i = rr["mm"] % len(bk)
        rr["mm"] += 1
        return bk[i]

    def next_s():
        i = rr["s"] % 2
        rr["s"] += 1
        return ps_s[i], b_ps_s[i]

    def next_acc():
        i = rr["acc"] % 2
        rr["acc"] += 1
        return ps_acc[i], b_ps_acc[i]

    A0 = Arena(nc, SBUF_BASE, SBUF_TOP, "pers")
    ident_f = A0.alloc([128, 128], F32, "identf")
    utri_f = A0.alloc([128, 128], F32, "utri")
    ones_f = A0.alloc([128, 128], F32, "ones")
    ident_b = A0.alloc([128, 128], BF16, "identb")
    mtri_b = A0.alloc([128, 128], BF16, "mtri")
    mchk_b = A0.alloc([128, 128], BF16, "mchk")
    gmix_t = A0.alloc([128, D], F32, "gmix")
    gffn_t = A0.alloc([128, D], F32, "gffn")
    gfin_t = A0.alloc([128, D], F32, "gfin")
    gq_t = A0.alloc([128, QL], F32, "gq")
    gkv_t = A0.alloc([128, KVL], F32, "gkv")
    bfg_t = A0.alloc([128, H], F32, "bfg")
    cwb_t = A0.alloc([128, NCH, 4], F32, "cwb")
    wuq_t = A0.alloc([128, 3, H * MQK], BF16, "wuq")
    wuqr_t = A0.alloc([128, 3, H * MQK], BF16, "wuqr")
    wukvk_t = A0.alloc([128, 2, FOXW], BF16, "wukvk")
    wukvv_t = A0.alloc([128, 2, FOXW], BF16, "wukvv")
    cst_t = A0.alloc([128, 4], F32, "cst")
    b_const = Buf("const")
    PERS_END = A0.cur

    A1 = Arena(nc, PERS_END, SBUF_TOP, "mid")
    XT = A1.alloc([128, 8, L], BF16, "XT")
    OTF = A1.alloc([128, 4, L], BF16, "OTF")
    OTM = A1.alloc([128, 4, L], BF16, "OTM")
    PH_BASE = A1.cur
    OT_BASE = PH_BASE - 2 * (4 * L * 2)
    b_h2scr = Buf("h2scr")
    b_XT = {}
    b_OTF = {}
    b_OTM = {}

    def bXT(c0):
        return b_XT.setdefault(c0, Buf("XT%d" % c0))

    def bOTF(c0):
        return b_OTF.setdefault(c0, Buf("OTF%d" % c0))

    def bOTM(c0):
        return b_OTM.setdefault(c0, Buf("OTM%d" % c0))

    def dma(eng, out, in_, reads, writes):
        return P.op(eng, lambda en: en.dma_start(out=out, in_=in_), reads=reads, writes=writes, dma=True)

    def mm(out, lhsT, rhs, start, stop, reads, writes):
        return P.op("pe", lambda en: en.matmul(out, lhsT=lhsT, rhs=rhs, start=start, stop=stop),
                    reads=reads, writes=writes)

    def tr(out, in_, ident, reads, writes):
        return P.op("pe", lambda en: en.transpose(out, in_, ident), reads=reads, writes=writes)

    def act(out, in_, func, reads, writes, bias=None, scale=None, accum_out=None):
        kw = {}
        if bias is not None:
            kw["bias"] = bias
        if scale is not None:
            kw["scale"] = scale
        if accum_out is not None:
            kw["accum_out"] = accum_out
        return P.op("act", lambda en: en.activation(out=out, in_=in_, func=func, **kw), reads=reads, writes=writes)

    def vcopy(out, in_, reads, writes, eng="dve"):
        return P.op(eng, lambda en: en.tensor_copy(out=out, in_=in_), reads=reads, writes=writes)

    def vtt(out, in0, in1, op, reads, writes, eng="dve"):
        return P.op(eng, lambda en: en.tensor_tensor(out=out, in0=in0, in1=in1, op=op), reads=reads, writes=writes)

    def vts(out, in0, s1, s2, op0, op1, reads, writes, eng="dve"):
        if op1 is None:
            return P.op(eng, lambda en: en.tensor_scalar(out=out, in0=in0, scalar1=s1, scalar2=None, op0=op0),
                        reads=reads, writes=writes)
        return P.op(eng, lambda en: en.tensor_scalar(out=out, in0=in0, scalar1=s1, scalar2=s2, op0=op0, op1=op1),
                    reads=reads, writes=writes)

    def vstt(out, in0, scalar, in1, op0, op1, reads, writes):
        return P.op("dve", lambda en: en.scalar_tensor_tensor(out=out, in0=in0, scalar=scalar, in1=in1, op0=op0, op1=op1),
                    reads=reads, writes=writes)

    def vmemset(ap, val, writes, eng="dve"):
        return P.op(eng, lambda en: en.memset(ap, val), writes=writes)

    def vrecip(out, in_, reads, writes):
        return P.op("dve", lambda en: en.reciprocal(out=out, in_=in_), reads=reads, writes=writes)

    AS = Arena(nc, PH_BASE, SBUF_TOP, "setup")
    vmemset(cst_t[:, 0:1], EPS, [b_const])
    vmemset(cst_t[:, 1:2], 1.0, [b_const])
    vmemset(cst_t[:, 2:3], 0.0, [b_const])
    b_stage = Buf("setup_stage")
    for t, src in ((ident_f, c_ident), (utri_f, c_utri), (ones_f, c_ones), (gmix_t, g_mix), (gffn_t, g_ffn),
                   (gfin_t, g_fin), (gq_t, g_q), (gkv_t, g_kv), (bfg_t, b_fg)):
        dma("sp", t[:], src[:, :], [], [b_const])
    STAGE = int(os.environ.get("KSTAGE", "9"))
    for t, src in ((ident_b, c_ident), (mtri_b, c_mtri), (mchk_b, c_mchk)):
        if STAGE >= 2:
            dma("pool", t[:], src[:, :], [], [b_const])
    if STAGE >= 3:
        dma("pool", wuq_t[:], w_uq.rearrange("(kc p) f -> p kc f", p=128), [], [b_const])
    for kc in range(2 if STAGE >= 4 else 0):
        src = w_ukv[kc * 128:(kc + 1) * 128, :].rearrange("p (h x) -> p h x", x=128)
        dma("pool", wukvk_t[:, kc, :].rearrange("p (h x) -> p h x", x=64), src[:, :, 0:64], [], [b_const])
        dma("pool", wukvv_t[:, kc, :].rearrange("p (h x) -> p h x", x=64), src[:, :, 64:128], [], [b_const])
    wq4 = wuq_t[:].rearrange("p kc (h x) -> p kc h x", x=MQK)
    wr4 = wuqr_t[:].rearrange("p kc (h x) -> p kc h x", x=MQK)
    vmemset(wuqr_t[:], 0.0, [b_const])
    for kc in range(3 if STAGE >= 5 else 0):
        P.op("dve", (lambda kc: lambda en: en.tensor_scalar(out=wr4[:, kc, :, 64:80], in0=wq4[:, kc, :, 80:96], scalar1=-1.0,
                                                             scalar2=None, op0=ALU.mult))(kc), reads=[b_const], writes=[b_const])
        vcopy(wr4[:, kc, :, 80:96], wq4[:, kc, :, 64:80], [b_const], [b_const])
    dma("sp", cwb_t[:], convwb[:, :, :], [], [b_const])
    P.fence()

    seqs = [make_prompt_seq(b) for b in range(n_prompt)] + [make_sample_seq(b) for b in range(n_sample)]
    if phases.startswith('S'):
        seqs = []

    wb_state = {"i": 0}

    for seq in seqs:
        isP = seq.kind == "p"
        b = seq.b
        if isP:
            out_fk, out_fv, out_lf, out_ckv, out_kr, out_cv, out_y = (o_fk_p[b], o_fv_p[b], o_lf_p[b], o_ckv_p[b],
                                                                     o_kr_p[b], o_cv_p[b], o_y_p[b])
        else:
            out_fk, out_fv, out_lf, out_ckv, out_kr, out_cv, out_y = (o_fk_s[b], o_fv_s[b], o_lf_s[b], o_ckv_s[b],
                                                                     o_kr_s[b], o_cv_s[b], o_y_s[b])
        nc0 = seq.ncache
        NT = seq.nt
        KT = seq.kt
        QB = seq.qb
        NKT = len(KT)
        NQB = len(QB)

        def src_rows(c0, n):
            if not isP:
                return x_s[b, c0 - nc0:c0 - nc0 + n, :]
            if c0 == 0:
                return meta[0:n, :]
            return x_p[b, c0 - NMETA:c0 - NMETA + n, :]

        def out_rows(ap, c0, n):
            return ap[c0 - nc0:c0 - nc0 + n, :]

        AB = Arena(nc, PH_BASE, SBUF_TOP, "ab%s%d" % (seq.kind, b))
        WB = [AB.alloc([128, 8, 512], BF16, "wb%d" % i) for i in range(3)]
        b_WB = [Buf("wb%d" % i) for i in range(3)]

        def load_w_group(col0, ncols):
            i = wb_state["i"] % 3
            wb_state["i"] += 1
            dma("pool", WB[i][:, :, 0:ncols], w_in.rearrange("(kc p) f -> p kc f", p=128)[:, :, col0:col0 + ncols],
                [], [b_WB[i]])
            return WB[i], b_WB[i]

        xin = [AB.alloc([128, D], F32, "xin%d" % i) for i in range(2)]
        b_xin = [Buf("xin%d" % i) for i in range(2)]
        junk = AB.alloc([128, D], BF16, "junk")
        b_junk = Buf("junk")
        xnb = [AB.alloc([128, D], BF16, "xnb%d" % i) for i in range(2)]
        b_xnb = [Buf("xnb%d" % i) for i in range(2)]
        stat = [AB.alloc([128, 4], F32, "stat%d" % i) for i in range(2)]
        b_stat = [Buf("stat%d" % i) for i in range(2)]
        ost = [AB.alloc([128, 512], F32, "ost%d" % i) for i in range(2)]
        b_ost = [Buf("ost%d" % i) for i in range(2)]
        LOGF = AB.alloc([128, NKT, H], F32, "logf")
        b_LOGF = [Buf("logf%d" % i) for i in range(NKT)]
        FCUM = AB.alloc([128, NKT, H], F32, "fcum")
        b_FCUM = [Buf("fcum%d" % i) for i in range(NKT)]
        CREF = AB.alloc([128, NQB, H], F32, "cref")
        b_CREF = [Buf("cref%d" % j) for j in range(NQB)]
        BIAS = AB.alloc([128, NQB, NKT, H], F32, "bias")
        b_BIAS = [Buf("bias%d" % j) for j in range(NQB)]
        PT = [AB.alloc([128, 512], BF16, "pt%d" % i) for i in range(4)]
        b_PT = [Buf("pt%d" % i) for i in range(4)]
        OSB = [AB.alloc([128, 512], F32, "osb%d" % i) for i in range(2)]
        b_OSB = [Buf("osb%d" % i) for i in range(2)]
        ATT_BASE = AB.cur
        AF_ = Arena(nc, ATT_BASE, SBUF_TOP, "fox%s%d" % (seq.kind, b))
        FQT = AF_.alloc([128, 4, LK], BF16, "fqt")
        FKT = AF_.alloc([128, 4, LK], BF16, "fkt")
        VF = AF_.alloc([128, NKT, H, 65], BF16, "vf")
        b_FQT = [Buf("fqt%d" % j) for j in range(NQB)]
        b_FKT = [Buf("fkt%d" % i) for i in range(NKT)]
        b_VF = [Buf("vf%d" % i) for i in range(NKT)]
        b_VF1 = Buf("vf_ones")

        for ti, (c0, n) in enumerate(NT):
            s = ti % 2
            dma("sp", xin[s][0:n, :], src_rows(c0, n), [], [b_xin[s]])
            vmemset(stat[s][0:n, 0:1], 0.0, [b_stat[s]])
            act(junk[0:n, :], xin[s][0:n, :], AF.Square, [b_xin[s]], [b_junk, b_stat[s]], accum_out=stat[s][0:n, 0:1])
            act(stat[s][0:n, 1:2], stat[s][0:n, 0:1], AF.Sqrt, [b_stat[s], b_const], [b_stat[s]],
                bias=cst_t[0:n, 0:1], scale=1.0 / D)
            vrecip(stat[s][0:n, 2:3], stat[s][0:n, 1:2], [b_stat[s]], [b_stat[s]])
            vstt(xnb[s][0:n, :], xin[s][0:n, :], stat[s][0:n, 2:3], gmix_t[0:n, :], ALU.mult, ALU.mult,
                 [b_xin[s], b_stat[s], b_const], [b_xnb[s]])
            for kc in range(8):
                tr(ps_t[:, kc * 128:kc * 128 + n], xnb[s][0:n, kc * 128:(kc + 1) * 128], ident_b[0:n, 0:n],
                   [b_xnb[s], b_const], [b_ps_t])
            vcopy(XT[:, :, c0:c0 + n], ps_t[:].rearrange("p (kc t) -> p kc t", t=128)[:, :, 0:n], [b_ps_t], [bXT(c0)])
        def tiles_in(lst, c0, n):
            return [i for i, (t0, tn) in enumerate(lst) if t0 < c0 + n and c0 < t0 + tn]

        def xt_bufs(c0, n):
            return [bXT(NT[i][0]) for i in tiles_in(NT, c0, n)]

        def pipelined(items, head, tail):
            prev = None
            for it in items:
                gen = head(it)
                next(gen)
                if prev is not None:
                    tail(prev)
                for _ in gen:
                    pass
                prev = it
            if prev is not None:
                tail(prev)

        ev = {"i": 0}

        def evac(out, in_, reads, writes):
            ev["i"] += 1
            if ev["i"] % 2:
                return act(out, in_, AF.Copy, reads, writes)
            return vcopy(out, in_, reads, writes)

        def tm_proj(Wt, bW, c0, n, wcol0, ncols):
            pm, bpm = next_mm()
            for kc in range(8):
                mm(pm[0:n, 0:ncols], XT[:, kc, c0:c0 + n], Wt[:, kc, wcol0:wcol0 + ncols], kc == 0, kc == 7,
                   [bXT(c0), bW], [bpm])
            return pm, bpm

        def fm_proj(Wt, bW, wcol0, m, qc0, nq):
            pm, bpm = next_mm()
            xb = xt_bufs(qc0, nq)
            for kc in range(8):
                mm(pm[0:m, 0:nq], Wt[:, kc, wcol0:wcol0 + m], XT[:, kc, qc0:qc0 + nq], kc == 0, kc == 7,
                   xb + [bW], [bpm])
            return pm, bpm

        KI0 = NKT - len(NT)

        vmemset(VF[:, :, :, 64:65], 1.0, b_VF + [b_VF1])
        if not isP:
            for i in range(8):
                s = i % 2
                dma("pool", xnb[s][:, 0:512], c_fk[b, i * 128:(i + 1) * 128, :], [], [b_xnb[s]])
                for g in range(4):
                    tr(ps_t[:, g * 128:(g + 1) * 128], xnb[s][:, g * 128:(g + 1) * 128], ident_b[:, :],
                       [b_xnb[s], b_const], [b_ps_t])
                vcopy(FKT[:, :, i * 128:(i + 1) * 128], ps_t[:, 0:512].rearrange("p (g t) -> p g t", t=128),
                      [b_ps_t], [b_FKT[i]])
                dma("pool", VF[:, i, :, 0:64], c_fv[b, i * 128:(i + 1) * 128, :].rearrange("p (h x) -> p h x", x=64),
                    [], [b_VF[i]])
            dma("sp", LOGF[:, 0:8, :], c_lf[b].rearrange("(i p) h -> p i h", p=128), [], b_LOGF[0:8])

        mmset["banks"] = mm_wide7
        Wt, bW = load_w_group(OFF_FK, 512)
        Wv, bWv = load_w_group(OFF_FV, 512)
        Wq, bWq = load_w_group(0, 512)
        for ti, (c0, n) in enumerate(NT):
            s = ti % 2
            pm, bpm = tm_proj(Wt, bW, c0, n, 0, 512)
            evac(ost[s][0:n, :], pm[0:n, 0:512], [bpm], [b_ost[s]])
            dma("sp", out_rows(out_fk, c0, n), ost[s][0:n, :], [b_ost[s]], [])
        for j, (qc0, nq) in enumerate(QB):
            kts = tiles_in(KT, qc0, nq)
            for g in range(4):
                pm, bpm = fm_proj(Wt, bW, g * 128, 128, qc0, nq)
                evac(FKT[:, g, qc0:qc0 + nq], pm[:, 0:nq], [bpm], [b_FKT[i] for i in kts])
        for ti, (c0, n) in enumerate(NT):
            s = ti % 2
            ki = KI0 + ti
            pm, bpm = tm_proj(Wv, bWv, c0, n, 0, 512)
            act(ost[s][0:n, :], pm[0:n, 0:512], AF.Copy, [bpm], [b_ost[s]])
            vcopy(VF[0:n, ki, :, 0:64], pm[0:n, 0:512].rearrange("p (h x) -> p h x", x=64), [bpm], [b_VF[ki]])
            dma("sp", out_rows(out_fv, c0, n), ost[s][0:n, :], [b_ost[s]], [])
        for j, (qc0, nq) in enumerate(QB):
            for g in range(4):
                pm, bpm = fm_proj(Wq, bWq, g * 128, 128, qc0, nq)
                evac(FQT[:, g, qc0:qc0 + nq], pm[:, 0:nq], [bpm], [b_FQT[j]])
        Wf, bWf = load_w_group(OFF_FF, 8)
        for ti, (c0, n) in enumerate(NT):
            s = ti % 2
            ki = KI0 + ti
            pm, bpm = tm_proj(Wf, bWf, c0, n, 0, 8)
            z = ost[s]
            vtt(z[0:n, 0:8], pm[0:n, 0:8], bfg_t[0:n, :], ALU.add, [bpm, b_const], [b_ost[s]])
            act(z[0:n, 8:16], z[0:n, 0:8], AF.Exp, [b_ost[s]], [b_ost[s]], scale=-1.0)
            act(z[0:n, 16:24], z[0:n, 8:16], AF.Ln, [b_ost[s], b_const], [b_ost[s]], bias=cst_t[0:n, 1:2])
            vts(LOGF[0:n, ki, :], z[0:n, 16:24], -1.0, None, ALU.mult, None, [b_ost[s]], [b_LOGF[ki]])
            dma("sp", out_rows(out_lf, c0, n), LOGF[0:n, ki, :], [b_LOGF[ki]], [])
        pm, bpm = next_mm()
        vmemset(pm[:, 0:NKT * 8], 0.0, [bpm])
        for i, (kc0, nk) in enumerate(KT):
            for jj in range(i):
                nj = KT[jj][1]
                mm(pm[0:nk, i * 8:(i + 1) * 8], ones_f[0:nj, 0:nk], LOGF[0:nj, jj, :], jj == 0, False,
                   [b_LOGF[jj], b_const], [bpm])
            mm(pm[0:nk, i * 8:(i + 1) * 8], utri_f[0:nk, 0:nk], LOGF[0:nk, i, :], i == 0, True,
               [b_LOGF[i], b_const], [bpm])
        vcopy(FCUM[:].rearrange("p i h -> p (i h)"), pm[:, 0:NKT * 8], [bpm], b_FCUM)
        pm, bpm = next_mm()
        vmemset(pm[:, 0:NQB * 8], 0.0, [bpm])
        for j in range(NQB):
            if isP:
                upto = 0 if j == 0 else 4 * (j - 1) + 3
            else:
                upto = 8
            if upto == 0:
                continue
            for jj in range(upto):
                nj = KT[jj][1]
                mm(pm[:, j * 8:(j + 1) * 8], ones_f[0:nj, :], LOGF[0:nj, jj, :], jj == 0, jj == upto - 1,
                   [b_LOGF[jj], b_const], [bpm])
        vcopy(CREF[:].rearrange("p j h -> p (j h)"), pm[:, 0:NQB * 8], [bpm], b_CREF)
        for j in range(NQB):
            zero_ref = isP and j == 0
            for i, (kc0, nk) in enumerate(KT):
                if vis(seq, j, i, "fox") is None:
                    continue
                if zero_ref:
                    vts(BIAS[0:nk, j, i, :], FCUM[0:nk, i, :], -1.0, None, ALU.mult, None, [b_FCUM[i]], [b_BIAS[j]])
                else:
                    vtt(BIAS[0:nk, j, i, :], CREF[0:nk, j, :], FCUM[0:nk, i, :], ALU.subtract,
                        [b_CREF[j], b_FCUM[i]], [b_BIAS[j]])

        mmset["banks"] = mm_all
        ptc = {"i": 0, "o": 0}

        def attention(kind, kdim, Kap, bK, Qap, bQ, Vap, bV, OT, bOT, scale, heads):
            groups = []
            for h in heads:
                for j in range(NQB):
                    vl = [(i, vis(seq, j, i, kind)) for i in range(NKT)]
                    vl = [(i, v) for i, v in vl if v is not None]
                    groups.append((h, j, vl))
            pairs = []
            for gi, (h, j, vl) in enumerate(groups):
                for idx, (i, v) in enumerate(vl):
                    pairs.append((gi, h, j, idx, i, v, idx == len(vl) - 1))
            sinfo = {}
            ginfo = {}

            def emit_S(p):
                gi, h, j, idx, i, v, last = pairs[p]
                qc0, nq = QB[j]
                kc0, nk = KT[i]
                c0 = v[1]
                w = nq - c0
                ps, bps = next_s()
                mm(ps[0:nk, 0:w], Kap(i, h), Qap(j, h, c0), True, True, [bK(i, h), bQ(j, h)], [bps])
                sinfo[p] = (ps, bps)

            def emit_rest(p):
                gi, h, j, idx, i, v, last = pairs[p]
                qc0, nq = QB[j]
                kc0, nk = KT[i]
                c0 = v[1]
                w = nq - c0
                ps, bps = sinfo.pop(p)
                if idx == 0:
                    ginfo[gi] = next_acc()
                acc, bacc = ginfo[gi]
                k = ptc["i"] % 4
                ptc["i"] += 1
                if kind == "fox":
                    act(PT[k][0:nk, 0:w], ps[0:nk, 0:w], AF.Exp, [bps, b_BIAS[j]], [b_PT[k]],
                        bias=BIAS[0:nk, j, i, h:h + 1], scale=scale)
                else:
                    act(PT[k][0:nk, 0:w], ps[0:nk, 0:w], AF.Exp, [bps, b_const], [b_PT[k]],
                        bias=cst_t[0:nk, 2:3], scale=scale)
                if v[0] == "diag":
                    mw = min(128, w)
                    mt = mtri_b if v[2] == "tri" else mchk_b
                    vtt(PT[k][0:nk, 0:mw], PT[k][0:nk, 0:mw], mt[0:nk, 0:mw], ALU.mult,
                        [b_PT[k], b_const], [b_PT[k]])
                mm(acc[0:65, c0:nq], Vap(i, h), PT[k][0:nk, 0:w], idx == 0, last,
                   [bV(i, h), b_PT[k]], [bacc])

            def emit_norm(gi):
                h, j, vl = groups[gi]
                g, sl = h // 2, h % 2
                qc0, nq = QB[j]
                acc, bacc = ginfo.pop(gi)
                o = ptc["o"] % 2
                ptc["o"] += 1
                vcopy(OSB[o][0:65, 0:nq], acc[0:65, 0:nq], [bacc], [b_OSB[o]])
                vrecip(OSB[o][64:65, 0:nq], OSB[o][64:65, 0:nq], [b_OSB[o]], [b_OSB[o]])
                pm, bpm = next_mm()
                mm(pm[0:64, 0:nq], ones_f[64:65, 0:64], OSB[o][64:65, 0:nq], True, True, [b_OSB[o], b_const], [bpm])
                vtt(OT[sl * 64:(sl + 1) * 64, g, qc0:qc0 + nq], OSB[o][0:64, 0:nq], pm[0:64, 0:nq], ALU.mult,
                    [b_OSB[o], bpm], [bOT(qc0)])

            pending = None
            emit_S(0)
            for p in range(len(pairs)):
                if p + 1 < len(pairs):
                    emit_S(p + 1)
                emit_rest(p)
                gi, h, j, idx, i, v, last = pairs[p]
                if pending is not None and (idx >= 3 or last):
                    emit_norm(pending)
                    pending = None
                if last:
                    pending = gi
            if pending is not None:
                emit_norm(pending)

        if "B" in phases:
            attention(
                "fox", 64,
                lambda i, h: FKT[(h % 2) * 64:(h % 2) * 64 + 64, h // 2, KT[i][0]:KT[i][0] + KT[i][1]],
                lambda i, h: b_FKT[i],
                lambda j, h, c0: FQT[(h % 2) * 64:(h % 2) * 64 + 64, h // 2, QB[j][0] + c0:QB[j][0] + QB[j][1]],
                lambda j, h: b_FQT[j],
                lambda i, h: VF[0:KT[i][1], i, h, :],
                lambda i, h: b_VF[i],
                OTF, bOTF, FOX_SCALE, range(H))
        P.fence()
        if "M" in phases:
            mmset["banks"] = mm_wide7
            AM = Arena(nc, ATT_BASE, SBUF_TOP, "mla%s%d" % (seq.kind, b))
            CQT = AM.alloc([128, 3, LK], BF16, "cqt")
            CKVT = AM.alloc([128, 2, LK], BF16, "ckvt")
            KRT = AM.alloc([128, LK], BF16, "krt")
            QP = AM.alloc([128, 2, LK], BF16, "qp")
            KP = AM.alloc([128, 2, LK], BF16, "kp")
            VM = AM.alloc([128, NKT, 2, 65], BF16, "vm")
            CSF = AM.alloc([128, 2, 512], F32, "csf")
            RT = AM.alloc([128, 2, 512], F32, "rt")
            CST = [AM.alloc([128, 64], F32, "cst%d" % i) for i in range(2)]
            b_CQT = [Buf("cqt%d" % j) for j in range(NQB)]
            b_CKVT = [Buf("ckvt%d" % i) for i in range(NKT)]
            b_KRT = [Buf("krt%d" % i) for i in range(NKT)]
            b_QP = [Buf("qp%d" % j) for j in range(NQB)]
            b_KP = [Buf("kp%d" % i) for i in range(NKT)]
            b_VM = [Buf("vm%d" % i) for i in range(NKT)]
            b_CSF = Buf("csf")
            b_RT = Buf("rt")
            b_CST = [Buf("cst%d" % i) for i in range(2)]
            vmemset(VM[:, :, :, 64:65], 1.0, b_VM)

            def qb_of(c0):
                return [j for j, (q0, qn) in enumerate(QB) if q0 <= c0 < q0 + qn][0]

            if not isP:
                for i in range(8):
                    s = i % 2
                    dma("pool", xnb[s][:, 0:256], c_ckv[b, i * 128:(i + 1) * 128, :], [], [b_xnb[s]])
                    dma("pool", xnb[s][:, 256:288], c_kr[b, i * 128:(i + 1) * 128, :], [], [b_xnb[s]])
                    for kc in range(2):
                        tr(ps_t[:, kc * 128:(kc + 1) * 128], xnb[s][:, kc * 128:(kc + 1) * 128], ident_b[:, :],
                           [b_xnb[s], b_const], [b_ps_t])
                    tr(ps_t[0:32, 256:384], xnb[s][:, 256:288], ident_b[:, :], [b_xnb[s], b_const], [b_ps_t])
                    vcopy(CKVT[:, :, i * 128:(i + 1) * 128], ps_t[:, 0:256].rearrange("p (g t) -> p g t", t=128),
                          [b_ps_t], [b_CKVT[i]])
                    vcopy(KRT[64:96, i * 128:(i + 1) * 128], ps_t[0:32, 256:384], [b_ps_t], [b_KRT[i]])
            Wc, bWc = load_w_group(OFF_CQ, QL)
            Wk, bWk = load_w_group(OFF_CKV, KVL + ROPE)
            def cq_head(ti):
                c0, n = NT[ti]
                s = ti % 2
                pm, bpm = tm_proj(Wc, bWc, c0, n, 0, QL)
                yield
                vmemset(stat[s][0:n, 0:1], 0.0, [b_stat[s]])
                act(junk[0:n, 0:QL], pm[0:n, 0:QL], AF.Square, [bpm], [b_junk, b_stat[s]], accum_out=stat[s][0:n, 0:1])
                act(stat[s][0:n, 1:2], stat[s][0:n, 0:1], AF.Sqrt, [b_stat[s], b_const], [b_stat[s]],
                    bias=cst_t[0:n, 0:1], scale=1.0 / QL)
                vrecip(stat[s][0:n, 2:3], stat[s][0:n, 1:2], [b_stat[s]], [b_stat[s]])
                vstt(xnb[s][0:n, 0:QL], pm[0:n, 0:QL], stat[s][0:n, 2:3], gq_t[0:n, :], ALU.mult, ALU.mult,
                     [bpm, b_stat[s], b_const], [b_xnb[s]])

            def cq_tail(ti):
                c0, n = NT[ti]
                s = ti % 2
                for kc in range(3):
                    tr(ps_t[:, kc * 128:kc * 128 + n], xnb[s][0:n, kc * 128:(kc + 1) * 128], ident_b[0:n, 0:n],
                       [b_xnb[s], b_const], [b_ps_t])
                vcopy(CQT[:, :, c0:c0 + n], ps_t[:, 0:384].rearrange("p (kc t) -> p kc t", t=128)[:, :, 0:n],
                      [b_ps_t], [b_CQT[qb_of(c0)]])

            pipelined(range(len(NT)), cq_head, cq_tail)
            def kv_head(ti):
                c0, n = NT[ti]
                s = ti % 2
                ki = KI0 + ti
                dma("sp", CST[s][0:n, :], c_cs_tm[c0:c0 + n, :], [], [b_CST[s]])
                pm, bpm = tm_proj(Wk, bWk, c0, n, 0, KVL + ROPE)
                yield
                vmemset(stat[s][0:n, 0:1], 0.0, [b_stat[s]])
                act(junk[0:n, 0:KVL], pm[0:n, 0:KVL], AF.Square, [bpm], [b_junk, b_stat[s]], accum_out=stat[s][0:n, 0:1])
                act(stat[s][0:n, 1:2], stat[s][0:n, 0:1], AF.Sqrt, [b_stat[s], b_const], [b_stat[s]],
                    bias=cst_t[0:n, 0:1], scale=1.0 / KVL)
                vrecip(stat[s][0:n, 2:3], stat[s][0:n, 1:2], [b_stat[s]], [b_stat[s]])
                o = ost[s]
                vstt(o[0:n, 0:KVL], pm[0:n, 0:KVL], stat[s][0:n, 2:3], gkv_t[0:n, :], ALU.mult, ALU.mult,
                     [bpm, b_stat[s], b_const], [b_ost[s]])
                vtt(o[0:n, 256:288], pm[0:n, 256:288], CST[s][0:n, 0:32], ALU.mult, [bpm, b_CST[s]], [b_ost[s]])
                vtt(o[0:n, 288:304], pm[0:n, 272:288], CST[s][0:n, 32:48], ALU.mult, [bpm, b_CST[s]], [b_ost[s]])
                vtt(o[0:n, 304:320], pm[0:n, 256:272], CST[s][0:n, 48:64], ALU.mult, [bpm, b_CST[s]], [b_ost[s]])
                vtt(o[0:n, 256:288], o[0:n, 256:288], o[0:n, 288:320], ALU.add, [b_ost[s]], [b_ost[s]])
                dma("sp", out_rows(out_ckv, c0, n), o[0:n, 0:KVL], [b_ost[s]], [])
                dma("sp", out_rows(out_kr, c0, n), o[0:n, 256:288], [b_ost[s]], [])
                vcopy(xnb[s][0:n, 0:288], o[0:n, 0:288], [b_ost[s]], [b_xnb[s]])

            def kv_tail(ti):
                c0, n = NT[ti]
                s = ti % 2
                ki = KI0 + ti
                for kc in range(2):
                    tr(ps_t[:, kc * 128:kc * 128 + n], xnb[s][0:n, kc * 128:(kc + 1) * 128], ident_b[0:n, 0:n],
                       [b_xnb[s], b_const], [b_ps_t])
                tr(ps_t[0:32, 256:256 + n], xnb[s][0:n, 256:288], ident_b[0:n, 0:n], [b_xnb[s], b_const], [b_ps_t])
                vcopy(CKVT[:, :, c0:c0 + n], ps_t[:, 0:256].rearrange("p (g t) -> p g t", t=128)[:, :, 0:n],
                      [b_ps_t], [b_CKVT[ki]])
                vcopy(KRT[64:96, c0:c0 + n], ps_t[0:32, 256:256 + n], [b_ps_t], [b_KRT[ki]])

            pipelined(range(len(NT)), kv_head, kv_tail)
            if isP:
                KB = list(QB)
            else:
                KB = [(0, 512), (512, 512), (PAST, DSEQ)]
            mmset["banks"] = mm_all
            for g in range(4):
                for sl in range(2):
                    h = 2 * g + sl
                    for (k0, kw) in KB:
                        kts = tiles_in(KT, k0, kw)
                        pm, bpm = next_mm()
                        for kc in range(2):
                            mm(pm[0:64, 0:kw], wukvk_t[:, kc, h * 64:(h + 1) * 64], CKVT[:, kc, k0:k0 + kw], kc == 0, kc == 1,
                               [b_CKVT[i] for i in kts] + [b_const], [bpm])
                        evac(KP[0:64, sl, k0:k0 + kw], pm[0:64, 0:kw], [bpm], [b_KP[i] for i in kts])
                        vcopy(KP[64:96, sl, k0:k0 + kw], KRT[64:96, k0:k0 + kw], [b_KRT[i] for i in kts],
                              [b_KP[i] for i in kts])
                for i, (k0, nk) in enumerate(KT):
                    pm, bpm = next_mm()
                    for kc in range(2):
                        mm(pm[0:nk, 0:128], CKVT[:, kc, k0:k0 + nk], wukvv_t[:, kc, g * 128:(g + 1) * 128], kc == 0, kc == 1,
                           [b_CKVT[i], b_const], [bpm])
                    evac(VM[0:nk, i, :, 0:64], pm[0:nk, 0:128].rearrange("p (s x) -> p s x", x=64), [bpm], [b_VM[i]])
                for j, (qc0, nq) in enumerate(QB):
                    dma("sp", CSF[64:96, 0, 0:nq], c_cs_fm[0, :, qc0:qc0 + nq], [], [b_CSF])
                    dma("sp", CSF[64:96, 1, 0:nq], c_cs_fm[1, :, qc0:qc0 + nq], [], [b_CSF])
                    for sl in range(2):
                        h = 2 * g + sl
                        pm1, bpm1 = next_mm()
                        for kc in range(3):
                            mm(pm1[0:96, 0:nq], wuq_t[:, kc, h * 96:(h + 1) * 96], CQT[:, kc, qc0:qc0 + nq], kc == 0, kc == 2,
                               [b_CQT[j], b_const], [bpm1])
                        pm2, bpm2 = next_mm()
                        for kc in range(3):
                            mm(pm2[0:96, 0:nq], wuqr_t[:, kc, h * 96:(h + 1) * 96], CQT[:, kc, qc0:qc0 + nq], kc == 0, kc == 2,
                               [b_CQT[j], b_const], [bpm2])
                        act(QP[0:64, sl, qc0:qc0 + nq], pm1[0:64, 0:nq], AF.Copy, [bpm1], [b_QP[j]])
                        vtt(RT[64:96, 0, 0:nq], pm1[64:96, 0:nq], CSF[64:96, 0, 0:nq], ALU.mult, [bpm1, b_CSF], [b_RT])
                        vtt(RT[64:96, 1, 0:nq], pm2[64:96, 0:nq], CSF[64:96, 1, 0:nq], ALU.mult, [bpm2, b_CSF], [b_RT])
                        vtt(QP[64:96, sl, qc0:qc0 + nq], RT[64:96, 0, 0:nq], RT[64:96, 1, 0:nq], ALU.add, [b_RT], [b_QP[j]])
                attention(
                    "mla", 96,
                    lambda i, h: KP[0:96, h % 2, KT[i][0]:KT[i][0] + KT[i][1]],
                    lambda i, h: b_KP[i],
                    lambda j, h, c0: QP[0:96, h % 2, QB[j][0] + c0:QB[j][0] + QB[j][1]],
                    lambda j, h: b_QP[j],
                    lambda i, h: VM[0:KT[i][1], i, h % 2, :],
                    lambda i, h: b_VM[i],
                    OTM, bOTM, MLA_SCALE, [2 * g, 2 * g + 1])
            P.fence()
        if "C" in phases:
            mmset["banks"] = mm_wide7
            AC = Arena(nc, PH_BASE, SBUF_TOP, "c%s%d" % (seq.kind, b))
            WOF = AC.alloc([128, 4, D], BF16, "wof")
            WOM = AC.alloc([128, 4, D], BF16, "wom")
            WG = AC.alloc([128, 8, 2 * D], BF16, "wg")
            WO = AC.alloc([128, 8, D], BF16, "wo")
            b_WC = Buf("wc")
            MT = AC.alloc([128, 8, 512], BF16, "mt")
            b_MT = Buf("mt")
            G0 = [AC.alloc([128, 512], F32, "g0%d" % i) for i in range(2)]
            G1 = [AC.alloc([128, 512], F32, "g1%d" % i) for i in range(2)]
            M0 = [AC.alloc([128, 512], F32, "m0%d" % i) for i in range(2)]
            b_G0 = [Buf("g0%d" % i) for i in range(2)]
            b_G1 = [Buf("g1%d" % i) for i in range(2)]
            b_M0 = [Buf("m0%d" % i) for i in range(2)]
            cxin = [AC.alloc([128, D], F32, "cxin%d" % i) for i in range(2)]
            b_cxin = [Buf("cxin%d" % i) for i in range(2)]
            cxnb = [AC.alloc([128, D], BF16, "cxnb%d" % i) for i in range(2)]
            b_cxnb = [Buf("cxnb%d" % i) for i in range(2)]
            cjunk = AC.alloc([128, D], BF16, "cjunk")
            b_cjunk = Buf("cjunk")
            cstat = [AC.alloc([128, 4], F32, "cstat%d" % i) for i in range(2)]
            b_cstat = [Buf("cstat%d" % i) for i in range(2)]
            b_WOF = Buf("wof")
            b_WOM = Buf("wom")
            b_WG = [Buf("wg%d" % i) for i in range(4)]
            b_WO = [Buf("wo%d" % i) for i in range(2)]
            w_in_v = w_in.rearrange("(kc p) f -> p kc f", p=128)

            def ld_wg(hh):
                dma("pool", WG[:, :, hh * 512:(hh + 1) * 512],
                    w_in_v[:, :, OFF_GATE + hh * 512:OFF_GATE + (hh + 1) * 512], [], [b_WG[hh]])

            dma("pool", WOF[:], w_ofox.rearrange("(kc p) f -> p kc f", p=128), [], [b_WOF])
            ld_wg(0)
            dma("pool", WOM[:], w_omla.rearrange("(kc p) f -> p kc f", p=128), [], [b_WOM])
            ld_wg(2)
            ld_wg(1)
            ld_wg(3)
            for hh in range(2):
                dma("pool", WO[:, :, hh * 512:(hh + 1) * 512],
                    w_out.rearrange("(kc p) f -> p kc f", p=128)[:, :, hh * 512:(hh + 1) * 512], [], [b_WO[hh]])
            tcount = 0
            for j, (qc0, nq) in enumerate(QB):
                xb = xt_bufs(qc0, nq)
                for m in range(8):
                    s = m % 2
                    pa, bpa = next_mm()
                    for kc in range(4):
                        mm(pa[:, 0:nq], WOF[:, kc, m * 128:(m + 1) * 128], OTF[:, kc, qc0:qc0 + nq], kc == 0, kc == 3,
                           [b_WOF, bOTF(qc0)], [bpa])
                    pg, bpg = next_mm()
                    for kc in range(8):
                        mm(pg[:, 0:nq], WG[:, kc, m * 128:(m + 1) * 128], XT[:, kc, qc0:qc0 + nq], kc == 0, kc == 7,
                           [b_WG[m // 4]] + xb, [bpg])
                    act(G0[s][:, 0:nq], pg[:, 0:nq], AF.Sigmoid, [bpg], [b_G0[s]])
                    vtt(M0[s][:, 0:nq], G0[s][:, 0:nq], pa[:, 0:nq], ALU.mult, [b_G0[s], bpa], [b_M0[s]])
                    pb, bpb = next_mm()
                    for kc in range(4):
                        mm(pb[:, 0:nq], WOM[:, kc, m * 128:(m + 1) * 128], OTM[:, kc, qc0:qc0 + nq], kc == 0, kc == 3,
                           [b_WOM, bOTM(qc0)], [bpb])
                    pg2, bpg2 = next_mm()
                    for kc in range(8):
                        mm(pg2[:, 0:nq], WG[:, kc, D + m * 128:D + (m + 1) * 128], XT[:, kc, qc0:qc0 + nq], kc == 0, kc == 7,
                           [b_WG[2 + m // 4]] + xb, [bpg2])
                    act(G1[s][:, 0:nq], pg2[:, 0:nq], AF.Sigmoid, [bpg2], [b_G1[s]])
                    vtt(G1[s][:, 0:nq], G1[s][:, 0:nq], pb[:, 0:nq], ALU.mult, [b_G1[s], bpb], [b_G1[s]])
                    vtt(MT[:, m, 0:nq], M0[s][:, 0:nq], G1[s][:, 0:nq], ALU.add, [b_M0[s], b_G1[s]], [b_MT])
                def c_head(arg):
                    ti, s = arg
                    c0, n = NT[ti]
                    o = c0 - qc0
                    dma("sp", cxin[s][0:n, :], src_rows(c0, n), [], [b_cxin[s]])
                    pms = []
                    for hw in range(2):
                        pm, bpm = next_mm()
                        for kc in range(8):
                            mm(pm[0:n, :], MT[:, kc, o:o + n], WO[:, kc, hw * 512:(hw + 1) * 512], kc == 0, kc == 7,
                               [b_MT, b_WO[hw]], [bpm])
                        pms.append((pm, bpm))
                    yield
                    for hw in range(2):
                        pm, bpm = pms[hw]
                        vtt(cxin[s][0:n, hw * 512:(hw + 1) * 512], cxin[s][0:n, hw * 512:(hw + 1) * 512], pm[0:n, :], ALU.add,
                            [b_cxin[s], bpm], [b_cxin[s]])
                    dma("sp", h2_scr[c0 - nc0:c0 - nc0 + n, :], cxin[s][0:n, :], [b_cxin[s]], [b_h2scr])
                    vmemset(cstat[s][0:n, 0:1], 0.0, [b_cstat[s]])
                    act(cjunk[0:n, :], cxin[s][0:n, :], AF.Square, [b_cxin[s]], [b_cjunk, b_cstat[s]],
                        accum_out=cstat[s][0:n, 0:1])
                    act(cstat[s][0:n, 1:2], cstat[s][0:n, 0:1], AF.Sqrt, [b_cstat[s], b_const], [b_cstat[s]],
                        bias=cst_t[0:n, 0:1], scale=1.0 / D)
                    vrecip(cstat[s][0:n, 2:3], cstat[s][0:n, 1:2], [b_cstat[s]], [b_cstat[s]])
                    vstt(cxnb[s][0:n, :], cxin[s][0:n, :], cstat[s][0:n, 2:3], gffn_t[0:n, :], ALU.mult, ALU.mult,
                         [b_cxin[s], b_cstat[s], b_const], [b_cxnb[s]])

                def c_tail(arg):
                    ti, s = arg
                    c0, n = NT[ti]
                    for kc in range(8):
                        tr(ps_t[:, kc * 128:kc * 128 + n], cxnb[s][0:n, kc * 128:(kc + 1) * 128], ident_b[0:n, 0:n],
                           [b_cxnb[s], b_const], [b_ps_t])
                    vcopy(XT[:, :, c0:c0 + n], ps_t[:].rearrange("p (kc t) -> p kc t", t=128)[:, :, 0:n], [b_ps_t], [bXT(c0)])

                targs = []
                for ti in tiles_in(NT, qc0, nq):
                    targs.append((ti, tcount % 2))
                    tcount += 1
                pipelined(targs, c_head, c_tail)
            P.fence()
        if "D" in phases:
            mmset["banks"] = mm_wide5
            AD = Arena(nc, OT_BASE, SBUF_TOP, "d%s%d" % (seq.kind, b))
            if isP:
                halves = [[0, 1, 2], [3, 4]]
            else:
                halves = [[0]]
            HW_MAX = max(sum(QB[j][1] for j in blocks) for blocks in halves)
            WD = AD.alloc([128, NGC, D], BF16, "wd")
            b_WD = Buf("wd")
            AT = AD.alloc([128, NGC, HW_MAX], BF16, "at")
            b_AT = Buf("at")
            UFG = [AD.alloc([128, 2 + HW_MAX], BF16, "ufg%d" % i) for i in range(2)]
            UFV = [AD.alloc([128, 2 + HW_MAX], BF16, "ufv%d" % i) for i in range(2)]
            b_UFG = [Buf("ufg%d" % i) for i in range(2)]
            b_UFV = [Buf("ufv%d" % i) for i in range(2)]
            WUP = [AD.alloc([128, 8, 256], BF16, "wup%d" % i) for i in range(3)]
            b_WUP = [Buf("wup%d" % i) for i in range(3)]
            DG = [AD.alloc([128, 6, 128], BF16, "dg%d" % i) for i in range(2)]
            b_DG = [Buf("dg%d" % i) for i in range(2)]
            SG = [AD.alloc([128, 512], F32, "sg%d" % i) for i in range(2)]
            b_SG = [Buf("sg%d" % i) for i in range(2)]
            ULAST = AD.alloc([128, NCH, 2], BF16, "ulast")
            b_UL = Buf("ulast")
            CSO = [AD.alloc([2, 256], F32, "cso%d" % i) for i in range(2)]
            b_CSO = [Buf("cso%d" % i) for i in range(2)]
            dh2 = [AD.alloc([128, D], F32, "dh2%d" % i) for i in range(2)]
            b_dh2 = [Buf("dh2%d" % i) for i in range(2)]
            dy = [AD.alloc([128, D], F32, "dy%d" % i) for i in range(2)]
            b_dy = [Buf("dy%d" % i) for i in range(2)]
            djunk = AD.alloc([128, D], BF16, "djunk")
            b_djunk = Buf("djunk")
            dstat = [AD.alloc([128, 4], F32, "dstat%d" % i) for i in range(2)]
            b_dstat = [Buf("dstat%d" % i) for i in range(2)]
            if isP:
                vmemset(ULAST[:], 0.0, [b_UL])
            else:
                ccv = AD.alloc([2, UPW], BF16, "ccv")
                b_ccv = Buf("ccv")
                dma("pool", ccv[:], c_conv[b], [], [b_ccv])
                for c in range(NCH):
                    tr(ps_t[:, c * 2:c * 2 + 2], ccv[0:2, c * 128:(c + 1) * 128], ident_b[0:2, 0:2], [b_ccv, b_const], [b_ps_t])
                vcopy(ULAST[:].rearrange("p c j -> p (c j)"), ps_t[:, 0:NCH * 2], [b_ps_t], [b_UL])
            wupc = 0
            tcount = 0
            for hi, blocks in enumerate(halves):
                hc0 = QB[blocks[0]][0]
                hw_ = sum(QB[j][1] for j in blocks)
                last_half = hi == len(halves) - 1
                for c in range(NGC):
                    ws = wupc % 3
                    us = wupc % 2
                    wupc += 1
                    wv = w_up.rearrange("(kc p) f -> p kc f", p=128)
                    dma("pool", WUP[ws][:, :, 0:128], wv[:, :, c * 128:(c + 1) * 128], [], [b_WUP[ws]])
                    dma("pool", WUP[ws][:, :, 128:256], wv[:, :, DFF + c * 128:DFF + (c + 1) * 128], [], [b_WUP[ws]])
                    if hi == 0 and c == 2:
                        for hh in range(2):
                            dma("pool", WD[:, :, hh * 512:(hh + 1) * 512],
                                w_down.rearrange("(c p) f -> p c f", p=128)[:, :, hh * 512:(hh + 1) * 512], [], [b_WD])
                    def d_prep(cc, uu):
                        for t in range(3):
                            vts(DG[uu][:, t, :], ident_b[:, :], cwb_t[:, cc, t:t + 1], None, ALU.mult, None, [b_const], [b_DG[uu]])
                            vts(DG[uu][:, 3 + t, :], ident_b[:, :], cwb_t[:, NGC + cc, t:t + 1], None, ALU.mult, None,
                                [b_const], [b_DG[uu]])
                        vcopy(UFG[uu][:, 0:2], ULAST[:, cc, :], [b_UL], [b_UFG[uu]])
                        vcopy(UFV[uu][:, 0:2], ULAST[:, NGC + cc, :], [b_UL], [b_UFV[uu]])

                    if c == 0:
                        d_prep(0, us)
                    for bi, j in enumerate(blocks):
                        if bi == 1 or (bi == 0 and len(blocks) == 1):
                            if c + 1 < NGC:
                                d_prep(c + 1, (us + 1) % 2)
                        qc0, nq = QB[j]
                        o = qc0 - hc0
                        xb = xt_bufs(qc0, nq)
                        pg, bpg = next_mm()
                        for kc in range(8):
                            mm(pg[:, 0:nq], WUP[ws][:, kc, 0:128], XT[:, kc, qc0:qc0 + nq], kc == 0, kc == 7,
                               [b_WUP[ws]] + xb, [bpg])
                        act(UFG[us][:, 2 + o:2 + o + nq], pg[:, 0:nq], AF.Copy, [bpg], [b_UFG[us]])
                        pv, bpv = next_mm()
                        for kc in range(8):
                            mm(pv[:, 0:nq], WUP[ws][:, kc, 128:256], XT[:, kc, qc0:qc0 + nq], kc == 0, kc == 7,
                               [b_WUP[ws]] + xb, [bpv])
                        vcopy(UFV[us][:, 2 + o:2 + o + nq], pv[:, 0:nq], [bpv], [b_UFV[us]])
                        pcg, bpcg = next_s()
                        for t in range(3):
                            mm(pcg[:, 0:nq], DG[us][:, t, :], UFG[us][:, o + t:o + t + nq], t == 0, t == 2,
                               [b_DG[us], b_UFG[us]], [bpcg])
                        pcv, bpcv = next_s()
                        for t in range(3):
                            mm(pcv[:, 0:nq], DG[us][:, 3 + t, :], UFV[us][:, o + t:o + t + nq], t == 0, t == 2,
                               [b_DG[us], b_UFV[us]], [bpcv])
                        act(SG[us][:, 0:nq], pcg[:, 0:nq], AF.Silu, [bpcg, b_const], [b_SG[us]], bias=cwb_t[:, c, 3:4])
                        vstt(AT[:, c, o:o + nq], pcv[:, 0:nq], cwb_t[:, NGC + c, 3:4], SG[us][:, 0:nq], ALU.add, ALU.mult,
                             [bpcv, b_const, b_SG[us]], [b_AT])
                    if not last_half:
                        vcopy(ULAST[:, c, :], UFG[us][:, hw_:hw_ + 2], [b_UFG[us]], [b_UL])
                        vcopy(ULAST[:, NGC + c, :], UFV[us][:, hw_:hw_ + 2], [b_UFV[us]], [b_UL])
                    else:
                        lc = seq.lk - 2
                        pm, bpm = next_mm()
                        for kc in range(8):
                            mm(pm[0:2, 0:256], XT[:, kc, lc:lc + 2], WUP[ws][:, kc, :], kc == 0, kc == 7,
                               [b_WUP[ws]] + xt_bufs(lc, 2), [bpm])
                        vcopy(CSO[us][0:2, 0:256], pm[0:2, 0:256], [bpm], [b_CSO[us]])
                        dma("sp", out_cv[:, c * 128:(c + 1) * 128], CSO[us][0:2, 0:128], [b_CSO[us]], [])
                        dma("sp", out_cv[:, DFF + c * 128:DFF + (c + 1) * 128], CSO[us][0:2, 128:256], [b_CSO[us]], [])
                for ti in tiles_in(NT, hc0, hw_):
                    c0, n = NT[ti]
                    s = tcount % 2
                    tcount += 1
                    o = c0 - hc0
                    dma("sp", dh2[s][0:n, :], h2_scr[c0 - nc0:c0 - nc0 + n, :], [b_h2scr], [b_dh2[s]])
                    for hw in range(2):
                        pm, bpm = next_mm()
                        for c in range(NGC):
                            mm(pm[0:n, :], AT[:, c, o:o + n], WD[:, c, hw * 512:(hw + 1) * 512], c == 0, c == NGC - 1,
                               [b_AT, b_WD], [bpm])
                        vtt(dh2[s][0:n, hw * 512:(hw + 1) * 512], dh2[s][0:n, hw * 512:(hw + 1) * 512], pm[0:n, :], ALU.add,
                            [b_dh2[s], bpm], [b_dh2[s]])
                    vmemset(dstat[s][0:n, 0:1], 0.0, [b_dstat[s]])
                    act(djunk[0:n, :], dh2[s][0:n, :], AF.Square, [b_dh2[s]], [b_djunk, b_dstat[s]],
                        accum_out=dstat[s][0:n, 0:1])
                    act(dstat[s][0:n, 1:2], dstat[s][0:n, 0:1], AF.Sqrt, [b_dstat[s], b_const], [b_dstat[s]],
                        bias=cst_t[0:n, 0:1], scale=1.0 / D)
                    vrecip(dstat[s][0:n, 2:3], dstat[s][0:n, 1:2], [b_dstat[s]], [b_dstat[s]])
                    vstt(dy[s][0:n, :], dh2[s][0:n, :], dstat[s][0:n, 2:3], gfin_t[0:n, :], ALU.mult, ALU.mult,
                         [b_dh2[s], b_dstat[s], b_const], [b_dy[s]])
                    if isP:
                        if c0 >= NMETA:
                            dma("sp", out_y[c0 - NMETA:c0 - NMETA + n, :], dy[s][0:n, :], [b_dy[s]], [])
                    else:
                        dma("sp", out_y[c0 - nc0:c0 - nc0 + n, :], dy[s][0:n, :], [b_dy[s]], [])
            P.fence()

    P.fence()
    P.emit(nc)
    return nc


_CFG = {"n_prompt": PB, "n_sample": SB, "phases": "ABMCD"}
_NC_CACHE = {}


def _constants():
    c = {}
    c["c_ident"] = np.eye(128, dtype=np.float32)
    p = np.arange(128)
    c["c_utri"] = (p[:, None] <= p[None, :]).astype(np.float32)
    c["c_ones"] = np.ones((128, 128), np.float32)
    c["c_mtri"] = (p[:, None] <= p[None, :]).astype(np.float32)
    c["c_mchk"] = ((p[:, None] // 64) <= (p[None, :] // 64)).astype(np.float32)
    inv = (10000.0 ** (-np.arange(0, ROPE, 2, dtype=np.float32) / np.float32(ROPE))).astype(np.float32)
    pos = np.arange(L, dtype=np.float32)
    ang = (pos[:, None] * inv[None, :]).astype(np.float32).astype(np.float64)
    cos = np.cos(ang).astype(np.float32)
    sin = np.sin(ang).astype(np.float32)
    c["c_cs_tm"] = np.ascontiguousarray(np.concatenate([cos, cos, -sin, sin], axis=1))
    c["c_cs_fm"] = np.ascontiguousarray(np.stack([np.concatenate([cos, cos], axis=1).T,
                                                  np.concatenate([sin, sin], axis=1).T], axis=0))
    return c


def kernel(x_prompt, x_sample, cache_fox_k, cache_fox_v, cache_fox_logf, cache_mla_ckv, cache_mla_krope,
           state_ffn_conv, meta_tokens, norm_mix_g, w_in, b_forget, mla_q_norm_g, w_uq, mla_kv_norm_g, w_ukv,
           w_o_fox, w_o_mla, w_out, norm_ffn_g, w_up, conv_w, conv_b, w_down, norm_final_g):
    f = lambda a: np.ascontiguousarray(np.asarray(a, dtype=np.float32))
    key = (_CFG["n_prompt"], _CFG["n_sample"], _CFG["phases"])
    if key not in _NC_CACHE:
        _NC_CACHE[key] = build_program(*key)
    nc = _NC_CACHE[key]
    rep = lambda v: np.ascontiguousarray(np.broadcast_to(f(v).reshape(1, -1), (128, f(v).size)))
    shared = {
        "meta_tokens": f(meta_tokens), "w_in": f(w_in)[0], "w_uq": f(w_uq)[0], "w_ukv": f(w_ukv)[0],
        "w_o_fox": f(w_o_fox)[0], "w_o_mla": f(w_o_mla)[0], "w_out": f(w_out)[0], "w_up": f(w_up)[0],
        "w_down": f(w_down)[0],
        "convwb": np.ascontiguousarray(np.concatenate([f(conv_w)[0], f(conv_b)[0][None, :]], axis=0)
                                       .reshape(4, NCH, 128).transpose(2, 1, 0)),
        "g_mix_bc": rep(norm_mix_g), "g_ffn_bc": rep(norm_ffn_g), "g_fin_bc": rep(norm_final_g),
        "g_q_bc": rep(mla_q_norm_g), "g_kv_bc": rep(mla_kv_norm_g), "b_forget_bc": rep(b_forget),
    }
    shared.update(_constants())
    xp = f(x_prompt)
    xs = f(x_sample)
    in_maps = []
    for c in range(N_CORES):
        m = dict(shared)
        m["x_prompt"] = xp[c * PB:(c + 1) * PB]
        m["x_sample"] = xs[c * SB:(c + 1) * SB]
        m["cache_fox_k"] = f(cache_fox_k)[0, c * SB:(c + 1) * SB].reshape(SB, PAST, FOXW)
        m["cache_fox_v"] = f(cache_fox_v)[0, c * SB:(c + 1) * SB].reshape(SB, PAST, FOXW)
        m["cache_fox_logf"] = f(cache_fox_logf)[0, c * SB:(c + 1) * SB]
        m["cache_mla_ckv"] = f(cache_mla_ckv)[0, c * SB:(c + 1) * SB]
        m["cache_mla_krope"] = f(cache_mla_krope)[0, c * SB:(c + 1) * SB]
        m["state_ffn_conv"] = f(state_ffn_conv)[0, c * SB:(c + 1) * SB]
        in_maps.append({k: np.ascontiguousarray(v) for k, v in m.items()})
    res = run_bass_kernel_spmd(nc, in_maps, core_ids=list(range(N_CORES)))
    R = res.results
    cat = lambda name: np.concatenate([np.asarray(r[name], dtype=np.float32) for r in R], axis=0)
    B = N_CORES * PB
    S = N_CORES * SB
    return (
        cat("y_prompt"), cat("y_sample"),
        cat("new_fox_k_p").reshape(1, B, L, H, HD), cat("new_fox_v_p").reshape(1, B, L, H, HD),
        cat("new_fox_logf_p").reshape(1, B, L, H), cat("new_mla_ckv_p").reshape(1, B, L, KVL),
        cat("new_mla_krope_p").reshape(1, B, L, ROPE), cat("new_ffn_conv_p").reshape(1, B, 2, UPW),
        cat("new_fox_k_s").reshape(1, S, DSEQ, H, HD), cat("new_fox_v_s").reshape(1, S, DSEQ, H, HD),
        cat("new_fox_logf_s").reshape(1, S, DSEQ, H), cat("new_mla_ckv_s").reshape(1, S, DSEQ, KVL),
        cat("new_mla_krope_s").reshape(1, S, DSEQ, ROPE), cat("new_ffn_conv_s").reshape(1, S, 2, UPW),
    )
```

```python
import math
import os
import numpy as np
import concourse.bass as bass
import concourse.mybir as mybir
from concourse.bass_utils import run_bass_kernel_spmd

F32 = mybir.dt.float32
BF16 = mybir.dt.bfloat16
AF = mybir.ActivationFunctionType
ALU = mybir.AluOpType

N_CORES = 8
D = 1024
SEQ = 2048
NMETA = 16
L = NMETA + SEQ
PB = 4
SB = 2
DSEQ = 16
PAST = 1024
H = 8
HD = 64
FOXW = 512
QL = 384
KVL = 256
ROPE = 32
NOPE = 64
MQK = 96
DFF = 2816
UPW = 2 * DFF
NCH = UPW // 128
NGC = DFF // 128
OFF_FK = 512
OFF_FV = 1024
OFF_FF = 1536
OFF_CQ = 1544
OFF_CKV = 1928
OFF_KR = 2184
OFF_GATE = 2216
INW = OFF_GATE + 2 * D
EPS = 1e-6
FOX_SCALE = 1.0 / math.sqrt(HD)
MLA_SCALE = 1.0 / math.sqrt(MQK)
LK = L

SBUF_BASE = 16640
SBUF_TOP = 229344


class Buf:
    __slots__ = ("name", "last_w", "rd_eng", "rd_dma", "excl")

    def __init__(self, name, excl=False):
        self.name = name
        self.excl = excl
        self.last_w = None
        self.rd_eng = {}
        self.rd_dma = []


class Op:
    __slots__ = ("eng", "fn", "dma", "deps", "signal", "sem", "semval", "prev")

    def __init__(self, eng, fn, dma):
        self.eng = eng
        self.fn = fn
        self.dma = dma
        self.deps = []
        self.signal = False
        self.sem = None
        self.semval = 0
        self.prev = 0


ENGS = ("pe", "act", "dve", "pool", "sp")
NDMASEM = {"sp": 20, "pool": 12, "act": 6}


class Prog:
    def __init__(self):
        self.ops = {e: [] for e in ENGS}
        self.last_op = {e: None for e in ENGS}
        self.last_real = {}
        self.dma_since_fence = {e: [] for e in ENGS}
        self.nops = 0

    def op(self, eng, fn, reads=(), writes=(), dma=False, extra=(), real=True):
        o = Op(eng, fn, dma)
        deps = {}
        xr = [b for b in reads if b.excl]
        if xr:
            reads = [b for b in reads if not b.excl]
            writes = list(writes) + [b for b in xr if b not in writes]
        for b in reads:
            w = b.last_w
            if w is not None:
                deps[id(w)] = (w, True)
        for b in writes:
            w = b.last_w
            if w is not None and id(w) not in deps:
                deps[id(w)] = (w, False)
            for r in b.rd_eng.values():
                if id(r) not in deps:
                    deps[id(r)] = (r, False)
            for r in b.rd_dma:
                if id(r) not in deps:
                    deps[id(r)] = (r, False)
        for d in extra:
            deps[id(d)] = (d, True)
        for d, raw in deps.values():
            if d is o:
                continue
            if d.dma or o.dma or d.eng != o.eng:
                need = True
            elif o.eng == "pe":
                need = False
            else:
                need = True
            if need:
                o.deps.append(d)
                d.signal = True
        for b in reads:
            if dma:
                b.rd_dma.append(o)
            else:
                b.rd_eng[eng] = o
        for b in writes:
            b.last_w = o
            b.rd_eng = {}
            b.rd_dma = []
        self.ops[eng].append(o)
        self.last_op[eng] = o
        if real and not dma:
            self.last_real[eng] = o
        if dma:
            self.dma_since_fence[eng].append(o)
        self.nops += 1
        return o

    def fence(self):
        dmas = []
        for e in ENGS:
            dmas.extend(self.dma_since_fence[e])
            self.dma_since_fence[e] = []
        lasts = [self.last_real[e] for e in ENGS if self.last_real.get(e) is not None]
        for e in ENGS:
            extra = [d for d in dmas] + [l for l in lasts if l.eng != e]
            self.op(e, lambda en: en.nop(nofuse=True), extra=extra, real=False)

    def emit(self, nc):
        sems = {}
        for e in ENGS:
            sems[e] = nc.alloc_semaphore("s_" + e)
        dsem = {e: [nc.alloc_semaphore("d_%s%d" % (e, i)) for i in range(n)] for e, n in NDMASEM.items()}
        for e in ENGS:
            cnt = 0
            di = 0
            dvals = [0] * NDMASEM.get(e, 1)
            for o in self.ops[e]:
                if o.dma:
                    k = di % NDMASEM[e]
                    di += 1
                    o.sem = dsem[e][k]
                    o.prev = dvals[k]
                    dvals[k] += 16
                    o.semval = dvals[k]
                elif o.signal:
                    cnt += 1
                    o.sem = sems[e]
                    o.semval = cnt
        engobj = {"pe": "tensor", "act": "scalar", "dve": "vector", "pool": "gpsimd", "sp": "sync"}
        ops = self.ops

        def make(e):
            def body(en):
                waited = {}
                for o in ops[e]:
                    for d in o.deps:
                        k = id(d.sem)
                        if waited.get(k, 0) < d.semval:
                            en.wait_ge(d.sem, d.semval)
                            waited[k] = d.semval
                    if o.dma and o.prev > 0:
                        k = id(o.sem)
                        if waited.get(k, 0) < o.prev:
                            en.wait_ge(o.sem, o.prev)
                            waited[k] = o.prev
                    inst = o.fn(en)
                    if o.dma:
                        inst.then_inc(o.sem, 16)
                    elif o.signal:
                        inst.then_inc(o.sem, 1)
            return body

        with nc.Block() as block:
            for e in ENGS:
                if ops[e]:
                    getattr(block, engobj[e])(make(e))


class Arena:
    def __init__(self, nc, base, top, tag):
        self.nc = nc
        self.base = base
        self.cur = base
        self.top = top
        self.tag = tag
        self.n = 0

    def alloc(self, shape, dtype, name):
        per = 1
        for s in shape[1:]:
            per *= s
        nbytes = per * (4 if dtype == F32 else 2)
        off = (self.cur + 31) // 32 * 32
        if off + nbytes > self.top:
            raise RuntimeError("SBUF arena %s overflow allocating %s %s (%d > %d)" % (self.tag, name, shape, off + nbytes, self.top))
        self.cur = off + nbytes
        self.n += 1
        return self.nc.alloc_sbuf_tensor_at("%s_%s_%d" % (self.tag, name, self.n), list(shape), dtype, offset=off)


class Seq:
    pass


def make_prompt_seq(b):
    s = Seq()
    s.kind = "p"
    s.b = b
    s.ncache = 0
    s.nt = [(0, NMETA)] + [(NMETA + 128 * i, 128) for i in range(16)]
    s.kt = list(s.nt)
    s.qb = [(0, NMETA)] + [(NMETA + 512 * j, 512) for j in range(4)]
    s.lk = L
    return s


def make_sample_seq(b):
    s = Seq()
    s.kind = "s"
    s.b = b
    s.ncache = PAST
    s.nt = [(PAST, DSEQ)]
    s.kt = [(128 * i, 128) for i in range(8)] + [(PAST, DSEQ)]
    s.qb = [(PAST, DSEQ)]
    s.lk = PAST + DSEQ
    return s


def vis(seq, j, i, kind):
    if seq.kind == "s":
        if i < 8:
            return ("full", 0, None)
        return ("diag", 0, "tri") if kind == "fox" else ("full", 0, None)
    if j == 0:
        if i != 0:
            return None
        return ("diag", 0, "tri") if kind == "fox" else ("full", 0, None)
    if i == 0:
        return ("full", 0, None)
    first = 4 * (j - 1) + 1
    if i < first:
        return ("full", 0, None)
    d = i - first
    if d > 3:
        return None
    return ("diag", 128 * d, "tri" if kind == "fox" else "chunk")


def build_program(n_prompt=PB, n_sample=SB, phases="ABCD"):
    nc = bass.Bass("TRN2", target_bir_lowering=False)
    P = Prog()

    def din(name, shape):
        return nc.dram_tensor(name, list(shape), F32, kind="ExternalInput").ap()

    def dout(name, shape):
        return nc.dram_tensor(name, list(shape), F32, kind="ExternalOutput").ap()

    x_p = din("x_prompt", [PB, SEQ, D])
    x_s = din("x_sample", [SB, DSEQ, D])
    c_fk = din("cache_fox_k", [SB, PAST, FOXW])
    c_fv = din("cache_fox_v", [SB, PAST, FOXW])
    c_lf = din("cache_fox_logf", [SB, PAST, H])
    c_ckv = din("cache_mla_ckv", [SB, PAST, KVL])
    c_kr = din("cache_mla_krope", [SB, PAST, ROPE])
    c_conv = din("state_ffn_conv", [SB, 2, UPW])
    meta = din("meta_tokens", [NMETA, D])
    w_in = din("w_in", [D, INW])
    w_uq = din("w_uq", [QL, H * MQK])
    w_ukv = din("w_ukv", [KVL, H * 128])
    w_ofox = din("w_o_fox", [FOXW, D])
    w_omla = din("w_o_mla", [FOXW, D])
    w_out = din("w_out", [D, D])
    w_up = din("w_up", [D, UPW])
    w_down = din("w_down", [DFF, D])
    convwb = din("convwb", [128, NCH, 4])
    g_mix = din("g_mix_bc", [128, D])
    g_ffn = din("g_ffn_bc", [128, D])
    g_fin = din("g_fin_bc", [128, D])
    g_q = din("g_q_bc", [128, QL])
    g_kv = din("g_kv_bc", [128, KVL])
    b_fg = din("b_forget_bc", [128, H])
    c_ident = din("c_ident", [128, 128])
    c_utri = din("c_utri", [128, 128])
    c_ones = din("c_ones", [128, 128])
    c_mtri = din("c_mtri", [128, 128])
    c_mchk = din("c_mchk", [128, 128])
    c_cs_tm = din("c_cs_tm", [L, 64])
    c_cs_fm = din("c_cs_fm", [2, ROPE, L])

    o_y_p = dout("y_prompt", [PB, SEQ, D])
    o_y_s = dout("y_sample", [SB, DSEQ, D])
    o_fk_p = dout("new_fox_k_p", [PB, L, FOXW])
    o_fv_p = dout("new_fox_v_p", [PB, L, FOXW])
    o_lf_p = dout("new_fox_logf_p", [PB, L, H])
    o_ckv_p = dout("new_mla_ckv_p", [PB, L, KVL])
    o_kr_p = dout("new_mla_krope_p", [PB, L, ROPE])
    o_cv_p = dout("new_ffn_conv_p", [PB, 2, UPW])
    o_fk_s = dout("new_fox_k_s", [SB, DSEQ, FOXW])
    o_fv_s = dout("new_fox_v_s", [SB, DSEQ, FOXW])
    o_lf_s = dout("new_fox_logf_s", [SB, DSEQ, H])
    o_ckv_s = dout("new_mla_ckv_s", [SB, DSEQ, KVL])
    o_kr_s = dout("new_mla_krope_s", [SB, DSEQ, ROPE])
    o_cv_s = dout("new_ffn_conv_s", [SB, 2, UPW])
    h2_scr = nc.dram_tensor("h2_scratch", [L, D], F32, kind="Internal").ap()

    ps_mm = [nc.alloc_psum_tensor("ps_mm%d" % i, [128, 512], F32) for i in range(3)]
    ps_s = [nc.alloc_psum_tensor("ps_s%d" % i, [128, 512], F32) for i in range(2)]
    ps_acc = [nc.alloc_psum_tensor("ps_acc%d" % i, [128, 512], F32) for i in range(2)]
    ps_t = nc.alloc_psum_tensor("ps_t", [128, 1024], BF16)
    b_ps_mm = [Buf("ps_mm%d" % i, True) for i in range(3)]
    b_ps_s = [Buf("ps_s%d" % i, True) for i in range(2)]
    b_ps_acc = [Buf("ps_acc%d" % i, True) for i in range(2)]
    b_ps_t = Buf("ps_t", True)
    rr = {"mm": 0, "s": 0, "acc": 0}

    mm_all = [(ps_mm[i], b_ps_mm[i]) for i in range(3)]
    mm_wide7 = mm_all + [(ps_s[i], b_ps_s[i]) for i in range(2)] + [(ps_acc[i], b_ps_acc[i]) for i in range(2)]
    mm_wide5 = mm_all + [(ps_acc[i], b_ps_acc[i]) for i in range(2)]
    mmset = {"banks": mm_all}

    def next_mm():
        bk = mmset["banks"]
        i = rr["mm"] % len(bk)
        rr["mm"] += 1
        return bk[i]

    def next_s():
        i = rr["s"] % 2
        rr["s"] += 1
        return ps_s[i], b_ps_s[i]

    def next_acc():
        i = rr["acc"] % 2
        rr["acc"] += 1
        return ps_acc[i], b_ps_acc[i]

    A0 = Arena(nc, SBUF_BASE, SBUF_TOP, "pers")
    ident_f = A0.alloc([128, 128], F32, "identf")
    utri_f = A0.alloc([128, 128], F32, "utri")
    ones_f = A0.alloc([128, 128], F32, "ones")
    ident_b = A0.alloc([128, 128], BF16, "identb")
    mtri_b = A0.alloc([128, 128], BF16, "mtri")
    mchk_b = A0.alloc([128, 128], BF16, "mchk")
    gmix_t = A0.alloc([128, D], F32, "gmix")
    gffn_t = A0.alloc([128, D], F32, "gffn")
    gfin_t = A0.alloc([128, D], F32, "gfin")
    gq_t = A0.alloc([128, QL], F32, "gq")
    gkv_t = A0.alloc([128, KVL], F32, "gkv")
    bfg_t = A0.alloc([128, H], F32, "bfg")
    cwb_t = A0.alloc([128, NCH, 4], F32, "cwb")
    wuq_t = A0.alloc([128, 3, H * MQK], BF16, "wuq")
    wuqr_t = A0.alloc([128, 3, H * MQK], BF16, "wuqr")
    wukvk_t = A0.alloc([128, 2, FOXW], BF16, "wukvk")
    wukvv_t = A0.alloc([128, 2, FOXW], BF16, "wukvv")
    cst_t = A0.alloc([128, 4], F32, "cst")
    b_const = Buf("const")
    PERS_END = A0.cur

    A1 = Arena(nc, PERS_END, SBUF_TOP, "mid")
    XT = A1.alloc([128, 8, L], BF16, "XT")
    OTF = A1.alloc([128, 4, L], BF16, "OTF")
    OTM = A1.alloc([128, 4, L], BF16, "OTM")
    PH_BASE = A1.cur
    OT_BASE = PH_BASE - 2 * (4 * L * 2)
    b_h2scr = Buf("h2scr")
    b_XT = {}
    b_OTF = {}
    b_OTM = {}

    def bXT(c0):
        return b_XT.setdefault(c0, Buf("XT%d" % c0))

    def bOTF(c0):
        return b_OTF.setdefault(c0, Buf("OTF%d" % c0))

    def bOTM(c0):
        return b_OTM.setdefault(c0, Buf("OTM%d" % c0))

    def dma(eng, out, in_, reads, writes):
        return P.op(eng, lambda en: en.dma_start(out=out, in_=in_), reads=reads, writes=writes, dma=True)

    def mm(out, lhsT, rhs, start, stop, reads, writes):
        return P.op("pe", lambda en: en.matmul(out, lhsT=lhsT, rhs=rhs, start=start, stop=stop),
                    reads=reads, writes=writes)

    def tr(out, in_, ident, reads, writes):
        return P.op("pe", lambda en: en.transpose(out, in_, ident), reads=reads, writes=writes)

    def act(out, in_, func, reads, writes, bias=None, scale=None, accum_out=None):
        kw = {}
        if bias is not None:
            kw["bias"] = bias
        if scale is not None:
            kw["scale"] = scale
        if accum_out is not None:
            kw["accum_out"] = accum_out
        return P.op("act", lambda en: en.activation(out=out, in_=in_, func=func, **kw), reads=reads, writes=writes)

    def vcopy(out, in_, reads, writes, eng="dve"):
        return P.op(eng, lambda en: en.tensor_copy(out=out, in_=in_), reads=reads, writes=writes)

    def vtt(out, in0, in1, op, reads, writes, eng="dve"):
        return P.op(eng, lambda en: en.tensor_tensor(out=out, in0=in0, in1=in1, op=op), reads=reads, writes=writes)

    def vts(out, in0, s1, s2, op0, op1, reads, writes, eng="dve"):
        if op1 is None:
            return P.op(eng, lambda en: en.tensor_scalar(out=out, in0=in0, scalar1=s1, scalar2=None, op0=op0),
                        reads=reads, writes=writes)
        return P.op(eng, lambda en: en.tensor_scalar(out=out, in0=in0, scalar1=s1, scalar2=s2, op0=op0, op1=op1),
                    reads=reads, writes=writes)

    def vstt(out, in0, scalar, in1, op0, op1, reads, writes):
        return P.op("dve", lambda en: en.scalar_tensor_tensor(out=out, in0=in0, scalar=scalar, in1=in1, op0=op0, op1=op1),
                    reads=reads, writes=writes)

    def vmemset(ap, val, writes, eng="dve"):
        return P.op(eng, lambda en: en.memset(ap, val), writes=writes)

    def vrecip(out, in_, reads, writes):
        return P.op("dve", lambda en: en.reciprocal(out=out, in_=in_), reads=reads, writes=writes)

    AS = Arena(nc, PH_BASE, SBUF_TOP, "setup")
    vmemset(cst_t[:, 0:1], EPS, [b_const])
    vmemset(cst_t[:, 1:2], 1.0, [b_const])
    vmemset(cst_t[:, 2:3], 0.0, [b_const])
    b_stage = Buf("setup_stage")
    for t, src in ((ident_f, c_ident), (utri_f, c_utri), (ones_f, c_ones), (gmix_t, g_mix), (gffn_t, g_ffn),
                   (gfin_t, g_fin), (gq_t, g_q), (gkv_t, g_kv), (bfg_t, b_fg)):
        dma("sp", t[:], src[:, :], [], [b_const])
    STAGE = int(os.environ.get("KSTAGE", "9"))
    for t, src in ((ident_b, c_ident), (mtri_b, c_mtri), (mchk_b, c_mchk)):
        if STAGE >= 2:
            dma("pool", t[:], src[:, :], [], [b_const])
    if STAGE >= 3:
        dma("pool", wuq_t[:], w_uq.rearrange("(kc p) f -> p kc f", p=128), [], [b_const])
    for kc in range(2 if STAGE >= 4 else 0):
        src = w_ukv[kc * 128:(kc + 1) * 128, :].rearrange("p (h x) -> p h x", x=128)
        dma("pool", wukvk_t[:, kc, :].rearrange("p (h x) -> p h x", x=64), src[:, :, 0:64], [], [b_const])
        dma("pool", wukvv_t[:, kc, :].rearrange("p (h x) -> p h x", x=64), src[:, :, 64:128], [], [b_const])
    wq4 = wuq_t[:].rearrange("p kc (h x) -> p kc h x", x=MQK)
    wr4 = wuqr_t[:].rearrange("p kc (h x) -> p kc h x", x=MQK)
    vmemset(wuqr_t[:], 0.0, [b_const])
    for kc in range(3 if STAGE >= 5 else 0):
        P.op("dve", (lambda kc: lambda en: en.tensor_scalar(out=wr4[:, kc, :, 64:80], in0=wq4[:, kc, :, 80:96], scalar1=-1.0,
                                                             scalar2=None, op0=ALU.mult))(kc), reads=[b_const], writes=[b_const])
        vcopy(wr4[:, kc, :, 80:96], wq4[:, kc, :, 64:80], [b_const], [b_const])
    dma("sp", cwb_t[:], convwb[:, :, :], [], [b_const])
    P.fence()

    seqs = [make_prompt_seq(b) for b in range(n_prompt)] + [make_sample_seq(b) for b in range(n_sample)]
    if phases.startswith('S'):
        seqs = []

    wb_state = {"i": 0}

    for seq in seqs:
        isP = seq.kind == "p"
        b = seq.b
        if isP:
            out_fk, out_fv, out_lf, out_ckv, out_kr, out_cv, out_y = (o_fk_p[b], o_fv_p[b], o_lf_p[b], o_ckv_p[b],
                                                                     o_kr_p[b], o_cv_p[b], o_y_p[b])
        else:
            out_fk, out_fv, out_lf, out_ckv, out_kr, out_cv, out_y = (o_fk_s[b], o_fv_s[b], o_lf_s[b], o_ckv_s[b],
                                                                     o_kr_s[b], o_cv_s[b], o_y_s[b])
        nc0 = seq.ncache
        NT = seq.nt
        KT = seq.kt
        QB = seq.qb
        NKT = len(KT)
        NQB = len(QB)

        def src_rows(c0, n):
            if not isP:
                return x_s[b, c0 - nc0:c0 - nc0 + n, :]
            if c0 == 0:
                return meta[0:n, :]
            return x_p[b, c0 - NMETA:c0 - NMETA + n, :]

        def out_rows(ap, c0, n):
            return ap[c0 - nc0:c0 - nc0 + n, :]

        AB = Arena(nc, PH_BASE, SBUF_TOP, "ab%s%d" % (seq.kind, b))
        WB = [AB.alloc([128, 8, 512], BF16, "wb%d" % i) for i in range(3)]
        b_WB = [Buf("wb%d" % i) for i in range(3)]

        def load_w_group(col0, ncols):
            i = wb_state["i"] % 3
            wb_state["i"] += 1
            dma("pool", WB[i][:, :, 0:ncols], w_in.rearrange("(kc p) f -> p kc f", p=128)[:, :, col0:col0 + ncols],
                [], [b_WB[i]])
            return WB[i], b_WB[i]

        XIN_OFF = (AB.cur + 31) // 32 * 32
        xin = [AB.alloc([128, D], F32, "xin%d" % i) for i in range(2)]
        b_xin = [Buf("xin%d" % i) for i in range(2)]
        junk = AB.alloc([128, D], BF16, "junk")
        b_junk = Buf("junk")
        xnb = [AB.alloc([128, D], BF16, "xnb%d" % i) for i in range(2)]
        b_xnb = [Buf("xnb%d" % i) for i in range(2)]
        stat = [AB.alloc([128, 4], F32, "stat%d" % i) for i in range(2)]
        b_stat = [Buf("stat%d" % i) for i in range(2)]
        ost = [AB.alloc([128, 512], F32, "ost%d" % i) for i in range(2)]
        b_ost = [Buf("ost%d" % i) for i in range(2)]
        LOGF = AB.alloc([128, NKT, H], F32, "logf")
        b_LOGF = [Buf("logf%d" % i) for i in range(NKT)]
        FCUM = AB.alloc([128, NKT, H], F32, "fcum")
        b_FCUM = [Buf("fcum%d" % i) for i in range(NKT)]
        CREF = AB.alloc([128, NQB, H], F32, "cref")
        b_CREF = [Buf("cref%d" % j) for j in range(NQB)]
        BIAS = AB.alloc([128, NQB, NKT, H], F32, "bias")
        b_BIAS = [Buf("bias%d" % j) for j in range(NQB)]
        PT = [AB.alloc([128, 512], BF16, "pt%d" % i) for i in range(4)]
        b_PT = [Buf("pt%d" % i) for i in range(4)]
        OSB = [AB.alloc([128, 512], F32, "osb%d" % i) for i in range(2)]
        b_OSB = [Buf("osb%d" % i) for i in range(2)]
        ATT_BASE = AB.cur
        AF_ = Arena(nc, ATT_BASE, SBUF_TOP, "fox%s%d" % (seq.kind, b))
        FQT = AF_.alloc([128, 4, LK], BF16, "fqt")
        FKT = AF_.alloc([128, 4, LK], BF16, "fkt")
        VF = AF_.alloc([128, NKT, H, 65], BF16, "vf")
        b_FQT = [Buf("fqt%d" % j) for j in range(NQB)]
        b_FKT = [Buf("fkt%d" % i) for i in range(NKT)]
        b_VF = [Buf("vf%d" % i) for i in range(NKT)]
        b_VF1 = Buf("vf_ones")
        QPAD = nc.alloc_sbuf_tensor_at("qpad_%s%d" % (seq.kind, b), [128, 2, LK], BF16, offset=XIN_OFF)
        b_QPAD = Buf("qpad")
        VA = [AF_.alloc([128, NKT, 128], BF16, "va%d" % i) for i in range(2)]
        b_VA = [Buf("va%d" % i) for i in range(2)]

        for ti, (c0, n) in enumerate(NT):
            s = ti % 2
            dma("sp", xin[s][0:n, :], src_rows(c0, n), [], [b_xin[s]])
            vmemset(stat[s][0:n, 0:1], 0.0, [b_stat[s]])
            act(junk[0:n, :], xin[s][0:n, :], AF.Square, [b_xin[s]], [b_junk, b_stat[s]], accum_out=stat[s][0:n, 0:1])
            act(stat[s][0:n, 1:2], stat[s][0:n, 0:1], AF.Sqrt, [b_stat[s], b_const], [b_stat[s]],
                bias=cst_t[0:n, 0:1], scale=1.0 / D)
            vrecip(stat[s][0:n, 2:3], stat[s][0:n, 1:2], [b_stat[s]], [b_stat[s]])
            vstt(xnb[s][0:n, :], xin[s][0:n, :], stat[s][0:n, 2:3], gmix_t[0:n, :], ALU.mult, ALU.mult,
                 [b_xin[s], b_stat[s], b_const], [b_xnb[s]])
            for kc in range(8):
                tr(ps_t[:, kc * 128:kc * 128 + n], xnb[s][0:n, kc * 128:(kc + 1) * 128], ident_b[0:n, 0:n],
                   [b_xnb[s], b_const], [b_ps_t])
            vcopy(XT[:, :, c0:c0 + n], ps_t[:].rearrange("p (kc t) -> p kc t", t=128)[:, :, 0:n], [b_ps_t], [bXT(c0)])
        def tiles_in(lst, c0, n):
            return [i for i, (t0, tn) in enumerate(lst) if t0 < c0 + n and c0 < t0 + tn]

        def xt_bufs(c0, n):
            return [bXT(NT[i][0]) for i in tiles_in(NT, c0, n)]

        def pipelined(items, head, tail):
            prev = None
            for it in items:
                head(it)
                if prev is not None:
                    tail(prev)
                prev = it
            if prev is not None:
                tail(prev)

        ev = {"i": 0}

        def evac(out, in_, reads, writes):
            ev["i"] += 1
            if ev["i"] % 2:
                return act(out, in_, AF.Copy, reads, writes)
            return vcopy(out, in_, reads, writes)

        def tm_proj(Wt, bW, c0, n, wcol0, ncols):
            pm, bpm = next_mm()
            for kc in range(8):
                mm(pm[0:n, 0:ncols], XT[:, kc, c0:c0 + n], Wt[:, kc, wcol0:wcol0 + ncols], kc == 0, kc == 7,
                   [bXT(c0), bW], [bpm])
            return pm, bpm

        def fm_proj(Wt, bW, wcol0, m, qc0, nq):
            pm, bpm = next_mm()
            xb = xt_bufs(qc0, nq)
            for kc in range(8):
                mm(pm[0:m, 0:nq], Wt[:, kc, wcol0:wcol0 + m], XT[:, kc, qc0:qc0 + nq], kc == 0, kc == 7,
                   xb + [bW], [bpm])
            return pm, bpm

        KI0 = NKT - len(NT)

        vmemset(VF[:], 0.0, b_VF + [b_VF1])
        if not isP:
            for i in range(8):
                s = i % 2
                dma("pool", xnb[s][:, 0:512], c_fk[b, i * 128:(i + 1) * 128, :], [], [b_xnb[s]])
                for g in range(4):
                    tr(ps_t[:, g * 128:(g + 1) * 128], xnb[s][:, g * 128:(g + 1) * 128], ident_b[:, :],
                       [b_xnb[s], b_const], [b_ps_t])
                vcopy(FKT[:, :, i * 128:(i + 1) * 128], ps_t[:, 0:512].rearrange("p (g t) -> p g t", t=128),
                      [b_ps_t], [b_FKT[i]])
                dma("pool", VF[:, i, :, 0:64], c_fv[b, i * 128:(i + 1) * 128, :].rearrange("p (h x) -> p h x", x=64),
                    [], [b_VF[i]])
            dma("sp", LOGF[:, 0:8, :], c_lf[b].rearrange("(i p) h -> p i h", p=128), [], b_LOGF[0:8])

        mmset["banks"] = mm_wide7
        Wt, bW = load_w_group(OFF_FK, 512)
        Wv, bWv = load_w_group(OFF_FV, 512)
        Wq, bWq = load_w_group(0, 512)
        for ti, (c0, n) in enumerate(NT):
            s = ti % 2
            pm, bpm = tm_proj(Wt, bW, c0, n, 0, 512)
            evac(ost[s][0:n, :], pm[0:n, 0:512], [bpm], [b_ost[s]])
            dma("sp", out_rows(out_fk, c0, n), ost[s][0:n, :], [b_ost[s]], [])
        for j, (qc0, nq) in enumerate(QB):
            kts = tiles_in(KT, qc0, nq)
            for g in range(4):
                pm, bpm = fm_proj(Wt, bW, g * 128, 128, qc0, nq)
                evac(FKT[:, g, qc0:qc0 + nq], pm[:, 0:nq], [bpm], [b_FKT[i] for i in kts])
        for ti, (c0, n) in enumerate(NT):
            s = ti % 2
            ki = KI0 + ti
            pm, bpm = tm_proj(Wv, bWv, c0, n, 0, 512)
            act(ost[s][0:n, :], pm[0:n, 0:512], AF.Copy, [bpm], [b_ost[s]])
            vcopy(VF[0:n, ki, :, 0:64], pm[0:n, 0:512].rearrange("p (h x) -> p h x", x=64), [bpm], [b_VF[ki]])
            dma("sp", out_rows(out_fv, c0, n), ost[s][0:n, :], [b_ost[s]], [])
        for j, (qc0, nq) in enumerate(QB):
            for g in range(4):
                pm, bpm = fm_proj(Wq, bWq, g * 128, 128, qc0, nq)
                evac(FQT[:, g, qc0:qc0 + nq], pm[:, 0:nq], [bpm], [b_FQT[j]])
        Wf, bWf = load_w_group(OFF_FF, 8)
        for ti, (c0, n) in enumerate(NT):
            s = ti % 2
            ki = KI0 + ti
            pm, bpm = tm_proj(Wf, bWf, c0, n, 0, 8)
            z = ost[s]
            vtt(z[0:n, 0:8], pm[0:n, 0:8], bfg_t[0:n, :], ALU.add, [bpm, b_const], [b_ost[s]])
            act(z[0:n, 8:16], z[0:n, 0:8], AF.Exp, [b_ost[s]], [b_ost[s]], scale=-1.0)
            act(z[0:n, 16:24], z[0:n, 8:16], AF.Ln, [b_ost[s], b_const], [b_ost[s]], bias=cst_t[0:n, 1:2])
            vts(LOGF[0:n, ki, :], z[0:n, 16:24], -1.0, None, ALU.mult, None, [b_ost[s]], [b_LOGF[ki]])
            dma("sp", out_rows(out_lf, c0, n), LOGF[0:n, ki, :], [b_LOGF[ki]], [])
        pm, bpm = next_mm()
        vmemset(pm[:, 0:NKT * 8], 0.0, [bpm])
        for i, (kc0, nk) in enumerate(KT):
            for jj in range(i):
                nj = KT[jj][1]
                mm(pm[0:nk, i * 8:(i + 1) * 8], ones_f[0:nj, 0:nk], LOGF[0:nj, jj, :], jj == 0, False,
                   [b_LOGF[jj], b_const], [bpm])
            mm(pm[0:nk, i * 8:(i + 1) * 8], utri_f[0:nk, 0:nk], LOGF[0:nk, i, :], i == 0, True,
               [b_LOGF[i], b_const], [bpm])
        vcopy(FCUM[:].rearrange("p i h -> p (i h)"), pm[:, 0:NKT * 8], [bpm], b_FCUM)
        pm, bpm = next_mm()
        vmemset(pm[:, 0:NQB * 8], 0.0, [bpm])
        for j in range(NQB):
            if isP:
                upto = 0 if j == 0 else 4 * (j - 1) + 3
            else:
                upto = 8
            if upto == 0:
                continue
            for jj in range(upto):
                nj = KT[jj][1]
                mm(pm[:, j * 8:(j + 1) * 8], ones_f[0:nj, :], LOGF[0:nj, jj, :], jj == 0, jj == upto - 1,
                   [b_LOGF[jj], b_const], [bpm])
        vcopy(CREF[:].rearrange("p j h -> p (j h)"), pm[:, 0:NQB * 8], [bpm], b_CREF)
        for j in range(NQB):
            zero_ref = isP and j == 0
            for i, (kc0, nk) in enumerate(KT):
                if vis(seq, j, i, "fox") is None:
                    continue
                if zero_ref:
                    vts(BIAS[0:nk, j, i, :], FCUM[0:nk, i, :], -1.0, None, ALU.mult, None, [b_FCUM[i]], [b_BIAS[j]])
                else:
                    vtt(BIAS[0:nk, j, i, :], CREF[0:nk, j, :], FCUM[0:nk, i, :], ALU.subtract,
                        [b_CREF[j], b_FCUM[i]], [b_BIAS[j]])

        mmset["banks"] = mm_all
        ptc = {"i": 0, "o": 0}

        def attention(kind, kdim, Kap, bK, Qap, bQ, Vap, bV, OT, bOT, scale, heads, pre_head=None):
            groups = []
            for h in heads:
                for j in range(NQB):
                    vl = [(i, vis(seq, j, i, kind)) for i in range(NKT)]
                    vl = [(i, v) for i, v in vl if v is not None]
                    groups.append((h, j, vl))
            pairs = []
            for gi, (h, j, vl) in enumerate(groups):
                for idx, (i, v) in enumerate(vl):
                    pairs.append((gi, h, j, idx, i, v, idx == len(vl) - 1))
            sinfo = {}
            ginfo = {}

            started = set()

            def emit_S(p):
                gi, h, j, idx, i, v, last = pairs[p]
                if pre_head is not None and h not in started:
                    started.add(h)
                    pre_head(h)
                qc0, nq = QB[j]
                kc0, nk = KT[i]
                c0 = v[1]
                w = nq - c0
                ps, bps = next_s()
                mm(ps[0:nk, 0:w], Kap(i, h), Qap(j, h, c0), True, True, [bK(i, h), bQ(j, h)], [bps])
                sinfo[p] = (ps, bps)

            def emit_rest(p):
                gi, h, j, idx, i, v, last = pairs[p]
                qc0, nq = QB[j]
                kc0, nk = KT[i]
                c0 = v[1]
                w = nq - c0
                ps, bps = sinfo.pop(p)
                if idx == 0:
                    ginfo[gi] = next_acc()
                acc, bacc = ginfo[gi]
                k = ptc["i"] % 4
                ptc["i"] += 1
                if kind == "fox":
                    act(PT[k][0:nk, 0:w], ps[0:nk, 0:w], AF.Exp, [bps, b_BIAS[j]], [b_PT[k]],
                        bias=BIAS[0:nk, j, i, h:h + 1], scale=scale)
                else:
                    act(PT[k][0:nk, 0:w], ps[0:nk, 0:w], AF.Exp, [bps, b_const], [b_PT[k]],
                        bias=cst_t[0:nk, 2:3], scale=scale)
                if v[0] == "diag":
                    mw = min(128, w)
                    mt = mtri_b if v[2] == "tri" else mchk_b
                    vtt(PT[k][0:nk, 0:mw], PT[k][0:nk, 0:mw], mt[0:nk, 0:mw], ALU.mult,
                        [b_PT[k], b_const], [b_PT[k]])
                mm(acc[0:128, c0:nq], Vap(i, h), PT[k][0:nk, 0:w], idx == 0, last,
                   [bV(i, h), b_PT[k]], [bacc])

            def emit_norm(gi):
                h, j, vl = groups[gi]
                g, sl = h // 2, h % 2
                qc0, nq = QB[j]
                acc, bacc = ginfo.pop(gi)
                o = ptc["o"] % 2
                ptc["o"] += 1
                vrecip(OSB[o][0:64, 0:nq], acc[64:128, 0:nq], [bacc], [b_OSB[o]])
                vtt(OT[sl * 64:(sl + 1) * 64, g, qc0:qc0 + nq], acc[0:64, 0:nq], OSB[o][0:64, 0:nq], ALU.mult,
                    [bacc, b_OSB[o]], [bOT(qc0)])

            pending = None
            emit_S(0)
            for p in range(len(pairs)):
                if p + 1 < len(pairs):
                    emit_S(p + 1)
                emit_rest(p)
                gi, h, j, idx, i, v, last = pairs[p]
                if pending is not None and (idx >= 3 or last):
                    emit_norm(pending)
                    pending = None
                if last:
                    pending = gi
            if pending is not None:
                emit_norm(pending)

        if "B" in phases:
            vmemset(QPAD[:], 0.0, [b_QPAD, b_xin[0], b_xin[1], b_junk])
            for i_ in range(2):
                vmemset(VA[i_][:, :, 64:128], 1.0, [b_VA[i_]])

            def fox_pre_head(h):
                g, sl = h // 2, h % 2
                qlo, qhi = QB[0][0], QB[-1][0] + QB[-1][1]
                vcopy(QPAD[sl * 64:(sl + 1) * 64, sl, qlo:qhi], FQT[sl * 64:(sl + 1) * 64, g, qlo:qhi], b_FQT, [b_QPAD])
                vcopy(VA[h % 2][:, :, 0:64], VF[:, :, h, 0:64], b_VF, [b_VA[h % 2]])

            attention(
                "fox", 128,
                lambda i, h: FKT[:, h // 2, KT[i][0]:KT[i][0] + KT[i][1]],
                lambda i, h: b_FKT[i],
                lambda j, h, c0: QPAD[:, h % 2, QB[j][0] + c0:QB[j][0] + QB[j][1]],
                lambda j, h: b_QPAD,
                lambda i, h: VA[h % 2][0:KT[i][1], i, :],
                lambda i, h: b_VA[h % 2],
                OTF, bOTF, FOX_SCALE, range(H), pre_head=fox_pre_head)
        P.fence()
        if "M" in phases:
            mmset["banks"] = mm_wide7
            AM = Arena(nc, ATT_BASE, SBUF_TOP, "mla%s%d" % (seq.kind, b))
            CQT = AM.alloc([128, 3, LK], BF16, "cqt")
            CKVT = AM.alloc([128, 2, LK], BF16, "ckvt")
            KRT = AM.alloc([128, LK], BF16, "krt")
            QP = AM.alloc([128, 2, LK], BF16, "qp")
            KP = AM.alloc([128, 2, LK], BF16, "kp")
            VM = AM.alloc([128, NKT, 2, 128], BF16, "vm")
            CSF = AM.alloc([128, 2, 512], F32, "csf")
            RT = AM.alloc([128, 2, 512], F32, "rt")
            CST = [AM.alloc([128, 64], F32, "cst%d" % i) for i in range(2)]
            b_CQT = [Buf("cqt%d" % j) for j in range(NQB)]
            b_CKVT = [Buf("ckvt%d" % i) for i in range(NKT)]
            b_KRT = [Buf("krt%d" % i) for i in range(NKT)]
            b_QP = [Buf("qp%d" % j) for j in range(NQB)]
            b_KP = [Buf("kp%d" % i) for i in range(NKT)]
            b_VM = [Buf("vm%d" % i) for i in range(NKT)]
            b_CSF = Buf("csf")
            b_RT = Buf("rt")
            b_CST = [Buf("cst%d" % i) for i in range(2)]
            vmemset(VM[:, :, :, 64:128], 1.0, b_VM)
            vmemset(KP[96:128, :, :], 0.0, b_KP)
            vmemset(QP[96:128, :, :], 0.0, b_QP)

            def qb_of(c0):
                return [j for j, (q0, qn) in enumerate(QB) if q0 <= c0 < q0 + qn][0]

            if not isP:
                for i in range(8):
                    s = i % 2
                    dma("pool", xnb[s][:, 0:256], c_ckv[b, i * 128:(i + 1) * 128, :], [], [b_xnb[s]])
                    dma("pool", xnb[s][:, 256:288], c_kr[b, i * 128:(i + 1) * 128, :], [], [b_xnb[s]])
                    for kc in range(2):
                        tr(ps_t[:, kc * 128:(kc + 1) * 128], xnb[s][:, kc * 128:(kc + 1) * 128], ident_b[:, :],
                           [b_xnb[s], b_const], [b_ps_t])
                    tr(ps_t[0:32, 256:384], xnb[s][:, 256:288], ident_b[:, :], [b_xnb[s], b_const], [b_ps_t])
                    vcopy(CKVT[:, :, i * 128:(i + 1) * 128], ps_t[:, 0:256].rearrange("p (g t) -> p g t", t=128),
                          [b_ps_t], [b_CKVT[i]])
                    vcopy(KRT[64:96, i * 128:(i + 1) * 128], ps_t[0:32, 256:384], [b_ps_t], [b_KRT[i]])
            Wc, bWc = load_w_group(OFF_CQ, QL)
            Wk, bWk = load_w_group(OFF_CKV, KVL + ROPE)
            def cq_head(ti):
                c0, n = NT[ti]
                s = ti % 2
                pm, bpm = tm_proj(Wc, bWc, c0, n, 0, QL)
                vmemset(stat[s][0:n, 0:1], 0.0, [b_stat[s]])
                act(junk[0:n, 0:QL], pm[0:n, 0:QL], AF.Square, [bpm], [b_junk, b_stat[s]], accum_out=stat[s][0:n, 0:1])
                act(stat[s][0:n, 1:2], stat[s][0:n, 0:1], AF.Sqrt, [b_stat[s], b_const], [b_stat[s]],
                    bias=cst_t[0:n, 0:1], scale=1.0 / QL)
                vrecip(stat[s][0:n, 2:3], stat[s][0:n, 1:2], [b_stat[s]], [b_stat[s]])
                vstt(xnb[s][0:n, 0:QL], pm[0:n, 0:QL], stat[s][0:n, 2:3], gq_t[0:n, :], ALU.mult, ALU.mult,
                     [bpm, b_stat[s], b_const], [b_xnb[s]])

            def cq_tail(ti):
                c0, n = NT[ti]
                s = ti % 2
                for kc in range(3):
                    tr(ps_t[:, kc * 128:kc * 128 + n], xnb[s][0:n, kc * 128:(kc + 1) * 128], ident_b[0:n, 0:n],
                       [b_xnb[s], b_const], [b_ps_t])
                vcopy(CQT[:, :, c0:c0 + n], ps_t[:, 0:384].rearrange("p (kc t) -> p kc t", t=128)[:, :, 0:n],
                      [b_ps_t], [b_CQT[qb_of(c0)]])

            pipelined(range(len(NT)), cq_head, cq_tail)
            def kv_head(ti):
                c0, n = NT[ti]
                s = ti % 2
                ki = KI0 + ti
                dma("sp", CST[s][0:n, :], c_cs_tm[c0:c0 + n, :], [], [b_CST[s]])
                pm, bpm = tm_proj(Wk, bWk, c0, n, 0, KVL + ROPE)
                vmemset(stat[s][0:n, 0:1], 0.0, [b_stat[s]])
                act(junk[0:n, 0:KVL], pm[0:n, 0:KVL], AF.Square, [bpm], [b_junk, b_stat[s]], accum_out=stat[s][0:n, 0:1])
                act(stat[s][0:n, 1:2], stat[s][0:n, 0:1], AF.Sqrt, [b_stat[s], b_const], [b_stat[s]],
                    bias=cst_t[0:n, 0:1], scale=1.0 / KVL)
                vrecip(stat[s][0:n, 2:3], stat[s][0:n, 1:2], [b_stat[s]], [b_stat[s]])
                o = ost[s]
                vstt(o[0:n, 0:KVL], pm[0:n, 0:KVL], stat[s][0:n, 2:3], gkv_t[0:n, :], ALU.mult, ALU.mult,
                     [bpm, b_stat[s], b_const], [b_ost[s]])
                vtt(o[0:n, 256:288], pm[0:n, 256:288], CST[s][0:n, 0:32], ALU.mult, [bpm, b_CST[s]], [b_ost[s]])
                vtt(o[0:n, 288:304], pm[0:n, 272:288], CST[s][0:n, 32:48], ALU.mult, [bpm, b_CST[s]], [b_ost[s]])
                vtt(o[0:n, 304:320], pm[0:n, 256:272], CST[s][0:n, 48:64], ALU.mult, [bpm, b_CST[s]], [b_ost[s]])
                vtt(o[0:n, 256:288], o[0:n, 256:288], o[0:n, 288:320], ALU.add, [b_ost[s]], [b_ost[s]])
                dma("sp", out_rows(out_ckv, c0, n), o[0:n, 0:KVL], [b_ost[s]], [])
                dma("sp", out_rows(out_kr, c0, n), o[0:n, 256:288], [b_ost[s]], [])
                vcopy(xnb[s][0:n, 0:288], o[0:n, 0:288], [b_ost[s]], [b_xnb[s]])

            def kv_tail(ti):
                c0, n = NT[ti]
                s = ti % 2
                ki = KI0 + ti
                for kc in range(2):
                    tr(ps_t[:, kc * 128:kc * 128 + n], xnb[s][0:n, kc * 128:(kc + 1) * 128], ident_b[0:n, 0:n],
                       [b_xnb[s], b_const], [b_ps_t])
                tr(ps_t[0:32, 256:256 + n], xnb[s][0:n, 256:288], ident_b[0:n, 0:n], [b_xnb[s], b_const], [b_ps_t])
                vcopy(CKVT[:, :, c0:c0 + n], ps_t[:, 0:256].rearrange("p (g t) -> p g t", t=128)[:, :, 0:n],
                      [b_ps_t], [b_CKVT[ki]])
                vcopy(KRT[64:96, c0:c0 + n], ps_t[0:32, 256:256 + n], [b_ps_t], [b_KRT[ki]])

            pipelined(range(len(NT)), kv_head, kv_tail)
            if isP:
                KB = list(QB)
            else:
                KB = [(0, 512), (512, 512), (PAST, DSEQ)]
            mmset["banks"] = mm_all
            for g in range(4):
                for sl in range(2):
                    h = 2 * g + sl
                    for (k0, kw) in KB:
                        kts = tiles_in(KT, k0, kw)
                        pm, bpm = next_mm()
                        for kc in range(2):
                            mm(pm[0:64, 0:kw], wukvk_t[:, kc, h * 64:(h + 1) * 64], CKVT[:, kc, k0:k0 + kw], kc == 0, kc == 1,
                               [b_CKVT[i] for i in kts] + [b_const], [bpm])
                        evac(KP[0:64, sl, k0:k0 + kw], pm[0:64, 0:kw], [bpm], [b_KP[i] for i in kts])
                        vcopy(KP[64:96, sl, k0:k0 + kw], KRT[64:96, k0:k0 + kw], [b_KRT[i] for i in kts],
                              [b_KP[i] for i in kts])
                for i, (k0, nk) in enumerate(KT):
                    pm, bpm = next_mm()
                    for kc in range(2):
                        mm(pm[0:nk, 0:128], CKVT[:, kc, k0:k0 + nk], wukvv_t[:, kc, g * 128:(g + 1) * 128], kc == 0, kc == 1,
                           [b_CKVT[i], b_const], [bpm])
                    evac(VM[0:nk, i, :, 0:64], pm[0:nk, 0:128].rearrange("p (s x) -> p s x", x=64), [bpm], [b_VM[i]])
                for j, (qc0, nq) in enumerate(QB):
                    dma("sp", CSF[64:96, 0, 0:nq], c_cs_fm[0, :, qc0:qc0 + nq], [], [b_CSF])
                    dma("sp", CSF[64:96, 1, 0:nq], c_cs_fm[1, :, qc0:qc0 + nq], [], [b_CSF])
                    for sl in range(2):
                        h = 2 * g + sl
                        pm1, bpm1 = next_mm()
                        for kc in range(3):
                            mm(pm1[0:96, 0:nq], wuq_t[:, kc, h * 96:(h + 1) * 96], CQT[:, kc, qc0:qc0 + nq], kc == 0, kc == 2,
                               [b_CQT[j], b_const], [bpm1])
                        pm2, bpm2 = next_mm()
                        for kc in range(3):
                            mm(pm2[0:96, 0:nq], wuqr_t[:, kc, h * 96:(h + 1) * 96], CQT[:, kc, qc0:qc0 + nq], kc == 0, kc == 2,
                               [b_CQT[j], b_const], [bpm2])
                        act(QP[0:64, sl, qc0:qc0 + nq], pm1[0:64, 0:nq], AF.Copy, [bpm1], [b_QP[j]])
                        vtt(RT[64:96, 0, 0:nq], pm1[64:96, 0:nq], CSF[64:96, 0, 0:nq], ALU.mult, [bpm1, b_CSF], [b_RT])
                        vtt(RT[64:96, 1, 0:nq], pm2[64:96, 0:nq], CSF[64:96, 1, 0:nq], ALU.mult, [bpm2, b_CSF], [b_RT])
                        vtt(QP[64:96, sl, qc0:qc0 + nq], RT[64:96, 0, 0:nq], RT[64:96, 1, 0:nq], ALU.add, [b_RT], [b_QP[j]])
                attention(
                    "mla", 96,
                    lambda i, h: KP[:, h % 2, KT[i][0]:KT[i][0] + KT[i][1]],
                    lambda i, h: b_KP[i],
                    lambda j, h, c0: QP[:, h % 2, QB[j][0] + c0:QB[j][0] + QB[j][1]],
                    lambda j, h: b_QP[j],
                    lambda i, h: VM[0:KT[i][1], i, h % 2, :],
                    lambda i, h: b_VM[i],
                    OTM, bOTM, MLA_SCALE, [2 * g, 2 * g + 1])
            P.fence()
        if "C" in phases:
            mmset["banks"] = mm_wide7
            AC = Arena(nc, PH_BASE, SBUF_TOP, "c%s%d" % (seq.kind, b))
            WOF = AC.alloc([128, 4, D], BF16, "wof")
            WOM = AC.alloc([128, 4, D], BF16, "wom")
            WG = AC.alloc([128, 8, 2 * D], BF16, "wg")
            WO = AC.alloc([128, 8, D], BF16, "wo")
            b_WC = Buf("wc")
            MT = AC.alloc([128, 8, 512], BF16, "mt")
            b_MT = Buf("mt")
            G0 = [AC.alloc([128, 512], F32, "g0%d" % i) for i in range(2)]
            G1 = [AC.alloc([128, 512], F32, "g1%d" % i) for i in range(2)]
            M0 = [AC.alloc([128, 512], F32, "m0%d" % i) for i in range(2)]
            b_G0 = [Buf("g0%d" % i) for i in range(2)]
            b_G1 = [Buf("g1%d" % i) for i in range(2)]
            b_M0 = [Buf("m0%d" % i) for i in range(2)]
            cxin = [AC.alloc([128, D], F32, "cxin%d" % i) for i in range(2)]
            b_cxin = [Buf("cxin%d" % i) for i in range(2)]
            cxnb = [AC.alloc([128, D], BF16, "cxnb%d" % i) for i in range(2)]
            b_cxnb = [Buf("cxnb%d" % i) for i in range(2)]
            cjunk = AC.alloc([128, D], BF16, "cjunk")
            b_cjunk = Buf("cjunk")
            cstat = [AC.alloc([128, 4], F32, "cstat%d" % i) for i in range(2)]
            b_cstat = [Buf("cstat%d" % i) for i in range(2)]
            b_WOF = Buf("wof")
            b_WOM = Buf("wom")
            b_WG = [Buf("wg%d" % i) for i in range(4)]
            b_WO = [Buf("wo%d" % i) for i in range(2)]
            w_in_v = w_in.rearrange("(kc p) f -> p kc f", p=128)

            def ld_wg(hh):
                dma("pool", WG[:, :, hh * 512:(hh + 1) * 512],
                    w_in_v[:, :, OFF_GATE + hh * 512:OFF_GATE + (hh + 1) * 512], [], [b_WG[hh]])

            dma("pool", WOF[:], w_ofox.rearrange("(kc p) f -> p kc f", p=128), [], [b_WOF])
            ld_wg(0)
            dma("pool", WOM[:], w_omla.rearrange("(kc p) f -> p kc f", p=128), [], [b_WOM])
            ld_wg(2)
            ld_wg(1)
            ld_wg(3)
            for hh in range(2):
                dma("pool", WO[:, :, hh * 512:(hh + 1) * 512],
                    w_out.rearrange("(kc p) f -> p kc f", p=128)[:, :, hh * 512:(hh + 1) * 512], [], [b_WO[hh]])
            tcount = 0
            for j, (qc0, nq) in enumerate(QB):
                xb = xt_bufs(qc0, nq)
                for m in range(8):
                    s = m % 2
                    pa, bpa = next_mm()
                    for kc in range(4):
                        mm(pa[:, 0:nq], WOF[:, kc, m * 128:(m + 1) * 128], OTF[:, kc, qc0:qc0 + nq], kc == 0, kc == 3,
                           [b_WOF, bOTF(qc0)], [bpa])
                    pg, bpg = next_mm()
                    for kc in range(8):
                        mm(pg[:, 0:nq], WG[:, kc, m * 128:(m + 1) * 128], XT[:, kc, qc0:qc0 + nq], kc == 0, kc == 7,
                           [b_WG[m // 4]] + xb, [bpg])
                    act(G0[s][:, 0:nq], pg[:, 0:nq], AF.Sigmoid, [bpg], [b_G0[s]])
                    vtt(M0[s][:, 0:nq], G0[s][:, 0:nq], pa[:, 0:nq], ALU.mult, [b_G0[s], bpa], [b_M0[s]])
                    pb, bpb = next_mm()
                    for kc in range(4):
                        mm(pb[:, 0:nq], WOM[:, kc, m * 128:(m + 1) * 128], OTM[:, kc, qc0:qc0 + nq], kc == 0, kc == 3,
                           [b_WOM, bOTM(qc0)], [bpb])
                    pg2, bpg2 = next_mm()
                    for kc in range(8):
                        mm(pg2[:, 0:nq], WG[:, kc, D + m * 128:D + (m + 1) * 128], XT[:, kc, qc0:qc0 + nq], kc == 0, kc == 7,
                           [b_WG[2 + m // 4]] + xb, [bpg2])
                    act(G1[s][:, 0:nq], pg2[:, 0:nq], AF.Sigmoid, [bpg2], [b_G1[s]])
                    vtt(G1[s][:, 0:nq], G1[s][:, 0:nq], pb[:, 0:nq], ALU.mult, [b_G1[s], bpb], [b_G1[s]])
                    vtt(MT[:, m, 0:nq], M0[s][:, 0:nq], G1[s][:, 0:nq], ALU.add, [b_M0[s], b_G1[s]], [b_MT])
                def c_head(arg):
                    ti, s = arg
                    c0, n = NT[ti]
                    o = c0 - qc0
                    dma("sp", cxin[s][0:n, :], src_rows(c0, n), [], [b_cxin[s]])
                    for hw in range(2):
                        pm, bpm = next_mm()
                        for kc in range(8):
                            mm(pm[0:n, :], MT[:, kc, o:o + n], WO[:, kc, hw * 512:(hw + 1) * 512], kc == 0, kc == 7,
                               [b_MT, b_WO[hw]], [bpm])
                        vtt(cxin[s][0:n, hw * 512:(hw + 1) * 512], cxin[s][0:n, hw * 512:(hw + 1) * 512], pm[0:n, :], ALU.add,
                            [b_cxin[s], bpm], [b_cxin[s]])
                    dma("sp", h2_scr[c0 - nc0:c0 - nc0 + n, :], cxin[s][0:n, :], [b_cxin[s]], [b_h2scr])
                    vmemset(cstat[s][0:n, 0:1], 0.0, [b_cstat[s]])
                    act(cjunk[0:n, :], cxin[s][0:n, :], AF.Square, [b_cxin[s]], [b_cjunk, b_cstat[s]],
                        accum_out=cstat[s][0:n, 0:1])
                    act(cstat[s][0:n, 1:2], cstat[s][0:n, 0:1], AF.Sqrt, [b_cstat[s], b_const], [b_cstat[s]],
                        bias=cst_t[0:n, 0:1], scale=1.0 / D)
                    vrecip(cstat[s][0:n, 2:3], cstat[s][0:n, 1:2], [b_cstat[s]], [b_cstat[s]])
                    vstt(cxnb[s][0:n, :], cxin[s][0:n, :], cstat[s][0:n, 2:3], gffn_t[0:n, :], ALU.mult, ALU.mult,
                         [b_cxin[s], b_cstat[s], b_const], [b_cxnb[s]])

                def c_tail(arg):
                    ti, s = arg
                    c0, n = NT[ti]
                    for kc in range(8):
                        tr(ps_t[:, kc * 128:kc * 128 + n], cxnb[s][0:n, kc * 128:(kc + 1) * 128], ident_b[0:n, 0:n],
                           [b_cxnb[s], b_const], [b_ps_t])
                    vcopy(XT[:, :, c0:c0 + n], ps_t[:].rearrange("p (kc t) -> p kc t", t=128)[:, :, 0:n], [b_ps_t], [bXT(c0)])

                targs = []
                for ti in tiles_in(NT, qc0, nq):
                    targs.append((ti, tcount % 2))
                    tcount += 1
                pipelined(targs, c_head, c_tail)
            P.fence()
        if "D" in phases:
            mmset["banks"] = mm_wide5
            AD = Arena(nc, OT_BASE, SBUF_TOP, "d%s%d" % (seq.kind, b))
            if isP:
                halves = [[0, 1, 2], [3, 4]]
            else:
                halves = [[0]]
            HW_MAX = max(sum(QB[j][1] for j in blocks) for blocks in halves)
            WD = AD.alloc([128, NGC, D], BF16, "wd")
            b_WD = Buf("wd")
            AT = AD.alloc([128, NGC, HW_MAX], BF16, "at")
            b_AT = Buf("at")
            UFG = [AD.alloc([128, 2 + HW_MAX], BF16, "ufg%d" % i) for i in range(2)]
            UFV = [AD.alloc([128, 2 + HW_MAX], BF16, "ufv%d" % i) for i in range(2)]
            b_UFG = [Buf("ufg%d" % i) for i in range(2)]
            b_UFV = [Buf("ufv%d" % i) for i in range(2)]
            WUP = [AD.alloc([128, 8, 256], BF16, "wup%d" % i) for i in range(3)]
            b_WUP = [Buf("wup%d" % i) for i in range(3)]
            DG = [AD.alloc([128, 6, 128], BF16, "dg%d" % i) for i in range(2)]
            b_DG = [Buf("dg%d" % i) for i in range(2)]
            SG = [AD.alloc([128, 512], F32, "sg%d" % i) for i in range(2)]
            b_SG = [Buf("sg%d" % i) for i in range(2)]
            ULAST = AD.alloc([128, NCH, 2], BF16, "ulast")
            b_UL = Buf("ulast")
            CSO = [AD.alloc([2, 256], F32, "cso%d" % i) for i in range(2)]
            b_CSO = [Buf("cso%d" % i) for i in range(2)]
            dh2 = [AD.alloc([128, D], F32, "dh2%d" % i) for i in range(2)]
            b_dh2 = [Buf("dh2%d" % i) for i in range(2)]
            dy = [AD.alloc([128, D], F32, "dy%d" % i) for i in range(2)]
            b_dy = [Buf("dy%d" % i) for i in range(2)]
            djunk = AD.alloc([128, D], BF16, "djunk")
            b_djunk = Buf("djunk")
            dstat = [AD.alloc([128, 4], F32, "dstat%d" % i) for i in range(2)]
            b_dstat = [Buf("dstat%d" % i) for i in range(2)]
            if isP:
                vmemset(ULAST[:], 0.0, [b_UL])
            else:
                ccv = AD.alloc([2, UPW], BF16, "ccv")
                b_ccv = Buf("ccv")
                dma("pool", ccv[:], c_conv[b], [], [b_ccv])
                for c in range(NCH):
                    tr(ps_t[:, c * 2:c * 2 + 2], ccv[0:2, c * 128:(c + 1) * 128], ident_b[0:2, 0:2], [b_ccv, b_const], [b_ps_t])
                vcopy(ULAST[:].rearrange("p c j -> p (c j)"), ps_t[:, 0:NCH * 2], [b_ps_t], [b_UL])
            wupc = 0
            tcount = 0
            for hi, blocks in enumerate(halves):
                hc0 = QB[blocks[0]][0]
                hw_ = sum(QB[j][1] for j in blocks)
                last_half = hi == len(halves) - 1
                for c in range(NGC):
                    ws = wupc % 3
                    us = wupc % 2
                    wupc += 1
                    wv = w_up.rearrange("(kc p) f -> p kc f", p=128)
                    dma("pool", WUP[ws][:, :, 0:128], wv[:, :, c * 128:(c + 1) * 128], [], [b_WUP[ws]])
                    dma("pool", WUP[ws][:, :, 128:256], wv[:, :, DFF + c * 128:DFF + (c + 1) * 128], [], [b_WUP[ws]])
                    if hi == 0 and c == 2:
                        for hh in range(2):
                            dma("pool", WD[:, :, hh * 512:(hh + 1) * 512],
                                w_down.rearrange("(c p) f -> p c f", p=128)[:, :, hh * 512:(hh + 1) * 512], [], [b_WD])
                    def d_prep(cc, uu):
                        for t in range(3):
                            vts(DG[uu][:, t, :], ident_b[:, :], cwb_t[:, cc, t:t + 1], None, ALU.mult, None, [b_const], [b_DG[uu]])
                            vts(DG[uu][:, 3 + t, :], ident_b[:, :], cwb_t[:, NGC + cc, t:t + 1], None, ALU.mult, None,
                                [b_const], [b_DG[uu]])
                        vcopy(UFG[uu][:, 0:2], ULAST[:, cc, :], [b_UL], [b_UFG[uu]])
                        vcopy(UFV[uu][:, 0:2], ULAST[:, NGC + cc, :], [b_UL], [b_UFV[uu]])

                    if c == 0:
                        d_prep(0, us)
                    for bi, j in enumerate(blocks):
                        if bi == 1 or (bi == 0 and len(blocks) == 1):
                            if c + 1 < NGC:
                                d_prep(c + 1, (us + 1) % 2)
                        qc0, nq = QB[j]
                        o = qc0 - hc0
                        xb = xt_bufs(qc0, nq)
                        pg, bpg = next_mm()
                        for kc in range(8):
                            mm(pg[:, 0:nq], WUP[ws][:, kc, 0:128], XT[:, kc, qc0:qc0 + nq], kc == 0, kc == 7,
                               [b_WUP[ws]] + xb, [bpg])
                        act(UFG[us][:, 2 + o:2 + o + nq], pg[:, 0:nq], AF.Copy, [bpg], [b_UFG[us]])
                        pv, bpv = next_mm()
                        for kc in range(8):
                            mm(pv[:, 0:nq], WUP[ws][:, kc, 128:256], XT[:, kc, qc0:qc0 + nq], kc == 0, kc == 7,
                               [b_WUP[ws]] + xb, [bpv])
                        vcopy(UFV[us][:, 2 + o:2 + o + nq], pv[:, 0:nq], [bpv], [b_UFV[us]])
                        pcg, bpcg = next_s()
                        for t in range(3):
                            mm(pcg[:, 0:nq], DG[us][:, t, :], UFG[us][:, o + t:o + t + nq], t == 0, t == 2,
                               [b_DG[us], b_UFG[us]], [bpcg])
                        pcv, bpcv = next_s()
                        for t in range(3):
                            mm(pcv[:, 0:nq], DG[us][:, 3 + t, :], UFV[us][:, o + t:o + t + nq], t == 0, t == 2,
                               [b_DG[us], b_UFV[us]], [bpcv])
                        act(SG[us][:, 0:nq], pcg[:, 0:nq], AF.Silu, [bpcg, b_const], [b_SG[us]], bias=cwb_t[:, c, 3:4])
                        vstt(AT[:, c, o:o + nq], pcv[:, 0:nq], cwb_t[:, NGC + c, 3:4], SG[us][:, 0:nq], ALU.add, ALU.mult,
                             [bpcv, b_const, b_SG[us]], [b_AT])
                    if not last_half:
                        vcopy(ULAST[:, c, :], UFG[us][:, hw_:hw_ + 2], [b_UFG[us]], [b_UL])
                        vcopy(ULAST[:, NGC + c, :], UFV[us][:, hw_:hw_ + 2], [b_UFV[us]], [b_UL])
                    else:
                        lc = seq.lk - 2
                        pm, bpm = next_mm()
                        for kc in range(8):
                            mm(pm[0:2, 0:256], XT[:, kc, lc:lc + 2], WUP[ws][:, kc, :], kc == 0, kc == 7,
                               [b_WUP[ws]] + xt_bufs(lc, 2), [bpm])
                        vcopy(CSO[us][0:2, 0:256], pm[0:2, 0:256], [bpm], [b_CSO[us]])
                        dma("sp", out_cv[:, c * 128:(c + 1) * 128], CSO[us][0:2, 0:128], [b_CSO[us]], [])
                        dma("sp", out_cv[:, DFF + c * 128:DFF + (c + 1) * 128], CSO[us][0:2, 128:256], [b_CSO[us]], [])
                for ti in tiles_in(NT, hc0, hw_):
                    c0, n = NT[ti]
                    s = tcount % 2
                    tcount += 1
                    o = c0 - hc0
                    dma("sp", dh2[s][0:n, :], h2_scr[c0 - nc0:c0 - nc0 + n, :], [b_h2scr], [b_dh2[s]])
                    for hw in range(2):
                        pm, bpm = next_mm()
                        for c in range(NGC):
                            mm(pm[0:n, :], AT[:, c, o:o + n], WD[:, c, hw * 512:(hw + 1) * 512], c == 0, c == NGC - 1,
                               [b_AT, b_WD], [bpm])
                        vtt(dh2[s][0:n, hw * 512:(hw + 1) * 512], dh2[s][0:n, hw * 512:(hw + 1) * 512], pm[0:n, :], ALU.add,
                            [b_dh2[s], bpm], [b_dh2[s]])
                    vmemset(dstat[s][0:n, 0:1], 0.0, [b_dstat[s]])
                    act(djunk[0:n, :], dh2[s][0:n, :], AF.Square, [b_dh2[s]], [b_djunk, b_dstat[s]],
                        accum_out=dstat[s][0:n, 0:1])
                    act(dstat[s][0:n, 1:2], dstat[s][0:n, 0:1], AF.Sqrt, [b_dstat[s], b_const], [b_dstat[s]],
                        bias=cst_t[0:n, 0:1], scale=1.0 / D)
                    vrecip(dstat[s][0:n, 2:3], dstat[s][0:n, 1:2], [b_dstat[s]], [b_dstat[s]])
                    vstt(dy[s][0:n, :], dh2[s][0:n, :], dstat[s][0:n, 2:3], gfin_t[0:n, :], ALU.mult, ALU.mult,
                         [b_dh2[s], b_dstat[s], b_const], [b_dy[s]])
                    if isP:
                        if c0 >= NMETA:
                            dma("sp", out_y[c0 - NMETA:c0 - NMETA + n, :], dy[s][0:n, :], [b_dy[s]], [])
                    else:
                        dma("sp", out_y[c0 - nc0:c0 - nc0 + n, :], dy[s][0:n, :], [b_dy[s]], [])
            P.fence()

    P.fence()
    P.emit(nc)
    return nc


_CFG = {"n_prompt": PB, "n_sample": SB, "phases": "ABMCD"}
_NC_CACHE = {}


def _constants():
    c = {}
    c["c_ident"] = np.eye(128, dtype=np.float32)
    p = np.arange(128)
    c["c_utri"] = (p[:, None] <= p[None, :]).astype(np.float32)
    c["c_ones"] = np.ones((128, 128), np.float32)
    c["c_mtri"] = (p[:, None] <= p[None, :]).astype(np.float32)
    c["c_mchk"] = ((p[:, None] // 64) <= (p[None, :] // 64)).astype(np.float32)
    inv = (10000.0 ** (-np.arange(0, ROPE, 2, dtype=np.float32) / np.float32(ROPE))).astype(np.float32)
    pos = np.arange(L, dtype=np.float32)
    ang = (pos[:, None] * inv[None, :]).astype(np.float32).astype(np.float64)
    cos = np.cos(ang).astype(np.float32)
    sin = np.sin(ang).astype(np.float32)
    c["c_cs_tm"] = np.ascontiguousarray(np.concatenate([cos, cos, -sin, sin], axis=1))
    c["c_cs_fm"] = np.ascontiguousarray(np.stack([np.concatenate([cos, cos], axis=1).T,
                                                  np.concatenate([sin, sin], axis=1).T], axis=0))
    return c


def kernel(x_prompt, x_sample, cache_fox_k, cache_fox_v, cache_fox_logf, cache_mla_ckv, cache_mla_krope,
           state_ffn_conv, meta_tokens, norm_mix_g, w_in, b_forget, mla_q_norm_g, w_uq, mla_kv_norm_g, w_ukv,
           w_o_fox, w_o_mla, w_out, norm_ffn_g, w_up, conv_w, conv_b, w_down, norm_final_g):
    f = lambda a: np.ascontiguousarray(np.asarray(a, dtype=np.float32))
    key = (_CFG["n_prompt"], _CFG["n_sample"], _CFG["phases"])
    if key not in _NC_CACHE:
        _NC_CACHE[key] = build_program(*key)
    nc = _NC_CACHE[key]
    rep = lambda v: np.ascontiguousarray(np.broadcast_to(f(v).reshape(1, -1), (128, f(v).size)))
    shared = {
        "meta_tokens": f(meta_tokens), "w_in": f(w_in)[0], "w_uq": f(w_uq)[0], "w_ukv": f(w_ukv)[0],
        "w_o_fox": f(w_o_fox)[0], "w_o_mla": f(w_o_mla)[0], "w_out": f(w_out)[0], "w_up": f(w_up)[0],
        "w_down": f(w_down)[0],
        "convwb": np.ascontiguousarray(np.concatenate([f(conv_w)[0], f(conv_b)[0][None, :]], axis=0)
                                       .reshape(4, NCH, 128).transpose(2, 1, 0)),
        "g_mix_bc": rep(norm_mix_g), "g_ffn_bc": rep(norm_ffn_g), "g_fin_bc": rep(norm_final_g),
        "g_q_bc": rep(mla_q_norm_g), "g_kv_bc": rep(mla_kv_norm_g), "b_forget_bc": rep(b_forget),
    }
    shared.update(_constants())
    xp = f(x_prompt)
    xs = f(x_sample)
    in_maps = []
    for c in range(N_CORES):
        m = dict(shared)
        m["x_prompt"] = xp[c * PB:(c + 1) * PB]
        m["x_sample"] = xs[c * SB:(c + 1) * SB]
        m["cache_fox_k"] = f(cache_fox_k)[0, c * SB:(c + 1) * SB].reshape(SB, PAST, FOXW)
        m["cache_fox_v"] = f(cache_fox_v)[0, c * SB:(c + 1) * SB].reshape(SB, PAST, FOXW)
        m["cache_fox_logf"] = f(cache_fox_logf)[0, c * SB:(c + 1) * SB]
        m["cache_mla_ckv"] = f(cache_mla_ckv)[0, c * SB:(c + 1) * SB]
        m["cache_mla_krope"] = f(cache_mla_krope)[0, c * SB:(c + 1) * SB]
        m["state_ffn_conv"] = f(state_ffn_conv)[0, c * SB:(c + 1) * SB]
        in_maps.append({k: np.ascontiguousarray(v) for k, v in m.items()})
    res = run_bass_kernel_spmd(nc, in_maps, core_ids=list(range(N_CORES)))
    R = res.results
    cat = lambda name: np.concatenate([np.asarray(r[name], dtype=np.float32) for r in R], axis=0)
    B = N_CORES * PB
    S = N_CORES * SB
    return (
        cat("y_prompt"), cat("y_sample"),
        cat("new_fox_k_p").reshape(1, B, L, H, HD), cat("new_fox_v_p").reshape(1, B, L, H, HD),
        cat("new_fox_logf_p").reshape(1, B, L, H), cat("new_mla_ckv_p").reshape(1, B, L, KVL),
        cat("new_mla_krope_p").reshape(1, B, L, ROPE), cat("new_ffn_conv_p").reshape(1, B, 2, UPW),
        cat("new_fox_k_s").reshape(1, S, DSEQ, H, HD), cat("new_fox_v_s").reshape(1, S, DSEQ, H, HD),
        cat("new_fox_logf_s").reshape(1, S, DSEQ, H), cat("new_mla_ckv_s").reshape(1, S, DSEQ, KVL),
        cat("new_mla_krope_s").reshape(1, S, DSEQ, ROPE), cat("new_ffn_conv_s").reshape(1, S, 2, UPW),
    )
```

```python
import math
import os
import numpy as np
import concourse.bass as bass
import concourse.mybir as mybir
from concourse.bass_utils import run_bass_kernel_spmd

F32 = mybir.dt.float32
BF16 = mybir.dt.bfloat16
AF = mybir.ActivationFunctionType
ALU = mybir.AluOpType

N_CORES = 8
D = 1024
SEQ = 2048
NMETA = 16
L = NMETA + SEQ
PB = 4
SB = 2
DSEQ = 16
PAST = 1024
H = 8
HD = 64
FOXW = 512
QL = 384
KVL = 256
ROPE = 32
NOPE = 64
MQK = 96
DFF = 2816
UPW = 2 * DFF
NCH = UPW // 128
NGC = DFF // 128
OFF_FK = 512
OFF_FV = 1024
OFF_FF = 1536
OFF_CQ = 1544
OFF_CKV = 1928
OFF_KR = 2184
OFF_GATE = 2216
INW = OFF_GATE + 2 * D
EPS = 1e-6
FOX_SCALE = 1.0 / math.sqrt(HD)
MLA_SCALE = 1.0 / math.sqrt(MQK)
LK = L

SBUF_BASE = 16640
SBUF_TOP = 229344


class Buf:
    __slots__ = ("name", "last_w", "rd_eng", "rd_dma", "excl")

    def __init__(self, name, excl=False):
        self.name = name
        self.excl = excl
        self.last_w = None
        self.rd_eng = {}
        self.rd_dma = []


class Op:
    __slots__ = ("eng", "fn", "dma", "deps", "signal", "sem", "semval", "prev")

    def __init__(self, eng, fn, dma):
        self.eng = eng
        self.fn = fn
        self.dma = dma
        self.deps = []
        self.signal = False
        self.sem = None
        self.semval = 0
        self.prev = 0


ENGS = ("pe", "act", "dve", "pool", "sp")
NDMASEM = {"sp": 20, "pool": 12, "act": 6}


class Prog:
    def __init__(self):
        self.ops = {e: [] for e in ENGS}
        self.last_op = {e: None for e in ENGS}
        self.last_real = {}
        self.dma_since_fence = {e: [] for e in ENGS}
        self.nops = 0

    def op(self, eng, fn, reads=(), writes=(), dma=False, extra=(), real=True):
        o = Op(eng, fn, dma)
        deps = {}
        xr = [b for b in reads if b.excl]
        if xr:
            reads = [b for b in reads if not b.excl]
            writes = list(writes) + [b for b in xr if b not in writes]
        for b in reads:
            w = b.last_w
            if w is not None:
                deps[id(w)] = (w, True)
        for b in writes:
            w = b.last_w
            if w is not None and id(w) not in deps:
                deps[id(w)] = (w, False)
            for r in b.rd_eng.values():
                if id(r) not in deps:
                    deps[id(r)] = (r, False)
            for r in b.rd_dma:
                if id(r) not in deps:
                    deps[id(r)] = (r, False)
        for d in extra:
            deps[id(d)] = (d, True)
        for d, raw in deps.values():
            if d is o:
                continue
            if d.dma or o.dma or d.eng != o.eng:
                need = True
            elif o.eng == "pe":
                need = False
            else:
                need = True
            if need:
                o.deps.append(d)
                d.signal = True
        for b in reads:
            if dma:
                b.rd_dma.append(o)
            else:
                b.rd_eng[eng] = o
        for b in writes:
            b.last_w = o
            b.rd_eng = {}
            b.rd_dma = []
        self.ops[eng].append(o)
        self.last_op[eng] = o
        if real and not dma:
            self.last_real[eng] = o
        if dma:
            self.dma_since_fence[eng].append(o)
        self.nops += 1
        return o

    def fence(self):
        dmas = []
        for e in ENGS:
            dmas.extend(self.dma_since_fence[e])
            self.dma_since_fence[e] = []
        lasts = [self.last_real[e] for e in ENGS if self.last_real.get(e) is not None]
        for e in ENGS:
            extra = [d for d in dmas] + [l for l in lasts if l.eng != e]
            self.op(e, lambda en: en.nop(nofuse=True), extra=extra, real=False)

    def emit(self, nc):
        sems = {}
        for e in ENGS:
            sems[e] = nc.alloc_semaphore("s_" + e)
        dsem = {e: [nc.alloc_semaphore("d_%s%d" % (e, i)) for i in range(n)] for e, n in NDMASEM.items()}
        for e in ENGS:
            cnt = 0
            di = 0
            dvals = [0] * NDMASEM.get(e, 1)
            for o in self.ops[e]:
                if o.dma:
                    k = di % NDMASEM[e]
                    di += 1
                    o.sem = dsem[e][k]
                    o.prev = dvals[k]
                    dvals[k] += 16
                    o.semval = dvals[k]
                elif o.signal:
                    cnt += 1
                    o.sem = sems[e]
                    o.semval = cnt
        engobj = {"pe": "tensor", "act": "scalar", "dve": "vector", "pool": "gpsimd", "sp": "sync"}
        ops = self.ops

        def make(e):
            def body(en):
                waited = {}
                for o in ops[e]:
                    for d in o.deps:
                        k = id(d.sem)
                        if waited.get(k, 0) < d.semval:
                            en.wait_ge(d.sem, d.semval)
                            waited[k] = d.semval
                    if o.dma and o.prev > 0:
                        k = id(o.sem)
                        if waited.get(k, 0) < o.prev:
                            en.wait_ge(o.sem, o.prev)
                            waited[k] = o.prev
                    inst = o.fn(en)
                    if o.dma:
                        inst.then_inc(o.sem, 16)
                    elif o.signal:
                        inst.then_inc(o.sem, 1)
            return body

        with nc.Block() as block:
            for e in ENGS:
                if ops[e]:
                    getattr(block, engobj[e])(make(e))


class Arena:
    def __init__(self, nc, base, top, tag):
        self.nc = nc
        self.base = base
        self.cur = base
        self.top = top
        self.tag = tag
        self.n = 0

    def alloc(self, shape, dtype, name):
        per = 1
        for s in shape[1:]:
            per *= s
        nbytes = per * (4 if dtype == F32 else 2)
        off = (self.cur + 31) // 32 * 32
        if off + nbytes > self.top:
            raise RuntimeError("SBUF arena %s overflow allocating %s %s (%d > %d)" % (self.tag, name, shape, off + nbytes, self.top))
        self.cur = off + nbytes
        self.n += 1
        return self.nc.alloc_sbuf_tensor_at("%s_%s_%d" % (self.tag, name, self.n), list(shape), dtype, offset=off)


class Seq:
    pass


def make_prompt_seq(b):
    s = Seq()
    s.kind = "p"
    s.b = b
    s.ncache = 0
    s.nt = [(0, NMETA)] + [(NMETA + 128 * i, 128) for i in range(16)]
    s.kt = list(s.nt)
    s.qb = [(0, NMETA)] + [(NMETA + 512 * j, 512) for j in range(4)]
    s.lk = L
    return s


def make_sample_seq(b):
    s = Seq()
    s.kind = "s"
    s.b = b
    s.ncache = PAST
    s.nt = [(PAST, DSEQ)]
    s.kt = [(128 * i, 128) for i in range(8)] + [(PAST, DSEQ)]
    s.qb = [(PAST, DSEQ)]
    s.lk = PAST + DSEQ
    return s


def vis(seq, j, i, kind):
    if seq.kind == "s":
        if i < 8:
            return ("full", 0, None)
        return ("diag", 0, "tri") if kind == "fox" else ("full", 0, None)
    if j == 0:
        if i != 0:
            return None
        return ("diag", 0, "tri") if kind == "fox" else ("full", 0, None)
    if i == 0:
        return ("full", 0, None)
    first = 4 * (j - 1) + 1
    if i < first:
        return ("full", 0, None)
    d = i - first
    if d > 3:
        return None
    return ("diag", 128 * d, "tri" if kind == "fox" else "chunk")


def build_program(n_prompt=PB, n_sample=SB, phases="ABCD"):
    nc = bass.Bass("TRN2", target_bir_lowering=False)
    P = Prog()

    def din(name, shape):
        return nc.dram_tensor(name, list(shape), F32, kind="ExternalInput").ap()

    def dout(name, shape):
        return nc.dram_tensor(name, list(shape), F32, kind="ExternalOutput").ap()

    x_p = din("x_prompt", [PB, SEQ, D])
    x_s = din("x_sample", [SB, DSEQ, D])
    c_fk = din("cache_fox_k", [SB, PAST, FOXW])
    c_fv = din("cache_fox_v", [SB, PAST, FOXW])
    c_lf = din("cache_fox_logf", [SB, PAST, H])
    c_ckv = din("cache_mla_ckv", [SB, PAST, KVL])
    c_kr = din("cache_mla_krope", [SB, PAST, ROPE])
    c_conv = din("state_ffn_conv", [SB, 2, UPW])
    meta = din("meta_tokens", [NMETA, D])
    w_in = din("w_in", [D, INW])
    w_uq = din("w_uq", [QL, H * MQK])
    w_ukv = din("w_ukv", [KVL, H * 128])
    w_ofox = din("w_o_fox", [FOXW, D])
    w_omla = din("w_o_mla", [FOXW, D])
    w_out = din("w_out", [D, D])
    w_up = din("w_up", [D, UPW])
    w_down = din("w_down", [DFF, D])
    convwb = din("convwb", [128, NCH, 4])
    g_mix = din("g_mix_bc", [128, D])
    g_ffn = din("g_ffn_bc", [128, D])
    g_fin = din("g_fin_bc", [128, D])
    g_q = din("g_q_bc", [128, QL])
    g_kv = din("g_kv_bc", [128, KVL])
    b_fg = din("b_forget_bc", [128, H])
    c_ident = din("c_ident", [128, 128])
    c_utri = din("c_utri", [128, 128])
    c_ones = din("c_ones", [128, 128])
    c_mtri = din("c_mtri", [128, 128])
    c_mchk = din("c_mchk", [128, 128])
    c_cs_tm = din("c_cs_tm", [L, 64])
    c_cs_fm = din("c_cs_fm", [2, ROPE, L])

    o_y_p = dout("y_prompt", [PB, SEQ, D])
    o_y_s = dout("y_sample", [SB, DSEQ, D])
    o_fk_p = dout("new_fox_k_p", [PB, L, FOXW])
    o_fv_p = dout("new_fox_v_p", [PB, L, FOXW])
    o_lf_p = dout("new_fox_logf_p", [PB, L, H])
    o_ckv_p = dout("new_mla_ckv_p", [PB, L, KVL])
    o_kr_p = dout("new_mla_krope_p", [PB, L, ROPE])
    o_cv_p = dout("new_ffn_conv_p", [PB, 2, UPW])
    o_fk_s = dout("new_fox_k_s", [SB, DSEQ, FOXW])
    o_fv_s = dout("new_fox_v_s", [SB, DSEQ, FOXW])
    o_lf_s = dout("new_fox_logf_s", [SB, DSEQ, H])
    o_ckv_s = dout("new_mla_ckv_s", [SB, DSEQ, KVL])
    o_kr_s = dout("new_mla_krope_s", [SB, DSEQ, ROPE])
    o_cv_s = dout("new_ffn_conv_s", [SB, 2, UPW])
    h2_scr = nc.dram_tensor("h2_scratch", [L, D], F32, kind="Internal").ap()

    ps_mm = [nc.alloc_psum_tensor("ps_mm%d" % i, [128, 512], F32) for i in range(3)]
    ps_s = [nc.alloc_psum_tensor("ps_s%d" % i, [128, 512], F32) for i in range(2)]
    ps_acc = [nc.alloc_psum_tensor("ps_acc%d" % i, [128, 512], F32) for i in range(2)]
    ps_t = nc.alloc_psum_tensor("ps_t", [128, 1024], BF16)
    b_ps_mm = [Buf("ps_mm%d" % i, True) for i in range(3)]
    b_ps_s = [Buf("ps_s%d" % i, True) for i in range(2)]
    b_ps_acc = [Buf("ps_acc%d" % i, True) for i in range(2)]
    b_ps_t = Buf("ps_t", True)
    rr = {"mm": 0, "s": 0, "acc": 0}

    mm_all = [(ps_mm[i], b_ps_mm[i]) for i in range(3)]
    mm_wide7 = mm_all + [(ps_s[i], b_ps_s[i]) for i in range(2)] + [(ps_acc[i], b_ps_acc[i]) for i in range(2)]
    mm_wide5 = mm_all + [(ps_acc[i], b_ps_acc[i]) for i in range(2)]
    mmset = {"banks": mm_all}

    def next_mm():
        bk = mmset["banks"]
        i = rr["mm"] % len(bk)
        rr["mm"] += 1
        return bk[i]

    def next_s():
        i = rr["s"] % 2
        rr["s"] += 1
        return ps_s[i], b_ps_s[i]

    def next_acc():
        i = rr["acc"] % 2
        rr["acc"] += 1
        return ps_acc[i], b_ps_acc[i]

    A0 = Arena(nc, SBUF_BASE, SBUF_TOP, "pers")
    ident_f = A0.alloc([128, 128], F32, "identf")
    utri_f = A0.alloc([128, 128], F32, "utri")
    ones_f = A0.alloc([128, 128], F32, "ones")
    ident_b = A0.alloc([128, 128], BF16, "identb")
    mtri_b = A0.alloc([128, 128], BF16, "mtri")
    mchk_b = A0.alloc([128, 128], BF16, "mchk")
    gmix_t = A0.alloc([128, D], F32, "gmix")
    gffn_t = A0.alloc([128, D], F32, "gffn")
    gfin_t = A0.alloc([128, D], F32, "gfin")
    gq_t = A0.alloc([128, QL], F32, "gq")
    gkv_t = A0.alloc([128, KVL], F32, "gkv")
    bfg_t = A0.alloc([128, H], F32, "bfg")
    cwb_t = A0.alloc([128, NCH, 4], F32, "cwb")
    wuq_t = A0.alloc([128, 3, H * MQK + 32], BF16, "wuq")
    wuqr_t = A0.alloc([128, 3, H * MQK + 32], BF16, "wuqr")
    wukvk_t = A0.alloc([128, 2, FOXW], BF16, "wukvk")
    wukvv_t = A0.alloc([128, 2, FOXW], BF16, "wukvv")
    cst_t = A0.alloc([128, 4], F32, "cst")
    b_const = Buf("const")
    PERS_END = A0.cur

    A1 = Arena(nc, PERS_END, SBUF_TOP, "mid")
    XT = A1.alloc([128, 8, L], BF16, "XT")
    OTF = A1.alloc([128, 4, L], BF16, "OTF")
    OTM = A1.alloc([128, 4, L], BF16, "OTM")
    PH_BASE = A1.cur
    OT_BASE = PH_BASE - 2 * (4 * L * 2)
    b_h2scr = Buf("h2scr")
    b_XT = {}
    b_OTF = {}
    b_OTM = {}

    def bXT(c0):
        return b_XT.setdefault(c0, Buf("XT%d" % c0))

    def bOTF(c0):
        return b_OTF.setdefault(c0, Buf("OTF%d" % c0))

    def bOTM(c0):
        return b_OTM.setdefault(c0, Buf("OTM%d" % c0))

    def dma(eng, out, in_, reads, writes):
        return P.op(eng, lambda en: en.dma_start(out=out, in_=in_), reads=reads, writes=writes, dma=True)

    def mm(out, lhsT, rhs, start, stop, reads, writes):
        return P.op("pe", lambda en: en.matmul(out, lhsT=lhsT, rhs=rhs, start=start, stop=stop),
                    reads=reads, writes=writes)

    def tr(out, in_, ident, reads, writes):
        return P.op("pe", lambda en: en.transpose(out, in_, ident), reads=reads, writes=writes)

    def act(out, in_, func, reads, writes, bias=None, scale=None, accum_out=None):
        kw = {}
        if bias is not None:
            kw["bias"] = bias
        if scale is not None:
            kw["scale"] = scale
        if accum_out is not None:
            kw["accum_out"] = accum_out
        return P.op("act", lambda en: en.activation(out=out, in_=in_, func=func, **kw), reads=reads, writes=writes)

    def vcopy(out, in_, reads, writes, eng="dve"):
        return P.op(eng, lambda en: en.tensor_copy(out=out, in_=in_), reads=reads, writes=writes)

    def vtt(out, in0, in1, op, reads, writes, eng="dve"):
        return P.op(eng, lambda en: en.tensor_tensor(out=out, in0=in0, in1=in1, op=op), reads=reads, writes=writes)

    def vts(out, in0, s1, s2, op0, op1, reads, writes, eng="dve"):
        if op1 is None:
            return P.op(eng, lambda en: en.tensor_scalar(out=out, in0=in0, scalar1=s1, scalar2=None, op0=op0),
                        reads=reads, writes=writes)
        return P.op(eng, lambda en: en.tensor_scalar(out=out, in0=in0, scalar1=s1, scalar2=s2, op0=op0, op1=op1),
                    reads=reads, writes=writes)

    def vstt(out, in0, scalar, in1, op0, op1, reads, writes):
        return P.op("dve", lambda en: en.scalar_tensor_tensor(out=out, in0=in0, scalar=scalar, in1=in1, op0=op0, op1=op1),
                    reads=reads, writes=writes)

    def vmemset(ap, val, writes, eng="dve"):
        return P.op(eng, lambda en: en.memset(ap, val), writes=writes)

    def vrecip(out, in_, reads, writes):
        return P.op("dve", lambda en: en.reciprocal(out=out, in_=in_), reads=reads, writes=writes)

    AS = Arena(nc, PH_BASE, SBUF_TOP, "setup")
    vmemset(cst_t[:, 0:1], EPS, [b_const])
    vmemset(cst_t[:, 1:2], 1.0, [b_const])
    vmemset(cst_t[:, 2:3], 0.0, [b_const])
    b_stage = Buf("setup_stage")
    for t, src in ((ident_f, c_ident), (utri_f, c_utri), (ones_f, c_ones), (gmix_t, g_mix), (gffn_t, g_ffn),
                   (gfin_t, g_fin), (gq_t, g_q), (gkv_t, g_kv), (bfg_t, b_fg)):
        dma("sp", t[:], src[:, :], [], [b_const])
    STAGE = int(os.environ.get("KSTAGE", "9"))
    for t, src in ((ident_b, c_ident), (mtri_b, c_mtri), (mchk_b, c_mchk)):
        if STAGE >= 2:
            dma("pool", t[:], src[:, :], [], [b_const])
    if STAGE >= 3:
        vmemset(wuq_t[:, :, H * MQK:H * MQK + 32], 0.0, [b_const])
    dma("pool", wuq_t[:, :, 0:H * MQK], w_uq.rearrange("(kc p) f -> p kc f", p=128), [], [b_const])
    for kc in range(2 if STAGE >= 4 else 0):
        src = w_ukv[kc * 128:(kc + 1) * 128, :].rearrange("p (h x) -> p h x", x=128)
        dma("pool", wukvk_t[:, kc, :].rearrange("p (h x) -> p h x", x=64), src[:, :, 0:64], [], [b_const])
        dma("pool", wukvv_t[:, kc, :].rearrange("p (h x) -> p h x", x=64), src[:, :, 64:128], [], [b_const])
    wq4 = wuq_t[:, :, 0:H * MQK].rearrange("p kc (h x) -> p kc h x", x=MQK)
    wr4 = wuqr_t[:, :, 0:H * MQK].rearrange("p kc (h x) -> p kc h x", x=MQK)
    vmemset(wuqr_t[:], 0.0, [b_const])
    for kc in range(3 if STAGE >= 5 else 0):
        P.op("dve", (lambda kc: lambda en: en.tensor_scalar(out=wr4[:, kc, :, 64:80], in0=wq4[:, kc, :, 80:96], scalar1=-1.0,
                                                             scalar2=None, op0=ALU.mult))(kc), reads=[b_const], writes=[b_const])
        vcopy(wr4[:, kc, :, 80:96], wq4[:, kc, :, 64:80], [b_const], [b_const])
    dma("sp", cwb_t[:], convwb[:, :, :], [], [b_const])
    P.fence()

    seqs = [make_prompt_seq(b) for b in range(n_prompt)] + [make_sample_seq(b) for b in range(n_sample)]
    if phases.startswith('S'):
        seqs = []

    wb_state = {"i": 0}

    for seq in seqs:
        isP = seq.kind == "p"
        b = seq.b
        if isP:
            out_fk, out_fv, out_lf, out_ckv, out_kr, out_cv, out_y = (o_fk_p[b], o_fv_p[b], o_lf_p[b], o_ckv_p[b],
                                                                     o_kr_p[b], o_cv_p[b], o_y_p[b])
        else:
            out_fk, out_fv, out_lf, out_ckv, out_kr, out_cv, out_y = (o_fk_s[b], o_fv_s[b], o_lf_s[b], o_ckv_s[b],
                                                                     o_kr_s[b], o_cv_s[b], o_y_s[b])
        nc0 = seq.ncache
        NT = seq.nt
        KT = seq.kt
        QB = seq.qb
        NKT = len(KT)
        NQB = len(QB)

        def src_rows(c0, n):
            if not isP:
                return x_s[b, c0 - nc0:c0 - nc0 + n, :]
            if c0 == 0:
                return meta[0:n, :]
            return x_p[b, c0 - NMETA:c0 - NMETA + n, :]

        def out_rows(ap, c0, n):
            return ap[c0 - nc0:c0 - nc0 + n, :]

        AB = Arena(nc, PH_BASE, SBUF_TOP, "ab%s%d" % (seq.kind, b))
        WB = [AB.alloc([128, 8, 512], BF16, "wb%d" % i) for i in range(3)]
        b_WB = [Buf("wb%d" % i) for i in range(3)]

        def load_w_group(col0, ncols):
            i = wb_state["i"] % 3
            wb_state["i"] += 1
            dma("pool", WB[i][:, :, 0:ncols], w_in.rearrange("(kc p) f -> p kc f", p=128)[:, :, col0:col0 + ncols],
                [], [b_WB[i]])
            return WB[i], b_WB[i]

        XIN_OFF = (AB.cur + 31) // 32 * 32
        xin = [AB.alloc([128, D], F32, "xin%d" % i) for i in range(2)]
        b_xin = [Buf("xin%d" % i) for i in range(2)]
        junk = AB.alloc([128, D], BF16, "junk")
        b_junk = Buf("junk")
        xnb = [AB.alloc([128, D], BF16, "xnb%d" % i) for i in range(2)]
        b_xnb = [Buf("xnb%d" % i) for i in range(2)]
        stat = [AB.alloc([128, 4], F32, "stat%d" % i) for i in range(2)]
        b_stat = [Buf("stat%d" % i) for i in range(2)]
        ost = [AB.alloc([128, 512], F32, "ost%d" % i) for i in range(2)]
        b_ost = [Buf("ost%d" % i) for i in range(2)]
        LOGF = AB.alloc([128, NKT, H], F32, "logf")
        b_LOGF = [Buf("logf%d" % i) for i in range(NKT)]
        FCUM = AB.alloc([128, NKT, H], F32, "fcum")
        b_FCUM = [Buf("fcum%d" % i) for i in range(NKT)]
        CREF = AB.alloc([128, NQB, H], F32, "cref")
        b_CREF = [Buf("cref%d" % j) for j in range(NQB)]
        BIAS = AB.alloc([128, NQB, NKT, H], F32, "bias")
        b_BIAS = [Buf("bias%d" % j) for j in range(NQB)]
        PT = [AB.alloc([128, 512], BF16, "pt%d" % i) for i in range(4)]
        b_PT = [Buf("pt%d" % i) for i in range(4)]
        OSB = [AB.alloc([128, 512], F32, "osb%d" % i) for i in range(2)]
        b_OSB = [Buf("osb%d" % i) for i in range(2)]
        ATT_BASE = AB.cur
        AF_ = Arena(nc, ATT_BASE, SBUF_TOP, "fox%s%d" % (seq.kind, b))
        FQT = AF_.alloc([128, 4, LK], BF16, "fqt")
        FKT = AF_.alloc([128, 4, LK], BF16, "fkt")
        VF = AF_.alloc([128, NKT, H, 65], BF16, "vf")
        b_FQT = [Buf("fqt%d" % j) for j in range(NQB)]
        b_FKT = [Buf("fkt%d" % i) for i in range(NKT)]
        b_VF = [Buf("vf%d" % i) for i in range(NKT)]
        b_VF1 = Buf("vf_ones")
        QPAD = nc.alloc_sbuf_tensor_at("qpad_%s%d" % (seq.kind, b), [128, 2, LK], BF16, offset=XIN_OFF)
        b_QPAD = Buf("qpad")
        VA = [AF_.alloc([128, NKT, 128], BF16, "va%d" % i) for i in range(2)]
        b_VA = [Buf("va%d" % i) for i in range(2)]

        for ti, (c0, n) in enumerate(NT):
            s = ti % 2
            dma("sp", xin[s][0:n, :], src_rows(c0, n), [], [b_xin[s]])
            vmemset(stat[s][0:n, 0:1], 0.0, [b_stat[s]])
            act(junk[0:n, :], xin[s][0:n, :], AF.Square, [b_xin[s]], [b_junk, b_stat[s]], accum_out=stat[s][0:n, 0:1])
            act(stat[s][0:n, 1:2], stat[s][0:n, 0:1], AF.Sqrt, [b_stat[s], b_const], [b_stat[s]],
                bias=cst_t[0:n, 0:1], scale=1.0 / D)
            vrecip(stat[s][0:n, 2:3], stat[s][0:n, 1:2], [b_stat[s]], [b_stat[s]])
            vstt(xnb[s][0:n, :], xin[s][0:n, :], stat[s][0:n, 2:3], gmix_t[0:n, :], ALU.mult, ALU.mult,
                 [b_xin[s], b_stat[s], b_const], [b_xnb[s]])
            for kc in range(8):
                tr(ps_t[:, kc * 128:kc * 128 + n], xnb[s][0:n, kc * 128:(kc + 1) * 128], ident_b[0:n, 0:n],
                   [b_xnb[s], b_const], [b_ps_t])
            vcopy(XT[:, :, c0:c0 + n], ps_t[:].rearrange("p (kc t) -> p kc t", t=128)[:, :, 0:n], [b_ps_t], [bXT(c0)])
        def tiles_in(lst, c0, n):
            return [i for i, (t0, tn) in enumerate(lst) if t0 < c0 + n and c0 < t0 + tn]

        def xt_bufs(c0, n):
            return [bXT(NT[i][0]) for i in tiles_in(NT, c0, n)]

        def pipelined(items, head, tail):
            prev = None
            for it in items:
                head(it)
                if prev is not None:
                    tail(prev)
                prev = it
            if prev is not None:
                tail(prev)

        ev = {"i": 0}

        def evac(out, in_, reads, writes):
            ev["i"] += 1
            if ev["i"] % 2:
                return act(out, in_, AF.Copy, reads, writes)
            return vcopy(out, in_, reads, writes)

        def tm_proj(Wt, bW, c0, n, wcol0, ncols):
            pm, bpm = next_mm()
            for kc in range(8):
                mm(pm[0:n, 0:ncols], XT[:, kc, c0:c0 + n], Wt[:, kc, wcol0:wcol0 + ncols], kc == 0, kc == 7,
                   [bXT(c0), bW], [bpm])
            return pm, bpm

        def fm_proj(Wt, bW, wcol0, m, qc0, nq):
            pm, bpm = next_mm()
            xb = xt_bufs(qc0, nq)
            for kc in range(8):
                mm(pm[0:m, 0:nq], Wt[:, kc, wcol0:wcol0 + m], XT[:, kc, qc0:qc0 + nq], kc == 0, kc == 7,
                   xb + [bW], [bpm])
            return pm, bpm

        KI0 = NKT - len(NT)

        vmemset(VF[:], 0.0, b_VF + [b_VF1])
        if not isP:
            for i in range(8):
                s = i % 2
                dma("pool", xnb[s][:, 0:512], c_fk[b, i * 128:(i + 1) * 128, :], [], [b_xnb[s]])
                for g in range(4):
                    tr(ps_t[:, g * 128:(g + 1) * 128], xnb[s][:, g * 128:(g + 1) * 128], ident_b[:, :],
                       [b_xnb[s], b_const], [b_ps_t])
                vcopy(FKT[:, :, i * 128:(i + 1) * 128], ps_t[:, 0:512].rearrange("p (g t) -> p g t", t=128),
                      [b_ps_t], [b_FKT[i]])
                dma("pool", VF[:, i, :, 0:64], c_fv[b, i * 128:(i + 1) * 128, :].rearrange("p (h x) -> p h x", x=64),
                    [], [b_VF[i]])
            dma("sp", LOGF[:, 0:8, :], c_lf[b].rearrange("(i p) h -> p i h", p=128), [], b_LOGF[0:8])

        mmset["banks"] = mm_wide7
        Wt, bW = load_w_group(OFF_FK, 512)
        Wv, bWv = load_w_group(OFF_FV, 512)
        Wq, bWq = load_w_group(0, 512)
        for ti, (c0, n) in enumerate(NT):
            s = ti % 2
            pm, bpm = tm_proj(Wt, bW, c0, n, 0, 512)
            evac(ost[s][0:n, :], pm[0:n, 0:512], [bpm], [b_ost[s]])
            dma("sp", out_rows(out_fk, c0, n), ost[s][0:n, :], [b_ost[s]], [])
        for j, (qc0, nq) in enumerate(QB):
            kts = tiles_in(KT, qc0, nq)
            for g in range(4):
                pm, bpm = fm_proj(Wt, bW, g * 128, 128, qc0, nq)
                evac(FKT[:, g, qc0:qc0 + nq], pm[:, 0:nq], [bpm], [b_FKT[i] for i in kts])
        for ti, (c0, n) in enumerate(NT):
            s = ti % 2
            ki = KI0 + ti
            pm, bpm = tm_proj(Wv, bWv, c0, n, 0, 512)
            act(ost[s][0:n, :], pm[0:n, 0:512], AF.Copy, [bpm], [b_ost[s]])
            vcopy(VF[0:n, ki, :, 0:64], pm[0:n, 0:512].rearrange("p (h x) -> p h x", x=64), [bpm], [b_VF[ki]])
            dma("sp", out_rows(out_fv, c0, n), ost[s][0:n, :], [b_ost[s]], [])
        for j, (qc0, nq) in enumerate(QB):
            for g in range(4):
                pm, bpm = fm_proj(Wq, bWq, g * 128, 128, qc0, nq)
                evac(FQT[:, g, qc0:qc0 + nq], pm[:, 0:nq], [bpm], [b_FQT[j]])
        Wf, bWf = load_w_group(OFF_FF, 8)
        for ti, (c0, n) in enumerate(NT):
            s = ti % 2
            ki = KI0 + ti
            pm, bpm = tm_proj(Wf, bWf, c0, n, 0, 8)
            z = ost[s]
            vtt(z[0:n, 0:8], pm[0:n, 0:8], bfg_t[0:n, :], ALU.add, [bpm, b_const], [b_ost[s]])
            act(z[0:n, 8:16], z[0:n, 0:8], AF.Exp, [b_ost[s]], [b_ost[s]], scale=-1.0)
            act(z[0:n, 16:24], z[0:n, 8:16], AF.Ln, [b_ost[s], b_const], [b_ost[s]], bias=cst_t[0:n, 1:2])
            vts(LOGF[0:n, ki, :], z[0:n, 16:24], -1.0, None, ALU.mult, None, [b_ost[s]], [b_LOGF[ki]])
            dma("sp", out_rows(out_lf, c0, n), LOGF[0:n, ki, :], [b_LOGF[ki]], [])
        pm, bpm = next_mm()
        vmemset(pm[:, 0:NKT * 8], 0.0, [bpm])
        for i, (kc0, nk) in enumerate(KT):
            for jj in range(i):
                nj = KT[jj][1]
                mm(pm[0:nk, i * 8:(i + 1) * 8], ones_f[0:nj, 0:nk], LOGF[0:nj, jj, :], jj == 0, False,
                   [b_LOGF[jj], b_const], [bpm])
            mm(pm[0:nk, i * 8:(i + 1) * 8], utri_f[0:nk, 0:nk], LOGF[0:nk, i, :], i == 0, True,
               [b_LOGF[i], b_const], [bpm])
        vcopy(FCUM[:].rearrange("p i h -> p (i h)"), pm[:, 0:NKT * 8], [bpm], b_FCUM)
        pm, bpm = next_mm()
        vmemset(pm[:, 0:NQB * 8], 0.0, [bpm])
        for j in range(NQB):
            if isP:
                upto = 0 if j == 0 else 4 * (j - 1) + 3
            else:
                upto = 8
            if upto == 0:
                continue
            for jj in range(upto):
                nj = KT[jj][1]
                mm(pm[:, j * 8:(j + 1) * 8], ones_f[0:nj, :], LOGF[0:nj, jj, :], jj == 0, jj == upto - 1,
                   [b_LOGF[jj], b_const], [bpm])
        vcopy(CREF[:].rearrange("p j h -> p (j h)"), pm[:, 0:NQB * 8], [bpm], b_CREF)
        for j in range(NQB):
            zero_ref = isP and j == 0
            for i, (kc0, nk) in enumerate(KT):
                if vis(seq, j, i, "fox") is None:
                    continue
                if zero_ref:
                    vts(BIAS[0:nk, j, i, :], FCUM[0:nk, i, :], -1.0, None, ALU.mult, None, [b_FCUM[i]], [b_BIAS[j]])
                else:
                    vtt(BIAS[0:nk, j, i, :], CREF[0:nk, j, :], FCUM[0:nk, i, :], ALU.subtract,
                        [b_CREF[j], b_FCUM[i]], [b_BIAS[j]])

        mmset["banks"] = mm_all
        ptc = {"i": 0, "o": 0}

        def attention(kind, kdim, Kap, bK, Qap, bQ, Vap, bV, OT, bOT, scale, heads, pre_head=None):
            groups = []
            for h in heads:
                for j in range(NQB):
                    vl = [(i, vis(seq, j, i, kind)) for i in range(NKT)]
                    vl = [(i, v) for i, v in vl if v is not None]
                    groups.append((h, j, vl))
            pairs = []
            for gi, (h, j, vl) in enumerate(groups):
                for idx, (i, v) in enumerate(vl):
                    pairs.append((gi, h, j, idx, i, v, idx == len(vl) - 1))
            sinfo = {}
            ginfo = {}

            started = set()

            def emit_S(p):
                gi, h, j, idx, i, v, last = pairs[p]
                if pre_head is not None and h not in started:
                    started.add(h)
                    pre_head(h)
                qc0, nq = QB[j]
                kc0, nk = KT[i]
                c0 = v[1]
                w = nq - c0
                ps, bps = next_s()
                mm(ps[0:nk, 0:w], Kap(i, h), Qap(j, h, c0), True, True, [bK(i, h), bQ(j, h)], [bps])
                sinfo[p] = (ps, bps)

            def emit_rest(p):
                gi, h, j, idx, i, v, last = pairs[p]
                qc0, nq = QB[j]
                kc0, nk = KT[i]
                c0 = v[1]
                w = nq - c0
                ps, bps = sinfo.pop(p)
                if idx == 0:
                    ginfo[gi] = next_acc()
                acc, bacc = ginfo[gi]
                k = ptc["i"] % 4
                ptc["i"] += 1
                if kind == "fox":
                    act(PT[k][0:nk, 0:w], ps[0:nk, 0:w], AF.Exp, [bps, b_BIAS[j]], [b_PT[k]],
                        bias=BIAS[0:nk, j, i, h:h + 1], scale=scale)
                else:
                    act(PT[k][0:nk, 0:w], ps[0:nk, 0:w], AF.Exp, [bps, b_const], [b_PT[k]],
                        bias=cst_t[0:nk, 2:3], scale=scale)
                if v[0] == "diag":
                    mw = min(128, w)
                    mt = mtri_b if v[2] == "tri" else mchk_b
                    vtt(PT[k][0:nk, 0:mw], PT[k][0:nk, 0:mw], mt[0:nk, 0:mw], ALU.mult,
                        [b_PT[k], b_const], [b_PT[k]])
                mm(acc[0:128, c0:nq], Vap(i, h), PT[k][0:nk, 0:w], idx == 0, last,
                   [bV(i, h), b_PT[k]], [bacc])

            def emit_norm(gi):
                h, j, vl = groups[gi]
                g, sl = h // 2, h % 2
                qc0, nq = QB[j]
                acc, bacc = ginfo.pop(gi)
                o = ptc["o"] % 2
                ptc["o"] += 1
                vrecip(OSB[o][0:64, 0:nq], acc[64:128, 0:nq], [bacc], [b_OSB[o]])
                vtt(OT[sl * 64:(sl + 1) * 64, g, qc0:qc0 + nq], acc[0:64, 0:nq], OSB[o][0:64, 0:nq], ALU.mult,
                    [bacc, b_OSB[o]], [bOT(qc0)])

            pending = None
            emit_S(0)
            for p in range(len(pairs)):
                if p + 1 < len(pairs):
                    emit_S(p + 1)
                emit_rest(p)
                gi, h, j, idx, i, v, last = pairs[p]
                if pending is not None and (idx >= 3 or last):
                    emit_norm(pending)
                    pending = None
                if last:
                    pending = gi
            if pending is not None:
                emit_norm(pending)

        if "B" in phases:
            vmemset(QPAD[:], 0.0, [b_QPAD, b_xin[0], b_xin[1], b_junk])
            for i_ in range(2):
                vmemset(VA[i_][:, :, 64:128], 1.0, [b_VA[i_]])

            def fox_pre_head(h):
                g, sl = h // 2, h % 2
                qlo, qhi = QB[0][0], QB[-1][0] + QB[-1][1]
                vcopy(QPAD[sl * 64:(sl + 1) * 64, sl, qlo:qhi], FQT[sl * 64:(sl + 1) * 64, g, qlo:qhi], b_FQT, [b_QPAD])
                vcopy(VA[h % 2][:, :, 0:64], VF[:, :, h, 0:64], b_VF, [b_VA[h % 2]])

            attention(
                "fox", 128,
                lambda i, h: FKT[:, h // 2, KT[i][0]:KT[i][0] + KT[i][1]],
                lambda i, h: b_FKT[i],
                lambda j, h, c0: QPAD[:, h % 2, QB[j][0] + c0:QB[j][0] + QB[j][1]],
                lambda j, h: b_QPAD,
                lambda i, h: VA[h % 2][0:KT[i][1], i, :],
                lambda i, h: b_VA[h % 2],
                OTF, bOTF, FOX_SCALE, range(H), pre_head=fox_pre_head)
        P.fence()
        if "M" in phases:
            mmset["banks"] = mm_wide7
            AM = Arena(nc, ATT_BASE, SBUF_TOP, "mla%s%d" % (seq.kind, b))
            CQT = AM.alloc([128, 3, LK], BF16, "cqt")
            CKVT = AM.alloc([128, 2, LK], BF16, "ckvt")
            KRT = AM.alloc([128, LK], BF16, "krt")
            QP = AM.alloc([128, 2, LK], BF16, "qp")
            KP = AM.alloc([128, 2, LK], BF16, "kp")
            VM = AM.alloc([128, NKT, 2, 128], BF16, "vm")
            CSF = AM.alloc([128, 2, 512], F32, "csf")
            RT = AM.alloc([128, 2, 512], F32, "rt")
            CST = [AM.alloc([128, 64], F32, "cst%d" % i) for i in range(2)]
            b_CQT = [Buf("cqt%d" % j) for j in range(NQB)]
            b_CKVT = [Buf("ckvt%d" % i) for i in range(NKT)]
            b_KRT = [Buf("krt%d" % i) for i in range(NKT)]
            b_QP = [Buf("qp%d" % j) for j in range(NQB)]
            b_KP = [Buf("kp%d" % i) for i in range(NKT)]
            b_VM = [Buf("vm%d" % i) for i in range(NKT)]
            b_CSF = Buf("csf")
            b_RT = Buf("rt")
            b_CST = [Buf("cst%d" % i) for i in range(2)]
            vmemset(VM[:, :, :, 64:128], 1.0, b_VM)
            vmemset(KP[96:128, :, :], 0.0, b_KP)
            vmemset(QP[96:128, :, :], 0.0, b_QP)

            def qb_of(c0):
                return [j for j, (q0, qn) in enumerate(QB) if q0 <= c0 < q0 + qn][0]

            if not isP:
                for i in range(8):
                    s = i % 2
                    dma("pool", xnb[s][:, 0:256], c_ckv[b, i * 128:(i + 1) * 128, :], [], [b_xnb[s]])
                    dma("pool", xnb[s][:, 256:288], c_kr[b, i * 128:(i + 1) * 128, :], [], [b_xnb[s]])
                    for kc in range(2):
                        tr(ps_t[:, kc * 128:(kc + 1) * 128], xnb[s][:, kc * 128:(kc + 1) * 128], ident_b[:, :],
                           [b_xnb[s], b_const], [b_ps_t])
                    tr(ps_t[0:32, 256:384], xnb[s][:, 256:288], ident_b[:, :], [b_xnb[s], b_const], [b_ps_t])
                    vcopy(CKVT[:, :, i * 128:(i + 1) * 128], ps_t[:, 0:256].rearrange("p (g t) -> p g t", t=128),
                          [b_ps_t], [b_CKVT[i]])
                    vcopy(KRT[64:96, i * 128:(i + 1) * 128], ps_t[0:32, 256:384], [b_ps_t], [b_KRT[i]])
            Wc, bWc = load_w_group(OFF_CQ, QL)
            Wk, bWk = load_w_group(OFF_CKV, KVL + ROPE)
            def cq_head(ti):
                c0, n = NT[ti]
                s = ti % 2
                pm, bpm = tm_proj(Wc, bWc, c0, n, 0, QL)
                vmemset(stat[s][0:n, 0:1], 0.0, [b_stat[s]])
                act(junk[0:n, 0:QL], pm[0:n, 0:QL], AF.Square, [bpm], [b_junk, b_stat[s]], accum_out=stat[s][0:n, 0:1])
                act(stat[s][0:n, 1:2], stat[s][0:n, 0:1], AF.Sqrt, [b_stat[s], b_const], [b_stat[s]],
                    bias=cst_t[0:n, 0:1], scale=1.0 / QL)
                vrecip(stat[s][0:n, 2:3], stat[s][0:n, 1:2], [b_stat[s]], [b_stat[s]])
                vstt(xnb[s][0:n, 0:QL], pm[0:n, 0:QL], stat[s][0:n, 2:3], gq_t[0:n, :], ALU.mult, ALU.mult,
                     [bpm, b_stat[s], b_const], [b_xnb[s]])

            def cq_tail(ti):
                c0, n = NT[ti]
                s = ti % 2
                for kc in range(3):
                    tr(ps_t[:, kc * 128:kc * 128 + n], xnb[s][0:n, kc * 128:(kc + 1) * 128], ident_b[0:n, 0:n],
                       [b_xnb[s], b_const], [b_ps_t])
                vcopy(CQT[:, :, c0:c0 + n], ps_t[:, 0:384].rearrange("p (kc t) -> p kc t", t=128)[:, :, 0:n],
                      [b_ps_t], [b_CQT[qb_of(c0)]])

            pipelined(range(len(NT)), cq_head, cq_tail)
            def kv_head(ti):
                c0, n = NT[ti]
                s = ti % 2
                ki = KI0 + ti
                dma("sp", CST[s][0:n, :], c_cs_tm[c0:c0 + n, :], [], [b_CST[s]])
                pm, bpm = tm_proj(Wk, bWk, c0, n, 0, KVL + ROPE)
                vmemset(stat[s][0:n, 0:1], 0.0, [b_stat[s]])
                act(junk[0:n, 0:KVL], pm[0:n, 0:KVL], AF.Square, [bpm], [b_junk, b_stat[s]], accum_out=stat[s][0:n, 0:1])
                act(stat[s][0:n, 1:2], stat[s][0:n, 0:1], AF.Sqrt, [b_stat[s], b_const], [b_stat[s]],
                    bias=cst_t[0:n, 0:1], scale=1.0 / KVL)
                vrecip(stat[s][0:n, 2:3], stat[s][0:n, 1:2], [b_stat[s]], [b_stat[s]])
                o = ost[s]
                vstt(o[0:n, 0:KVL], pm[0:n, 0:KVL], stat[s][0:n, 2:3], gkv_t[0:n, :], ALU.mult, ALU.mult,
                     [bpm, b_stat[s], b_const], [b_ost[s]])
                vtt(o[0:n, 256:288], pm[0:n, 256:288], CST[s][0:n, 0:32], ALU.mult, [bpm, b_CST[s]], [b_ost[s]])
                vtt(o[0:n, 288:304], pm[0:n, 272:288], CST[s][0:n, 32:48], ALU.mult, [bpm, b_CST[s]], [b_ost[s]])
                vtt(o[0:n, 304:320], pm[0:n, 256:272], CST[s][0:n, 48:64], ALU.mult, [bpm, b_CST[s]], [b_ost[s]])
                vtt(o[0:n, 256:288], o[0:n, 256:288], o[0:n, 288:320], ALU.add, [b_ost[s]], [b_ost[s]])
                dma("sp", out_rows(out_ckv, c0, n), o[0:n, 0:KVL], [b_ost[s]], [])
                dma("sp", out_rows(out_kr, c0, n), o[0:n, 256:288], [b_ost[s]], [])
                vcopy(xnb[s][0:n, 0:288], o[0:n, 0:288], [b_ost[s]], [b_xnb[s]])

            def kv_tail(ti):
                c0, n = NT[ti]
                s = ti % 2
                ki = KI0 + ti
                for kc in range(2):
                    tr(ps_t[:, kc * 128:kc * 128 + n], xnb[s][0:n, kc * 128:(kc + 1) * 128], ident_b[0:n, 0:n],
                       [b_xnb[s], b_const], [b_ps_t])
                tr(ps_t[0:32, 256:256 + n], xnb[s][0:n, 256:288], ident_b[0:n, 0:n], [b_xnb[s], b_const], [b_ps_t])
                vcopy(CKVT[:, :, c0:c0 + n], ps_t[:, 0:256].rearrange("p (g t) -> p g t", t=128)[:, :, 0:n],
                      [b_ps_t], [b_CKVT[ki]])
                vcopy(KRT[64:96, c0:c0 + n], ps_t[0:32, 256:256 + n], [b_ps_t], [b_KRT[ki]])

            pipelined(range(len(NT)), kv_head, kv_tail)
            if isP:
                KB = list(QB)
            else:
                KB = [(0, 512), (512, 512), (PAST, DSEQ)]
            mmset["banks"] = mm_all
            for g in range(4):
                for (k0, kw) in KB:
                    kts = tiles_in(KT, k0, kw)
                    pm, bpm = next_mm()
                    for kc in range(2):
                        mm(pm[0:128, 0:kw], wukvk_t[:, kc, g * 128:(g + 1) * 128], CKVT[:, kc, k0:k0 + kw], kc == 0, kc == 1,
                           [b_CKVT[i] for i in kts] + [b_const], [bpm])
                    evac(KP[0:64, 0, k0:k0 + kw], pm[0:64, 0:kw], [bpm], [b_KP[i] for i in kts])
                    evac(KP[0:64, 1, k0:k0 + kw], pm[64:128, 0:kw], [bpm], [b_KP[i] for i in kts])
                    for sl in range(2):
                        vcopy(KP[64:96, sl, k0:k0 + kw], KRT[64:96, k0:k0 + kw], [b_KRT[i] for i in kts],
                              [b_KP[i] for i in kts])
                for i, (k0, nk) in enumerate(KT):
                    pm, bpm = next_mm()
                    for kc in range(2):
                        mm(pm[0:nk, 0:128], CKVT[:, kc, k0:k0 + nk], wukvv_t[:, kc, g * 128:(g + 1) * 128], kc == 0, kc == 1,
                           [b_CKVT[i], b_const], [bpm])
                    evac(VM[0:nk, i, :, 0:64], pm[0:nk, 0:128].rearrange("p (s x) -> p s x", x=64), [bpm], [b_VM[i]])
                for j, (qc0, nq) in enumerate(QB):
                    dma("sp", CSF[64:96, 0, 0:nq], c_cs_fm[0, :, qc0:qc0 + nq], [], [b_CSF])
                    dma("sp", CSF[64:96, 1, 0:nq], c_cs_fm[1, :, qc0:qc0 + nq], [], [b_CSF])
                    for sl in range(2):
                        h = 2 * g + sl
                        pm1, bpm1 = next_mm()
                        for kc in range(3):
                            mm(pm1[0:128, 0:nq], wuq_t[:, kc, h * 96:h * 96 + 128], CQT[:, kc, qc0:qc0 + nq], kc == 0, kc == 2,
                               [b_CQT[j], b_const], [bpm1])
                        pm2, bpm2 = next_mm()
                        for kc in range(3):
                            mm(pm2[0:128, 0:nq], wuqr_t[:, kc, h * 96:h * 96 + 128], CQT[:, kc, qc0:qc0 + nq], kc == 0, kc == 2,
                               [b_CQT[j], b_const], [bpm2])
                        act(QP[0:64, sl, qc0:qc0 + nq], pm1[0:64, 0:nq], AF.Copy, [bpm1], [b_QP[j]])
                        vtt(RT[64:96, 0, 0:nq], pm1[64:96, 0:nq], CSF[64:96, 0, 0:nq], ALU.mult, [bpm1, b_CSF], [b_RT])
                        vtt(RT[64:96, 1, 0:nq], pm2[64:96, 0:nq], CSF[64:96, 1, 0:nq], ALU.mult, [bpm2, b_CSF], [b_RT])
                        vtt(QP[64:96, sl, qc0:qc0 + nq], RT[64:96, 0, 0:nq], RT[64:96, 1, 0:nq], ALU.add, [b_RT], [b_QP[j]])
                attention(
                    "mla", 96,
                    lambda i, h: KP[:, h % 2, KT[i][0]:KT[i][0] + KT[i][1]],
                    lambda i, h: b_KP[i],
                    lambda j, h, c0: QP[:, h % 2, QB[j][0] + c0:QB[j][0] + QB[j][1]],
                    lambda j, h: b_QP[j],
                    lambda i, h: VM[0:KT[i][1], i, h % 2, :],
                    lambda i, h: b_VM[i],
                    OTM, bOTM, MLA_SCALE, [2 * g, 2 * g + 1])
            P.fence()
        if "C" in phases:
            mmset["banks"] = mm_wide7
            AC = Arena(nc, PH_BASE, SBUF_TOP, "c%s%d" % (seq.kind, b))
            WOF = AC.alloc([128, 4, D], BF16, "wof")
            WOM = AC.alloc([128, 4, D], BF16, "wom")
            WG = AC.alloc([128, 8, 2 * D], BF16, "wg")
            WO = AC.alloc([128, 8, D], BF16, "wo")
            b_WC = Buf("wc")
            MT = AC.alloc([128, 8, 512], BF16, "mt")
            b_MT = Buf("mt")
            G0 = [AC.alloc([128, 512], F32, "g0%d" % i) for i in range(2)]
            G1 = [AC.alloc([128, 512], F32, "g1%d" % i) for i in range(2)]
            M0 = [AC.alloc([128, 512], F32, "m0%d" % i) for i in range(2)]
            b_G0 = [Buf("g0%d" % i) for i in range(2)]
            b_G1 = [Buf("g1%d" % i) for i in range(2)]
            b_M0 = [Buf("m0%d" % i) for i in range(2)]
            cxin = [AC.alloc([128, D], F32, "cxin%d" % i) for i in range(2)]
            b_cxin = [Buf("cxin%d" % i) for i in range(2)]
            cxnb = [AC.alloc([128, D], BF16, "cxnb%d" % i) for i in range(2)]
            b_cxnb = [Buf("cxnb%d" % i) for i in range(2)]
            cjunk = AC.alloc([128, D], BF16, "cjunk")
            b_cjunk = Buf("cjunk")
            cstat = [AC.alloc([128, 4], F32, "cstat%d" % i) for i in range(2)]
            b_cstat = [Buf("cstat%d" % i) for i in range(2)]
            b_WOF = Buf("wof")
            b_WOM = Buf("wom")
            b_WG = [Buf("wg%d" % i) for i in range(4)]
            b_WO = [Buf("wo%d" % i) for i in range(2)]
            w_in_v = w_in.rearrange("(kc p) f -> p kc f", p=128)

            def ld_wg(hh):
                dma("pool", WG[:, :, hh * 512:(hh + 1) * 512],
                    w_in_v[:, :, OFF_GATE + hh * 512:OFF_GATE + (hh + 1) * 512], [], [b_WG[hh]])

            dma("pool", WOF[:], w_ofox.rearrange("(kc p) f -> p kc f", p=128), [], [b_WOF])
            ld_wg(0)
            dma("pool", WOM[:], w_omla.rearrange("(kc p) f -> p kc f", p=128), [], [b_WOM])
            ld_wg(2)
            ld_wg(1)
            ld_wg(3)
            for hh in range(2):
                dma("pool", WO[:, :, hh * 512:(hh + 1) * 512],
                    w_out.rearrange("(kc p) f -> p kc f", p=128)[:, :, hh * 512:(hh + 1) * 512], [], [b_WO[hh]])
            tcount = 0
            for j, (qc0, nq) in enumerate(QB):
                xb = xt_bufs(qc0, nq)
                for m in range(8):
                    s = m % 2
                    pa, bpa = next_mm()
                    for kc in range(4):
                        mm(pa[:, 0:nq], WOF[:, kc, m * 128:(m + 1) * 128], OTF[:, kc, qc0:qc0 + nq], kc == 0, kc == 3,
                           [b_WOF, bOTF(qc0)], [bpa])
                    pg, bpg = next_mm()
                    for kc in range(8):
                        mm(pg[:, 0:nq], WG[:, kc, m * 128:(m + 1) * 128], XT[:, kc, qc0:qc0 + nq], kc == 0, kc == 7,
                           [b_WG[m // 4]] + xb, [bpg])
                    act(G0[s][:, 0:nq], pg[:, 0:nq], AF.Sigmoid, [bpg], [b_G0[s]])
                    vtt(M0[s][:, 0:nq], G0[s][:, 0:nq], pa[:, 0:nq], ALU.mult, [b_G0[s], bpa], [b_M0[s]])
                    pb, bpb = next_mm()
                    for kc in range(4):
                        mm(pb[:, 0:nq], WOM[:, kc, m * 128:(m + 1) * 128], OTM[:, kc, qc0:qc0 + nq], kc == 0, kc == 3,
                           [b_WOM, bOTM(qc0)], [bpb])
                    pg2, bpg2 = next_mm()
                    for kc in range(8):
                        mm(pg2[:, 0:nq], WG[:, kc, D + m * 128:D + (m + 1) * 128], XT[:, kc, qc0:qc0 + nq], kc == 0, kc == 7,
                           [b_WG[2 + m // 4]] + xb, [bpg2])
                    act(G1[s][:, 0:nq], pg2[:, 0:nq], AF.Sigmoid, [bpg2], [b_G1[s]])
                    vtt(G1[s][:, 0:nq], G1[s][:, 0:nq], pb[:, 0:nq], ALU.mult, [b_G1[s], bpb], [b_G1[s]])
                    vtt(MT[:, m, 0:nq], M0[s][:, 0:nq], G1[s][:, 0:nq], ALU.add, [b_M0[s], b_G1[s]], [b_MT])
                def c_head(arg):
                    ti, s = arg
                    c0, n = NT[ti]
                    o = c0 - qc0
                    dma("sp", cxin[s][0:n, :], src_rows(c0, n), [], [b_cxin[s]])
                    for hw in range(2):
                        pm, bpm = next_mm()
                        for kc in range(8):
                            mm(pm[0:n, :], MT[:, kc, o:o + n], WO[:, kc, hw * 512:(hw + 1) * 512], kc == 0, kc == 7,
                               [b_MT, b_WO[hw]], [bpm])
                        vtt(cxin[s][0:n, hw * 512:(hw + 1) * 512], cxin[s][0:n, hw * 512:(hw + 1) * 512], pm[0:n, :], ALU.add,
                            [b_cxin[s], bpm], [b_cxin[s]])
                    dma("sp", h2_scr[c0 - nc0:c0 - nc0 + n, :], cxin[s][0:n, :], [b_cxin[s]], [b_h2scr])
                    vmemset(cstat[s][0:n, 0:1], 0.0, [b_cstat[s]])
                    act(cjunk[0:n, :], cxin[s][0:n, :], AF.Square, [b_cxin[s]], [b_cjunk, b_cstat[s]],
                        accum_out=cstat[s][0:n, 0:1])
                    act(cstat[s][0:n, 1:2], cstat[s][0:n, 0:1], AF.Sqrt, [b_cstat[s], b_const], [b_cstat[s]],
                        bias=cst_t[0:n, 0:1], scale=1.0 / D)
                    vrecip(cstat[s][0:n, 2:3], cstat[s][0:n, 1:2], [b_cstat[s]], [b_cstat[s]])
                    vstt(cxnb[s][0:n, :], cxin[s][0:n, :], cstat[s][0:n, 2:3], gffn_t[0:n, :], ALU.mult, ALU.mult,
                         [b_cxin[s], b_cstat[s], b_const], [b_cxnb[s]])

                def c_tail(arg):
                    ti, s = arg
                    c0, n = NT[ti]
                    for kc in range(8):
                        tr(ps_t[:, kc * 128:kc * 128 + n], cxnb[s][0:n, kc * 128:(kc + 1) * 128], ident_b[0:n, 0:n],
                           [b_cxnb[s], b_const], [b_ps_t])
                    vcopy(XT[:, :, c0:c0 + n], ps_t[:].rearrange("p (kc t) -> p kc t", t=128)[:, :, 0:n], [b_ps_t], [bXT(c0)])

                targs = []
                for ti in tiles_in(NT, qc0, nq):
                    targs.append((ti, tcount % 2))
                    tcount += 1
                pipelined(targs, c_head, c_tail)
            P.fence()
        if "D" in phases:
            mmset["banks"] = mm_wide5
            AD = Arena(nc, OT_BASE, SBUF_TOP, "d%s%d" % (seq.kind, b))
            if isP:
                halves = [[0, 1, 2], [3, 4]]
            else:
                halves = [[0]]
            HW_MAX = max(sum(QB[j][1] for j in blocks) for blocks in halves)
            WD = AD.alloc([128, NGC, D], BF16, "wd")
            b_WD = Buf("wd")
            AT = AD.alloc([128, NGC, HW_MAX], BF16, "at")
            b_AT = Buf("at")
            UFG = [AD.alloc([128, 2 + HW_MAX], BF16, "ufg%d" % i) for i in range(2)]
            UFV = [AD.alloc([128, 2 + HW_MAX], BF16, "ufv%d" % i) for i in range(2)]
            b_UFG = [Buf("ufg%d" % i) for i in range(2)]
            b_UFV = [Buf("ufv%d" % i) for i in range(2)]
            WUP = [AD.alloc([128, 8, 256], BF16, "wup%d" % i) for i in range(3)]
            b_WUP = [Buf("wup%d" % i) for i in range(3)]
            DG = [AD.alloc([128, 6, 128], BF16, "dg%d" % i) for i in range(2)]
            b_DG = [Buf("dg%d" % i) for i in range(2)]
            SG = [AD.alloc([128, 512], F32, "sg%d" % i) for i in range(2)]
            b_SG = [Buf("sg%d" % i) for i in range(2)]
            ULAST = AD.alloc([128, NCH, 2], BF16, "ulast")
            b_UL = Buf("ulast")
            CSO = [AD.alloc([2, 256], F32, "cso%d" % i) for i in range(2)]
            b_CSO = [Buf("cso%d" % i) for i in range(2)]
            dh2 = [AD.alloc([128, D], F32, "dh2%d" % i) for i in range(2)]
            b_dh2 = [Buf("dh2%d" % i) for i in range(2)]
            dy = [AD.alloc([128, D], F32, "dy%d" % i) for i in range(2)]
            b_dy = [Buf("dy%d" % i) for i in range(2)]
            djunk = AD.alloc([128, D], BF16, "djunk")
            b_djunk = Buf("djunk")
            dstat = [AD.alloc([128, 4], F32, "dstat%d" % i) for i in range(2)]
            b_dstat = [Buf("dstat%d" % i) for i in range(2)]
            if isP:
                vmemset(ULAST[:], 0.0, [b_UL])
            else:
                ccv = AD.alloc([2, UPW], BF16, "ccv")
                b_ccv = Buf("ccv")
                dma("pool", ccv[:], c_conv[b], [], [b_ccv])
                for c in range(NCH):
                    tr(ps_t[:, c * 2:c * 2 + 2], ccv[0:2, c * 128:(c + 1) * 128], ident_b[0:2, 0:2], [b_ccv, b_const], [b_ps_t])
                vcopy(ULAST[:].rearrange("p c j -> p (c j)"), ps_t[:, 0:NCH * 2], [b_ps_t], [b_UL])
            wupc = 0
            tcount = 0
            for hi, blocks in enumerate(halves):
                hc0 = QB[blocks[0]][0]
                hw_ = sum(QB[j][1] for j in blocks)
                last_half = hi == len(halves) - 1
                for c in range(NGC):
                    ws = wupc % 3
                    us = wupc % 2
                    wupc += 1
                    wv = w_up.rearrange("(kc p) f -> p kc f", p=128)
                    dma("pool", WUP[ws][:, :, 0:128], wv[:, :, c * 128:(c + 1) * 128], [], [b_WUP[ws]])
                    dma("pool", WUP[ws][:, :, 128:256], wv[:, :, DFF + c * 128:DFF + (c + 1) * 128], [], [b_WUP[ws]])
                    if hi == 0 and c == 2:
                        for hh in range(2):
                            dma("pool", WD[:, :, hh * 512:(hh + 1) * 512],
                                w_down.rearrange("(c p) f -> p c f", p=128)[:, :, hh * 512:(hh + 1) * 512], [], [b_WD])
                    def d_prep(cc, uu):
                        for t in range(3):
                            vts(DG[uu][:, t, :], ident_b[:, :], cwb_t[:, cc, t:t + 1], None, ALU.mult, None, [b_const], [b_DG[uu]])
                            vts(DG[uu][:, 3 + t, :], ident_b[:, :], cwb_t[:, NGC + cc, t:t + 1], None, ALU.mult, None,
                                [b_const], [b_DG[uu]])
                        vcopy(UFG[uu][:, 0:2], ULAST[:, cc, :], [b_UL], [b_UFG[uu]])
                        vcopy(UFV[uu][:, 0:2], ULAST[:, NGC + cc, :], [b_UL], [b_UFV[uu]])

                    if c == 0:
                        d_prep(0, us)
                    for bi, j in enumerate(blocks):
                        if bi == 1 or (bi == 0 and len(blocks) == 1):
                            if c + 1 < NGC:
                                d_prep(c + 1, (us + 1) % 2)
                        qc0, nq = QB[j]
                        o = qc0 - hc0
                        xb = xt_bufs(qc0, nq)
                        pg, bpg = next_mm()
                        for kc in range(8):
                            mm(pg[:, 0:nq], WUP[ws][:, kc, 0:128], XT[:, kc, qc0:qc0 + nq], kc == 0, kc == 7,
                               [b_WUP[ws]] + xb, [bpg])
                        act(UFG[us][:, 2 + o:2 + o + nq], pg[:, 0:nq], AF.Copy, [bpg], [b_UFG[us]])
                        pv, bpv = next_mm()
                        for kc in range(8):
                            mm(pv[:, 0:nq], WUP[ws][:, kc, 128:256], XT[:, kc, qc0:qc0 + nq], kc == 0, kc == 7,
                               [b_WUP[ws]] + xb, [bpv])
                        vcopy(UFV[us][:, 2 + o:2 + o + nq], pv[:, 0:nq], [bpv], [b_UFV[us]])
                        pcg, bpcg = next_s()
                        for t in range(3):
                            mm(pcg[:, 0:nq], DG[us][:, t, :], UFG[us][:, o + t:o + t + nq], t == 0, t == 2,
                               [b_DG[us], b_UFG[us]], [bpcg])
                        pcv, bpcv = next_s()
                        for t in range(3):
                            mm(pcv[:, 0:nq], DG[us][:, 3 + t, :], UFV[us][:, o + t:o + t + nq], t == 0, t == 2,
                               [b_DG[us], b_UFV[us]], [bpcv])
                        act(SG[us][:, 0:nq], pcg[:, 0:nq], AF.Silu, [bpcg, b_const], [b_SG[us]], bias=cwb_t[:, c, 3:4])
                        vstt(AT[:, c, o:o + nq], pcv[:, 0:nq], cwb_t[:, NGC + c, 3:4], SG[us][:, 0:nq], ALU.add, ALU.mult,
                             [bpcv, b_const, b_SG[us]], [b_AT])
                    if not last_half:
                        vcopy(ULAST[:, c, :], UFG[us][:, hw_:hw_ + 2], [b_UFG[us]], [b_UL])
                        vcopy(ULAST[:, NGC + c, :], UFV[us][:, hw_:hw_ + 2], [b_UFV[us]], [b_UL])
                    else:
                        lc = seq.lk - 2
                        pm, bpm = next_mm()
                        for kc in range(8):
                            mm(pm[0:2, 0:256], XT[:, kc, lc:lc + 2], WUP[ws][:, kc, :], kc == 0, kc == 7,
                               [b_WUP[ws]] + xt_bufs(lc, 2), [bpm])
                        vcopy(CSO[us][0:2, 0:256], pm[0:2, 0:256], [bpm], [b_CSO[us]])
                        dma("sp", out_cv[:, c * 128:(c + 1) * 128], CSO[us][0:2, 0:128], [b_CSO[us]], [])
                        dma("sp", out_cv[:, DFF + c * 128:DFF + (c + 1) * 128], CSO[us][0:2, 128:256], [b_CSO[us]], [])
                for ti in tiles_in(NT, hc0, hw_):
                    c0, n = NT[ti]
                    s = tcount % 2
                    tcount += 1
                    o = c0 - hc0
                    dma("sp", dh2[s][0:n, :], h2_scr[c0 - nc0:c0 - nc0 + n, :], [b_h2scr], [b_dh2[s]])
                    for hw in range(2):
                        pm, bpm = next_mm()
                        for c in range(NGC):
                            mm(pm[0:n, :], AT[:, c, o:o + n], WD[:, c, hw * 512:(hw + 1) * 512], c == 0, c == NGC - 1,
                               [b_AT, b_WD], [bpm])
                        vtt(dh2[s][0:n, hw * 512:(hw + 1) * 512], dh2[s][0:n, hw * 512:(hw + 1) * 512], pm[0:n, :], ALU.add,
                            [b_dh2[s], bpm], [b_dh2[s]])
                    vmemset(dstat[s][0:n, 0:1], 0.0, [b_dstat[s]])
                    act(djunk[0:n, :], dh2[s][0:n, :], AF.Square, [b_dh2[s]], [b_djunk, b_dstat[s]],
                        accum_out=dstat[s][0:n, 0:1])
                    act(dstat[s][0:n, 1:2], dstat[s][0:n, 0:1], AF.Sqrt, [b_dstat[s], b_const], [b_dstat[s]],
                        bias=cst_t[0:n, 0:1], scale=1.0 / D)
                    vrecip(dstat[s][0:n, 2:3], dstat[s][0:n, 1:2], [b_dstat[s]], [b_dstat[s]])
                    vstt(dy[s][0:n, :], dh2[s][0:n, :], dstat[s][0:n, 2:3], gfin_t[0:n, :], ALU.mult, ALU.mult,
                         [b_dh2[s], b_dstat[s], b_const], [b_dy[s]])
                    if isP:
                        if c0 >= NMETA:
                            dma("sp", out_y[c0 - NMETA:c0 - NMETA + n, :], dy[s][0:n, :], [b_dy[s]], [])
                    else:
                        dma("sp", out_y[c0 - nc0:c0 - nc0 + n, :], dy[s][0:n, :], [b_dy[s]], [])
            P.fence()

    P.fence()
    P.emit(nc)
    return nc


_CFG = {"n_prompt": PB, "n_sample": SB, "phases": "ABMCD"}
_NC_CACHE = {}


def _constants():
    c = {}
    c["c_ident"] = np.eye(128, dtype=np.float32)
    p = np.arange(128)
    c["c_utri"] = (p[:, None] <= p[None, :]).astype(np.float32)
    c["c_ones"] = np.ones((128, 128), np.float32)
    c["c_mtri"] = (p[:, None] <= p[None, :]).astype(np.float32)
    c["c_mchk"] = ((p[:, None] // 64) <= (p[None, :] // 64)).astype(np.float32)
    inv = (10000.0 ** (-np.arange(0, ROPE, 2, dtype=np.float32) / np.float32(ROPE))).astype(np.float32)
    pos = np.arange(L, dtype=np.float32)
    ang = (pos[:, None] * inv[None, :]).astype(np.float32).astype(np.float64)
    cos = np.cos(ang).astype(np.float32)
    sin = np.sin(ang).astype(np.float32)
    c["c_cs_tm"] = np.ascontiguousarray(np.concatenate([cos, cos, -sin, sin], axis=1))
    c["c_cs_fm"] = np.ascontiguousarray(np.stack([np.concatenate([cos, cos], axis=1).T,
                                                  np.concatenate([sin, sin], axis=1).T], axis=0))
    return c


def kernel(x_prompt, x_sample, cache_fox_k, cache_fox_v, cache_fox_logf, cache_mla_ckv, cache_mla_krope,
           state_ffn_conv, meta_tokens, norm_mix_g, w_in, b_forget, mla_q_norm_g, w_uq, mla_kv_norm_g, w_ukv,
           w_o_fox, w_o_mla, w_out, norm_ffn_g, w_up, conv_w, conv_b, w_down, norm_final_g):
    f = lambda a: np.ascontiguousarray(np.asarray(a, dtype=np.float32))
    key = (_CFG["n_prompt"], _CFG["n_sample"], _CFG["phases"])
    if key not in _NC_CACHE:
        _NC_CACHE[key] = build_program(*key)
    nc = _NC_CACHE[key]
    rep = lambda v: np.ascontiguousarray(np.broadcast_to(f(v).reshape(1, -1), (128, f(v).size)))
    shared = {
        "meta_tokens": f(meta_tokens), "w_in": f(w_in)[0], "w_uq": f(w_uq)[0], "w_ukv": f(w_ukv)[0],
        "w_o_fox": f(w_o_fox)[0], "w_o_mla": f(w_o_mla)[0], "w_out": f(w_out)[0], "w_up": f(w_up)[0],
        "w_down": f(w_down)[0],
        "convwb": np.ascontiguousarray(np.concatenate([f(conv_w)[0], f(conv_b)[0][None, :]], axis=0)
                                       .reshape(4, NCH, 128).transpose(2, 1, 0)),
        "g_mix_bc": rep(norm_mix_g), "g_ffn_bc": rep(norm_ffn_g), "g_fin_bc": rep(norm_final_g),
        "g_q_bc": rep(mla_q_norm_g), "g_kv_bc": rep(mla_kv_norm_g), "b_forget_bc": rep(b_forget),
    }
    shared.update(_constants())
    xp = f(x_prompt)
    xs = f(x_sample)
    in_maps = []
    for c in range(N_CORES):
        m = dict(shared)
        m["x_prompt"] = xp[c * PB:(c + 1) * PB]
        m["x_sample"] = xs[c * SB:(c + 1) * SB]
        m["cache_fox_k"] = f(cache_fox_k)[0, c * SB:(c + 1) * SB].reshape(SB, PAST, FOXW)
        m["cache_fox_v"] = f(cache_fox_v)[0, c * SB:(c + 1) * SB].reshape(SB, PAST, FOXW)
        m["cache_fox_logf"] = f(cache_fox_logf)[0, c * SB:(c + 1) * SB]
        m["cache_mla_ckv"] = f(cache_mla_ckv)[0, c * SB:(c + 1) * SB]
        m["cache_mla_krope"] = f(cache_mla_krope)[0, c * SB:(c + 1) * SB]
        m["state_ffn_conv"] = f(state_ffn_conv)[0, c * SB:(c + 1) * SB]
        in_maps.append({k: np.ascontiguousarray(v) for k, v in m.items()})
    res = run_bass_kernel_spmd(nc, in_maps, core_ids=list(range(N_CORES)))
    R = res.results
    cat = lambda name: np.concatenate([np.asarray(r[name], dtype=np.float32) for r in R], axis=0)
    B = N_CORES * PB
    S = N_CORES * SB
    return (
        cat("y_prompt"), cat("y_sample"),
        cat("new_fox_k_p").reshape(1, B, L, H, HD), cat("new_fox_v_p").reshape(1, B, L, H, HD),
        cat("new_fox_logf_p").reshape(1, B, L, H), cat("new_mla_ckv_p").reshape(1, B, L, KVL),
        cat("new_mla_krope_p").reshape(1, B, L, ROPE), cat("new_ffn_conv_p").reshape(1, B, 2, UPW),
        cat("new_fox_k_s").reshape(1, S, DSEQ, H, HD), cat("new_fox_v_s").reshape(1, S, DSEQ, H, HD),
        cat("new_fox_logf_s").reshape(1, S, DSEQ, H), cat("new_mla_ckv_s").reshape(1, S, DSEQ, KVL),
        cat("new_mla_krope_s").reshape(1, S, DSEQ, ROPE), cat("new_ffn_conv_s").reshape(1, S, 2, UPW),
    )
```
